# Optimizing a Trainium2 kernel written in Bass

```python
import math
import jax, jax.numpy as jnp
from jax import lax
import numpy as np

D_MODEL = 1024
BATCH = 4
SEQ = 4096
DEPTH = 4
DEC_BATCH = 128
DEC_SEQ = 8
PAST_LEN = 8192
PAGE_SIZE = 128

N_MIXERS = 2
N_SSD_LAYERS = (DEPTH + 1) // 2
N_MLA_LAYERS = DEPTH // 2
NORM_EPS = 1e-6
NEG_INF = -1e30

SSD_EXPAND = 2
SSD_D_INNER = SSD_EXPAND * D_MODEL
SSD_HEADDIM = 64
SSD_HEADS = SSD_D_INNER // SSD_HEADDIM
SSD_GROUPS = 8
SSD_STATE = 128
SSD_CONV_W = 4
SSD_CHUNK = 128
SSD_GN = SSD_GROUPS * SSD_STATE
SSD_CONV_DIM = SSD_D_INNER + 2 * SSD_GN
SSD_IN_DIM = SSD_D_INNER + SSD_CONV_DIM + SSD_HEADS

MLA_HEADS = 16
MLA_Q_LORA = 512
MLA_KV_LORA = 256
MLA_QK_NOPE = 64
MLA_QK_ROPE = 32
MLA_V_HEAD = 64
MLA_WIDTH = MLA_HEADS * MLA_V_HEAD
MLA_IN_DIM = MLA_Q_LORA + MLA_KV_LORA + MLA_QK_ROPE + MLA_WIDTH
MLA_SCALE = (MLA_QK_NOPE + MLA_QK_ROPE) ** -0.5
ROPE_BASE = 10000.0
Q_BLOCK = 128
POOL_SPARE_DIV = 4

kernel_name = 'hybrid_ssd_mla_decode_step'

F32 = jnp.float32


def rms_norm(x, w):
    xf = x.astype(F32)
    y = xf * lax.rsqrt(jnp.mean(xf * xf, axis=-1, keepdims=True) + NORM_EPS)
    return (y * w.astype(F32)).astype(x.dtype)


def causal_conv(xbc, buf, w, b):
    t = xbc.shape[1]
    full = jnp.concatenate([buf.astype(xbc.dtype), xbc], axis=1)
    out = full[:, 0:t] * w[0]
    for k in range(1, SSD_CONV_W):
        out = out + full[:, k:k + t] * w[k]
    return out + b, full[:, full.shape[1] - (SSD_CONV_W - 1):]


def ssd_scan(x, dt, a_head, bm, cm, init_state):
    b, L, h, p = x.shape
    g, n = bm.shape[2], bm.shape[3]
    r = h // g
    cl = math.gcd(L, SSD_CHUNK)
    nc = L // cl
    a = (dt * a_head).reshape(b, nc, cl, g, r).transpose(0, 3, 4, 1, 2)
    a_cs = jnp.cumsum(a, axis=-1)
    xdt = (x * dt[..., None]).reshape(b, nc, cl, g, r, p)
    bc = bm.reshape(b, nc, cl, g, n)
    cc = cm.reshape(b, nc, cl, g, n)
    seg = a_cs[..., :, None] - a_cs[..., None, :]
    causal = jnp.tril(jnp.ones((cl, cl), bool))
    decay = jnp.where(causal, jnp.exp(jnp.where(causal, seg, 0.0)), 0.0)
    cb = jnp.einsum('bclgn,bcsgn->bgcls', cc, bc)
    y_diag = jnp.einsum('bgrcls,bcsgrp->bclgrp', cb[:, :, None] * decay, xdt)
    decay_end = jnp.exp(a_cs[..., -1:] - a_cs).transpose(0, 3, 4, 1, 2)
    states = jnp.einsum('bcsgn,bcsgrp->bcgrpn', bc, xdt * decay_end[..., None])
    chunk_decay = jnp.exp(a_cs[..., -1])

    def step(s, inp):
        st, dec = inp
        return dec[..., None, None] * s + st, s

    final, prev = lax.scan(step, init_state.reshape(b, g, r, p, n),
                           (states.transpose(1, 0, 2, 3, 4, 5), chunk_decay.transpose(3, 0, 1, 2)))
    prev = prev.transpose(1, 0, 2, 3, 4, 5)
    decay_in = jnp.exp(a_cs).transpose(0, 3, 4, 1, 2)
    y_off = jnp.einsum('bclgn,bcgrpn->bclgrp', cc, prev) * decay_in[..., None]
    y = (y_diag + y_off).reshape(b, L, h, p)
    return y, final.reshape(b, h, p, n)


def ssd_mixer(h, conv_buf, ssm_state, w_in, conv_w, conv_b, dt_bias, a_log, d_skip, norm_w, w_out):
    b, t, _ = h.shape
    proj = jnp.einsum('btd,de->bte', h, w_in)
    z = proj[..., :SSD_D_INNER]
    xbc = proj[..., SSD_D_INNER:SSD_D_INNER + SSD_CONV_DIM]
    dt_raw = proj[..., SSD_D_INNER + SSD_CONV_DIM:]
    xbc, new_buf = causal_conv(xbc, conv_buf, conv_w, conv_b)
    xbc = jax.nn.silu(xbc.astype(F32))
    xs = xbc[..., :SSD_D_INNER].reshape(b, t, SSD_HEADS, SSD_HEADDIM)
    bm = xbc[..., SSD_D_INNER:SSD_D_INNER + SSD_GN].reshape(b, t, SSD_GROUPS, SSD_STATE)
    cm = xbc[..., SSD_D_INNER + SSD_GN:].reshape(b, t, SSD_GROUPS, SSD_STATE)
    dt = jax.nn.softplus(dt_raw.astype(F32) + dt_bias.astype(F32))
    a_head = -jnp.exp(a_log.astype(F32))
    y, new_state = ssd_scan(xs, dt, a_head, bm, cm, ssm_state.astype(F32))
    y = y + d_skip.astype(F32)[:, None] * xs
    y = y.reshape(b, t, SSD_D_INNER) * jax.nn.silu(z.astype(F32))
    yg = y.reshape(b, t, SSD_GROUPS, SSD_D_INNER // SSD_GROUPS)
    yg = yg * lax.rsqrt(jnp.mean(yg * yg, axis=-1, keepdims=True) + NORM_EPS)
    y = yg.reshape(b, t, SSD_D_INNER) * norm_w.astype(F32)
    out = jnp.einsum('bte,ed->btd', y.astype(h.dtype), w_out)
    return out, new_buf, new_state


def rope_tables(pos):
    inv = 1.0 / (ROPE_BASE ** (jnp.arange(0, MLA_QK_ROPE, 2, dtype=F32) / MLA_QK_ROPE))
    ang = pos[:, None] * inv[None, :]
    return jnp.cos(ang), jnp.sin(ang)


def apply_rope(x, cos, sin):
    xf = x.astype(F32)
    half = MLA_QK_ROPE // 2
    x1, x2 = xf[..., :half], xf[..., half:]
    return jnp.concatenate([x1 * cos - x2 * sin, x2 * cos + x1 * sin], axis=-1).astype(x.dtype)


def mla_project(h, pos, w_in, q_norm_w, kv_norm_w, w_uq):
    proj = jnp.einsum('btd,de->bte', h, w_in)
    o1 = MLA_Q_LORA
    o2 = o1 + MLA_KV_LORA
    o3 = o2 + MLA_QK_ROPE
    cq = rms_norm(proj[..., :o1], q_norm_w)
    ckv = rms_norm(proj[..., o1:o2], kv_norm_w)
    gate = proj[..., o3:]
    q = jnp.einsum('btq,qhe->bthe', cq, w_uq)
    cos, sin = rope_tables(pos)
    q_nope = q[..., :MLA_QK_NOPE]
    q_pe = apply_rope(q[..., MLA_QK_NOPE:], cos[:, None, :], sin[:, None, :])
    kpe = apply_rope(proj[..., o2:o3], cos, sin)
    return q_nope, q_pe, ckv, kpe, gate


def absorb(q_nope, w_uk):
    return jnp.einsum('bqhn,chn->bqhc', q_nope.astype(F32), w_uk.astype(F32))


def latent_partial(q_lat, q_pe, ckv, kpe, mask):
    s = (jnp.einsum('bqhc,bkc->bhqk', q_lat, ckv.astype(F32))
         + jnp.einsum('bqhr,bkr->bhqk', q_pe.astype(F32), kpe.astype(F32))) * MLA_SCALE
    if mask is not None:
        s = jnp.where(mask, s, NEG_INF)
    m = jnp.max(s, axis=-1)
    p = jnp.exp(s - m[..., None])
    return m, jnp.sum(p, axis=-1), jnp.einsum('bhqk,bkc->bhqc', p, ckv.astype(F32))


def merge_partial(pa, pb):
    ma, la, aa = pa
    mb, lb, ab = pb
    m = jnp.maximum(ma, mb)
    ca = jnp.exp(ma - m)
    cb = jnp.exp(mb - m)
    return m, la * ca + lb * cb, aa * ca[..., None] + ab * cb[..., None]


def latent_out(l, acc, w_uv):
    return jnp.einsum('bhqc,chv->bqhv', acc / l[..., None], w_uv.astype(F32))


def mla_prompt_attn(q_nope, q_pe, ckv, kpe, w_uk, w_uv):
    b, s = q_nope.shape[:2]
    nqb = s // Q_BLOCK
    k_pos = jnp.arange(s)

    def block(args):
        qn, qp, q_pos = args
        _, l, acc = latent_partial(absorb(qn, w_uk), qp, ckv, kpe, q_pos[:, None] >= k_pos[None, :])
        return latent_out(l, acc, w_uv)

    def to_blocks(a):
        return a.reshape((b, nqb, Q_BLOCK) + a.shape[2:]).swapaxes(0, 1)

    o = lax.map(block, (to_blocks(q_nope), to_blocks(q_pe), k_pos.reshape(nqb, Q_BLOCK)))
    return o.swapaxes(0, 1).reshape(b, s, MLA_HEADS, MLA_V_HEAD)


def mla_sample_attn(q_nope, q_pe, ckv, kpe, pool_ckv, pool_kpe, layer, page_table, w_uk, w_uv):
    t = q_nope.shape[1]
    q_lat = absorb(q_nope, w_uk)
    causal = jnp.tril(jnp.ones((t, t), bool))
    stats = latent_partial(q_lat, q_pe, ckv, kpe, causal)

    def page_step(carry, pt):
        page = latent_partial(q_lat, q_pe, pool_ckv[layer, pt], pool_kpe[layer, pt], None)
        return merge_partial(carry, page), None

    (_, l, acc), _ = lax.scan(page_step, stats, page_table.T)
    return latent_out(l, acc, w_uv)


def mla_output(o, gate, w_out):
    b, t = o.shape[:2]
    gated = o.reshape(b, t, MLA_WIDTH) * jax.nn.silu(gate.astype(F32))
    return jnp.einsum('bte,ed->btd', gated.astype(gate.dtype), w_out)


def setup_inputs(seed: int = 0) -> dict:
    key = jax.random.key(seed)
    ks = iter(jax.random.split(key, 32))

    def nrm(shape, scale):
        return jax.random.normal(next(ks), shape, F32) * scale

    n_pages = PAST_LEN // PAGE_SIZE
    n_used = DEC_BATCH * n_pages
    n_pool = n_used + n_used // POOL_SPARE_DIV
    x_prompt = nrm((BATCH, SEQ, D_MODEL), 1.0)
    x_sample = nrm((DEC_BATCH, DEC_SEQ, D_MODEL), 1.0)
    state_ssm = nrm((N_SSD_LAYERS, DEC_BATCH, SSD_HEADS, SSD_HEADDIM, SSD_STATE), 0.1)
    state_conv = nrm((N_SSD_LAYERS, DEC_BATCH, SSD_CONV_W - 1, SSD_CONV_DIM), 1.0)
    cache_ckv = nrm((N_MLA_LAYERS, n_pool, PAGE_SIZE, MLA_KV_LORA), 1.0)
    cache_kpe = nrm((N_MLA_LAYERS, n_pool, PAGE_SIZE, MLA_QK_ROPE), 1.0)
    page_table = jax.random.permutation(next(ks), n_pool)[:n_used].reshape(DEC_BATCH, n_pages).astype(jnp.int32)
    norm_w = 1.0 + nrm((DEPTH, D_MODEL), 0.02)
    final_norm_w = 1.0 + nrm((D_MODEL,), 0.02)
    ssd_w_in = nrm((N_SSD_LAYERS, D_MODEL, SSD_IN_DIM), D_MODEL ** -0.5)
    ssd_conv_w = nrm((N_SSD_LAYERS, SSD_CONV_W, SSD_CONV_DIM), SSD_CONV_W ** -0.5)
    ssd_conv_b = nrm((N_SSD_LAYERS, SSD_CONV_DIM), 0.01)
    dt0 = jnp.exp(jax.random.uniform(next(ks), (N_SSD_LAYERS, SSD_HEADS), F32,
                                     minval=math.log(1e-3), maxval=math.log(1e-1)))
    ssd_dt_bias = dt0 + jnp.log(-jnp.expm1(-dt0))
    ssd_a_log = jnp.log(jax.random.uniform(next(ks), (N_SSD_LAYERS, SSD_HEADS), F32, minval=1.0, maxval=16.0))
    ssd_d = 1.0 + nrm((N_SSD_LAYERS, SSD_HEADS), 0.1)
    ssd_norm_w = 1.0 + nrm((N_SSD_LAYERS, SSD_D_INNER), 0.02)
    ssd_w_out = nrm((N_SSD_LAYERS, SSD_D_INNER, D_MODEL), SSD_D_INNER ** -0.5)
    mla_w_in = nrm((N_MLA_LAYERS, D_MODEL, MLA_IN_DIM), D_MODEL ** -0.5)
    mla_q_norm_w = 1.0 + nrm((N_MLA_LAYERS, MLA_Q_LORA), 0.02)
    mla_kv_norm_w = 1.0 + nrm((N_MLA_LAYERS, MLA_KV_LORA), 0.02)
    mla_w_uq = nrm((N_MLA_LAYERS, MLA_Q_LORA, MLA_HEADS, MLA_QK_NOPE + MLA_QK_ROPE), MLA_Q_LORA ** -0.5)
    mla_w_uk = nrm((N_MLA_LAYERS, MLA_KV_LORA, MLA_HEADS, MLA_QK_NOPE), MLA_KV_LORA ** -0.5)
    mla_w_uv = nrm((N_MLA_LAYERS, MLA_KV_LORA, MLA_HEADS, MLA_V_HEAD), MLA_KV_LORA ** -0.5)
    mla_w_out = nrm((N_MLA_LAYERS, MLA_WIDTH, D_MODEL), MLA_WIDTH ** -0.5)
    return {'x_prompt': x_prompt, 'x_sample': x_sample, 'state_ssm': state_ssm, 'state_conv': state_conv,
            'cache_ckv': cache_ckv, 'cache_kpe': cache_kpe, 'page_table': page_table,
            'norm_w': norm_w, 'final_norm_w': final_norm_w,
            'ssd_w_in': ssd_w_in, 'ssd_conv_w': ssd_conv_w, 'ssd_conv_b': ssd_conv_b,
            'ssd_dt_bias': ssd_dt_bias, 'ssd_a_log': ssd_a_log, 'ssd_d': ssd_d,
            'ssd_norm_w': ssd_norm_w, 'ssd_w_out': ssd_w_out,
            'mla_w_in': mla_w_in, 'mla_q_norm_w': mla_q_norm_w, 'mla_kv_norm_w': mla_kv_norm_w,
            'mla_w_uq': mla_w_uq, 'mla_w_uk': mla_w_uk, 'mla_w_uv': mla_w_uv, 'mla_w_out': mla_w_out}


def reference(x_prompt, x_sample, state_ssm, state_conv, cache_ckv, cache_kpe, page_table,
              norm_w, final_norm_w,
              ssd_w_in, ssd_conv_w, ssd_conv_b, ssd_dt_bias, ssd_a_log, ssd_d, ssd_norm_w, ssd_w_out,
              mla_w_in, mla_q_norm_w, mla_kv_norm_w, mla_w_uq, mla_w_uk, mla_w_uv, mla_w_out):
    pos_p = jnp.arange(SEQ, dtype=F32)
    pos_s = PAST_LEN + jnp.arange(DEC_SEQ, dtype=F32)
    hp, hs = x_prompt, x_sample
    p_ssm, p_conv, p_ckv, p_kpe = [], [], [], []
    s_ssm, s_conv, s_ckv, s_kpe = [], [], [], []
    for i in range(DEPTH):
        j = i // N_MIXERS
        ln_p = rms_norm(hp, norm_w[i])
        ln_s = rms_norm(hs, norm_w[i])
        if i % N_MIXERS == 0:
            params = (ssd_w_in[j], ssd_conv_w[j], ssd_conv_b[j], ssd_dt_bias[j], ssd_a_log[j],
                      ssd_d[j], ssd_norm_w[j], ssd_w_out[j])
            zero_buf = jnp.zeros((BATCH, SSD_CONV_W - 1, SSD_CONV_DIM), ln_p.dtype)
            zero_ssm = jnp.zeros((BATCH, SSD_HEADS, SSD_HEADDIM, SSD_STATE), F32)
            out_p, buf_p, st_p = ssd_mixer(ln_p, zero_buf, zero_ssm, *params)
            out_s, buf_s, st_s = ssd_mixer(ln_s, state_conv[j], state_ssm[j], *params)
            p_ssm.append(st_p)
            p_conv.append(buf_p)
            s_ssm.append(st_s)
            s_conv.append(buf_s)
        else:
            qn_p, qp_p, ckv_p, kpe_p, gate_p = mla_project(ln_p, pos_p, mla_w_in[j], mla_q_norm_w[j],
                                                           mla_kv_norm_w[j], mla_w_uq[j])
            qn_s, qp_s, ckv_s, kpe_s, gate_s = mla_project(ln_s, pos_s, mla_w_in[j], mla_q_norm_w[j],
                                                           mla_kv_norm_w[j], mla_w_uq[j])
            o_p = mla_prompt_attn(qn_p, qp_p, ckv_p, kpe_p, mla_w_uk[j], mla_w_uv[j])
            o_s = mla_sample_attn(qn_s, qp_s, ckv_s, kpe_s, cache_ckv, cache_kpe, j, page_table,
                                  mla_w_uk[j], mla_w_uv[j])
            out_p = mla_output(o_p, gate_p, mla_w_out[j])
            out_s = mla_output(o_s, gate_s, mla_w_out[j])
            p_ckv.append(ckv_p)
            p_kpe.append(kpe_p)
            s_ckv.append(ckv_s)
            s_kpe.append(kpe_s)
        hp = hp + out_p.astype(hp.dtype)
        hs = hs + out_s.astype(hs.dtype)
    y_prompt = rms_norm(hp, final_norm_w)
    y_sample = rms_norm(hs, final_norm_w)
    return (y_prompt, y_sample,
            jnp.stack(p_ssm), jnp.stack(p_conv), jnp.stack(p_ckv), jnp.stack(p_kpe),
            jnp.stack(s_ssm), jnp.stack(s_conv), jnp.stack(s_ckv), jnp.stack(s_kpe))
```

```python
import math
from contextlib import ExitStack

import numpy as np
import concourse.bass as bass
import concourse.mybir as mybir
from concourse.bass_utils import run_bass_kernel_spmd

F32 = mybir.dt.float32
BF16 = mybir.dt.bfloat16
I32 = mybir.dt.int32
AF = mybir.ActivationFunctionType
ALU = mybir.AluOpType

D = 1024
EPS = 1e-6
NEG = -1.0e5

FULL_CFG = dict(SEQ=4096, NPG=64, NPOOL=10240, LAYERS=("ssd", "mla", "ssd", "mla"))


class Buf:
    __slots__ = ("name", "w", "r", "dsem", "dcount", "dlast", "excl")

    def __init__(self, name):
        self.name = name
        self.w = None
        self.r = []
        self.dsem = None
        self.dcount = 0
        self.dlast = None
        self.excl = False


class Sched:
    ENG = ("pe", "act", "dve", "pool", "sp")

    def __init__(self, nc, es):
        self.nc = nc
        self.es = es
        self.ops = {e: [] for e in self.ENG}
        self.sem = {}
        for e in ("pe", "act", "dve", "pool"):
            self.sem[e] = es.enter_context(nc.semaphore("sem_" + e))
        self.count = {e: 0 for e in self.ENG}
        self.waited = {e: {} for e in self.ENG}
        self.bufs = {}
        self.dma_bufs = []
        self.nsem = 4

    def buf(self, name):
        b = Buf(name)
        return b

    def tbuf(self, ap):
        n = ap.tensor.name
        if n not in self.bufs:
            self.bufs[n] = Buf(n)
        return self.bufs[n]

    def _deps(self, eng, reads, writes):
        deps = []
        for b in reads:
            if b.w is not None:
                deps.append(b.w)
        for b in writes:
            if b.w is not None:
                deps.append(b.w)
            deps.extend(b.r)
        waits = []
        wd = self.waited[eng]
        best = {}
        for (sk, sem, v) in deps:
            if sk == eng and eng == "pe":
                continue
            if wd.get(sk, 0) >= v:
                continue
            if sk not in best or best[sk][1] < v:
                best[sk] = (sem, v)
        for sk, (sem, v) in best.items():
            wd[sk] = v
            waits.append((sem, v))
        return waits

    def op(self, eng, fn, reads=(), writes=()):
        writes = list(dict.fromkeys(list(writes) + [b for b in reads if b.excl]))
        reads = list(dict.fromkeys(b for b in reads if not b.excl))
        waits = self._deps(eng, reads, writes)
        self.count[eng] += 1
        tag = (eng, self.sem[eng], self.count[eng])
        self.ops[eng].append((waits, fn, self.sem[eng], 1))
        for b in reads:
            b.r.append(tag)
        for b in writes:
            b.w = tag
            b.r = []
        return tag

    def dma(self, queue, fn, sb, reads=(), writes=()):
        if sb.dsem is None:
            sb.dsem = self.es.enter_context(self.nc.semaphore("d_" + sb.name))
            self.dma_bufs.append(sb)
            self.nsem += 1
        reads = list(dict.fromkeys(reads))
        writes = list(dict.fromkeys(writes))
        waits = self._deps(queue, reads, writes)
        if sb.dlast is not None:
            sk, sem, v = sb.dlast
            if self.waited[queue].get(sk, 0) < v:
                self.waited[queue][sk] = v
                waits.append((sem, v))
        sb.dcount += 1
        tag = ("d_" + sb.name, sb.dsem, 16 * sb.dcount)
        sb.dlast = tag
        self.ops[queue].append((waits, fn, sb.dsem, 16))
        for b in reads:
            b.r.append(tag)
        for b in writes:
            b.w = tag
            b.r = []
        return tag

    def finish(self):
        waits = []
        for b in self.dma_bufs:
            waits.append((b.dsem, 16 * b.dcount))
        self.ops["sp"].append((waits, None, None, 0))

    def emit(self):
        nc = self.nc
        ops = self.ops

        def run(e, lst):
            for (waits, fn, sem, amt) in lst:
                for (s, v) in waits:
                    e.wait_ge(s, v)
                if fn is not None:
                    inst = fn(e)
                    inst.then_inc(sem, amt)

        with nc.Block() as block:
            @block.sync
            def _(e):
                run(e, ops["sp"])

            @block.scalar
            def _(e):
                run(e, ops["act"])

            @block.vector
            def _(e):
                run(e, ops["dve"])

            @block.gpsimd
            def _(e):
                run(e, ops["pool"])

            @block.tensor
            def _(e):
                run(e, ops["pe"])


class Prog:
    def __init__(self, cfg):
        self.cfg = cfg
        self.SEQ = cfg["SEQ"]
        self.NT = self.SEQ // 128
        self.NPG = cfg["NPG"]
        self.NPOOL = cfg["NPOOL"]
        self.LAYERS = cfg["LAYERS"]
        self.n_ssd = sum(1 for l in self.LAYERS if l == "ssd")
        self.n_mla = sum(1 for l in self.LAYERS if l == "mla")
        self.nc = bass.Bass("TRN2", target_bir_lowering=False)
        self.es = ExitStack()
        self.S = None
        self.dram = {}

    def din(self, name, shape, dt=F32):
        t = self.nc.dram_tensor(name, list(shape), dt, kind="ExternalInput")
        self.dram[name] = t
        return t

    def dout(self, name, shape, dt=F32):
        t = self.nc.dram_tensor(name, list(shape), dt, kind="ExternalOutput")
        self.dram[name] = t
        return t

    def dint(self, name, shape, dt=F32):
        t = self.nc.dram_tensor(name, list(shape), dt, kind="Internal")
        self.dram[name] = t
        return t

    def sb(self, name, shape, dt=F32):
        self._uid = getattr(self, "_uid", 0) + 1
        return self.es.enter_context(self.nc.sbuf_tensor("%s_%d" % (name, self._uid), list(shape), dt))

    def _rw(self, ins, outs, rd, wr):
        S = self.S
        reads = list(rd) if rd is not None else []
        writes = list(wr) if wr is not None else []
        if rd is None:
            reads = [S.tbuf(a) for a in ins if a is not None and not isinstance(a, (int, float))]
        if wr is None:
            writes = [S.tbuf(a) for a in outs if a is not None]
        return reads, writes

    def mm(self, out, lhsT, rhs, start=True, stop=True, rd=None, wr=None, xrd=()):
        reads, writes = self._rw([lhsT, rhs], [out], rd, wr)
        reads += list(xrd)
        self.S.op("pe", lambda e: e.matmul(out, lhsT, rhs, start=start, stop=stop,
                                           skip_group_check=True), reads, writes)

    def tr(self, out, in_, ident, rd=None, wr=None):
        reads, writes = self._rw([in_, ident], [out], rd, wr)
        self.S.op("pe", lambda e: e.transpose(out, in_, ident), reads, writes)

    def act(self, out, in_, func, bias=0.0, scale=1.0, accum=None, eng="act", rd=None, wr=None):
        ins = [in_]
        if not isinstance(bias, (int, float)):
            ins.append(bias)
        if not isinstance(scale, (int, float)):
            ins.append(scale)
        outs = [out] + ([accum] if accum is not None else [])
        reads, writes = self._rw(ins, outs, rd, wr)
        if accum is None:
            self.S.op("act", lambda e: e.activation(out, in_, func, bias=bias, scale=scale), reads, writes)
        else:
            self.S.op("act", lambda e: e.activation(out, in_, func, bias=bias, scale=scale,
                                                    accum_out=accum), reads, writes)

    def tt(self, out, in0, in1, op, eng="dve", rd=None, wr=None):
        reads, writes = self._rw([in0, in1], [out], rd, wr)
        self.S.op(eng, lambda e: e.tensor_tensor(out=out, in0=in0, in1=in1, op=op), reads, writes)

    def ts(self, out, in0, s1, op0, s2=None, op1=None, eng="dve", rd=None, wr=None):
        ins = [in0] + [s for s in (s1, s2) if s is not None and not isinstance(s, (int, float))]
        reads, writes = self._rw(ins, [out], rd, wr)
        if op1 is None:
            self.S.op(eng, lambda e: e.tensor_scalar(out=out, in0=in0, scalar1=s1, scalar2=None, op0=op0),
                      reads, writes)
        else:
            self.S.op(eng, lambda e: e.tensor_scalar(out=out, in0=in0, scalar1=s1, scalar2=s2, op0=op0,
                                                     op1=op1), reads, writes)

    def stt(self, out, in0, scalar, in1, op0, op1, rd=None, wr=None):
        ins = [in0, in1] + ([scalar] if not isinstance(scalar, (int, float)) else [])
        reads, writes = self._rw(ins, [out], rd, wr)
        self.S.op("dve", lambda e: e.scalar_tensor_tensor(out=out, in0=in0, scalar=scalar, in1=in1,
                                                          op0=op0, op1=op1), reads, writes)

    def cp(self, out, in_, eng="dve", rd=None, wr=None):
        reads, writes = self._rw([in_], [out], rd, wr)
        if eng == "act":
            self.S.op("act", lambda e: e.copy(out, in_), reads, writes)
        else:
            self.S.op(eng, lambda e: e.tensor_copy(out=out, in_=in_), reads, writes)

    def memset(self, ap, val, eng="dve", wr=None):
        reads, writes = self._rw([], [ap], None, wr)
        self.S.op(eng, lambda e: e.memset(ap, val), reads, writes)

    def recip(self, out, in_, rd=None, wr=None):
        reads, writes = self._rw([in_], [out], rd, wr)
        self.S.op("dve", lambda e: e.reciprocal(out=out, in_=in_), reads, writes)

    def load(self, out, in_, sbbuf=None, rd=(), wr=None, q="sp", nc_ok=False):
        S = self.S
        sbb = sbbuf if sbbuf is not None else S.tbuf(out)
        writes = [sbb] if wr is None else list(wr)
        if nc_ok:
            fn = lambda e: e.dma_start(out=out, in_=in_, allow_slow_non_contiguous=True)
        else:
            fn = lambda e: e.dma_start(out=out, in_=in_)
        S.dma(q, fn, sbb, reads=list(rd), writes=writes)

    def store(self, out, in_, sbbuf=None, rd=None, wr=(), q="sp", nc_ok=False):
        S = self.S
        sbb = sbbuf if sbbuf is not None else S.tbuf(in_)
        reads = [sbb] if rd is None else list(rd)
        if nc_ok:
            fn = lambda e: e.dma_start(out=out, in_=in_, allow_slow_non_contiguous=True)
        else:
            fn = lambda e: e.dma_start(out=out, in_=in_)
        S.dma(q, fn, sbb, reads=reads, writes=list(wr))

    def rstd_of(self, ssq, n, out):
        self.ts(out, ssq, 1.0 / n, ALU.mult, EPS, ALU.add)
        self.act(out, out, AF.Ln)
        self.act(out, out, AF.Exp, scale=-0.5)

    def build(self):
        nc = self.nc
        cfg = self.cfg
        SEQ, NT, NPG, NPOOL = self.SEQ, self.NT, self.NPG, self.NPOOL
        nS, nM = self.n_ssd, self.n_mla
        NTT = NT + 1
        ROWS = SEQ + 128
        es = self.es
        self.S = S = Sched(nc, es)

        x_in = self.din("x_in", [ROWS, D])
        cst = self.din("cst", [128, 128 * 6 + 16 + 1 + 4096])
        rope_tok = self.din("rope_tok", [ROWS, 64])
        rope_T = self.din("rope_T", [32, 2, ROWS])
        fnw = self.din("fnw", [1, D])
        y_out = self.dout("y_out", [ROWS, D])
        hb = [self.dint("hbA", [ROWS, D]), self.dint("hbB", [ROWS, D])]
        if nS:
            ssd_win = self.din("ssd_win", [nS * 2, D, 3088])
            ssd_wout = self.din("ssd_wout", [nS * 2, 1024, D])
            ssd_nrm = self.din("ssd_nrm", [nS, 128, 8])
            ssd_gnw = self.din("ssd_gnw", [nS * 2, 128, 8])
            ssd_cw = self.din("ssd_cw", [nS * 2, 128, 16, 5])
            ssd_hp = self.din("ssd_hp", [nS * 2, 3, 16])
            st_ssm = self.din("st_ssm", [nS, 16, 2048, 128])
            st_conv = self.din("st_conv", [nS, 48, 4096])
            o_pssm = self.dout("o_pssm", [nS, 2048, 128])
            o_pconv = self.dout("o_pconv", [nS, 3, 4096])
            o_sssm = self.dout("o_sssm", [nS, 16, 2048, 128])
            o_sconv = self.dout("o_sconv", [nS, 48, 4096])
        if nM:
            mla_win = self.din("mla_win", [nM, D, 1856])
            mla_nrm = self.din("mla_nrm", [nM, 128, 8])
            mla_qnw = self.din("mla_qnw", [nM, 128, 4])
            mla_kvw = self.din("mla_kvw", [nM, 1, 256])
            mla_wuq = self.din("mla_wuq", [nM, 512, 2048])
            mla_wuk = self.din("mla_wuk", [nM, 128, 8, 256])
            mla_wuv = self.din("mla_wuv", [nM, 256, 1024])
            mla_wout = self.din("mla_wout", [nM, 1024, D])
            pool_ckv = self.din("pool_all", [nM * NPOOL * 128, 288])
            ptab = self.din("ptab", [1, 16 * NPG], I32)
            o_pckv = self.dout("o_ckv", [nM, ROWS, 256])
            o_pkpe = self.dout("o_kpe", [nM, ROWS, 32])

        sb = self.sb
        NCS = 128 * 6 + 16 + 1
        cst_sb = sb("cst_sb", [128, NCS])
        self.load(cst_sb[:, :], cst[:, 0:NCS])
        o = 0
        ident_f = cst_sb[:, o:o + 128]; o += 128
        T_p = cst_sb[:, o:o + 128]; o += 128
        T_s = cst_sb[:, o:o + 128]; o += 128
        SEG_p = cst_sb[:, o:o + 128]; o += 128
        SEG_s = cst_sb[:, o:o + 128]; o += 128
        ones_f = cst_sb[:, o:o + 128]; o += 128
        rowmask = cst_sb[:, o:o + 16]; o += 16
        iota_p = cst_sb[:, o:o + 1]; o += 1
        ident_b = sb("ident_b", [128, 128], BF16)
        self.cp(ident_b[:, :], ident_f)
        ones_b = sb("ones_b", [128, 128], BF16)
        self.cp(ones_b[:, :], ones_f)
        Tb_p = sb("Tb_p", [128, 128], BF16)
        self.cp(Tb_p[:, :], T_p)
        Tb_s = sb("Tb_s", [128, 128], BF16)
        self.cp(Tb_s[:, :], T_s)
        negm_p = sb("negm_p", [128, 4, 128], BF16)
        negm_s = sb("negm_s", [128, 4, 128], BF16)
        for q4 in range(4):
            self.ts(negm_p[:, q4, :], T_p, -1.0, ALU.add, -NEG, ALU.mult)
            self.ts(negm_s[:, q4, :], T_s, -1.0, ALU.add, -NEG, ALU.mult)
        fnw_bc = sb("fnw_bc", [128, D])
        self.load(fnw_bc[:, :], fnw.ap().rearrange("a d -> (a d)").partition_broadcast(128))

        banks = [self.es.enter_context(nc.psum_tensor("bank%d" % i, [128, 512], F32)) for i in range(8)]
        bankb = [b.bitcast(BF16) for b in banks]
        for i in range(8):
            S.bufs[bankb[i][:, :].tensor.name] = S.tbuf(banks[i][:, :])
            S.tbuf(banks[i][:, :]).excl = True

        C = dict(ident_f=ident_f, ident_b=ident_b, ones_b=ones_b, ones_f=ones_f, T_p=T_p, T_s=T_s,
                 SEG_p=SEG_p, SEG_s=SEG_s, Tb_p=Tb_p, Tb_s=Tb_s, negm_p=negm_p, negm_s=negm_s,
                 rowmask=rowmask, iota_p=iota_p, fnw_bc=fnw_bc,
                 banks=banks, bankb=bankb)
        self.C = C

        hbufs = [[S.buf("hb%d_%d" % (k, t)) for t in range(NTT)] for k in range(3)]

        h_in = [sb("h_in", [128, D])] * 2
        h_acc = None
        h_new = [sb("h_new", [128, D])] * 2
        xn = sb("xn", [128, D], BF16)
        lnT = sb("lnT", [128, 8, 128], BF16)
        sq_junk = sb("sq_junk", [128, D], BF16)
        ssq = sb("ssq", [128, 1])
        rstd = sb("rstd", [128, 1])
        self.T = dict(h_in=h_in, h_acc=h_acc, h_new=h_new, xn=xn, lnT=lnT, sq_junk=sq_junk, ssq=ssq,
                      rstd=rstd)

        src = (x_in, None)
        li_s = li_m = 0
        nL = len(self.LAYERS)
        cur = None
        for li, kind in enumerate(self.LAYERS):
            last = li == nL - 1
            if kind == "ssd":
                self.ssd_layer(li, li_s, last, locals())
                li_s += 1
            else:
                self.mla_layer(li, li_m, last, locals())
                li_m += 1
        S.finish()
        S.emit()
        return nc

    def stream_src(self, li, sweep=0):
        raise NotImplementedError

    def norm_and_transpose(self, h_tile, fold_bank):
        T, C = self.T, self.C
        self.act(T["sq_junk"][:, :], h_tile, AF.Square, accum=T["ssq"][:, :])
        self.rstd_of(T["ssq"][:, :], D, T["rstd"][:, :])
        self.ts(T["xn"][:, :], h_tile, T["rstd"][:, 0:1], ALU.mult)
        pb = C["bankb"][fold_bank]
        for kc in range(8):
            self.tr(pb[:, kc * 128:(kc + 1) * 128], T["xn"][:, kc * 128:(kc + 1) * 128], C["ident_b"][:, :])
        self.cp(T["lnT"][:, :, :], pb[:, 0:1024].rearrange("p (k t) -> p k t", k=8), eng="act")
        return T["lnT"]

    def final_out(self, hn_ap, t, V):
        T, C = self.T, self.C
        y_out = V["y_out"]
        yt = self.yfin[t % 2]
        self.act(T["sq_junk"][:, :], hn_ap, AF.Square, accum=self.ssq2[:, :])
        self.rstd_of(self.ssq2[:, :], D, self.rstd2[:, :])
        self.stt(yt[:, :], hn_ap, self.rstd2[:, 0:1], C["fnw_bc"][:, :], ALU.mult, ALU.mult)
        self.store(y_out[t * 128:(t + 1) * 128, :], yt[:, :])

    def load_weight_bf(self, dst_bf, src_ap_fn, nk, ncols, scale_col_fn, stage, eng_cycle):
        CB = stage[0].shape[1]
        i = 0
        for kc in range(nk):
            for c0 in range(0, ncols, CB):
                w = min(CB, ncols - c0)
                st = stage[i % 2]
                self.load(st[:, 0:w], src_ap_fn(kc, c0, w))
                sc = scale_col_fn(kc) if scale_col_fn is not None else None
                eng = eng_cycle[i % len(eng_cycle)]
                if sc is None:
                    self.cp(dst_bf[:, kc, c0:c0 + w], st[:, 0:w], eng=eng)
                elif eng == "act":
                    self.act(dst_bf[:, kc, c0:c0 + w], st[:, 0:w], AF.Copy, scale=sc)
                else:
                    self.ts(dst_bf[:, kc, c0:c0 + w], st[:, 0:w], sc, ALU.mult, eng=eng)
                i += 1

    def ssd_layer(self, li, j, last, V):
        nc, S, C, T = self.nc, self.S, self.C, self.T
        sb = self.sb
        NT, NTT, SEQ = self.NT, self.NT + 1, self.SEQ
        banks, bankb = C["banks"], C["bankb"]
        x_in, hb, hbufs = V["x_in"], V["hb"], V["hbufs"]
        es_layer = ExitStack()
        old_es = self.es
        self.es = es_layer

        if li == 0:
            src_t, src_b = x_in, None
        else:
            src_t, src_b = hb[(li - 1) % 2], hbufs[(li - 1) % 2]
        mid_t, mid_b = hb[2 - 2] if False else None, None
        if "hbC" not in self.dram:
            self.dint("hbC", [SEQ + 128, D])
        mid_t, mid_b = self.dram["hbC"], hbufs[2]
        dst_t, dst_b = hb[li % 2], hbufs[li % 2]

        w_in = sb("s_win", [128, 8, 3088], BF16)
        w_out = sb("s_wout", [128, 8, D], BF16)
        stage = [sb("s_stage%d" % i, [128, 1024]) for i in range(2)]
        h_acc = [sb("s_hacc%d" % i, [128, D]) for i in range(2)]
        nrm = sb("s_nrm", [128, 8])
        gnw = sb("s_gnw", [128, 8])
        cw = sb("s_cw", [128, 16, 5])
        hp_bc = sb("s_hp", [128, 3, 16])
        A_bc = sb("s_A", [128, 16])
        raw = sb("s_raw", [128, 16, 131])
        raw_s = sb("s_raws", [128, 16, 16, 11])
        cacc4 = [sb("s_cacc%d" % i, [128, 4, 128]) for i in range(2)]
        cacc = [cacc4[0][:, 0, :], cacc4[1][:, 0, :]]
        xact4 = [sb("s_xact%d" % i, [128, 512]) for i in range(2)]
        x_tok = sb("s_xtok", [128, 1024])
        xdt = sb("s_xdt", [128, 1024], BF16)
        xdd = sb("s_xdd", [128, 1024], BF16)
        BT = sb("s_BT", [128, 4, 128], BF16)
        CT = sb("s_CT", [128, 4, 128], BF16)
        Btok = sb("s_Btok", [128, 4, 128], BF16)
        dtx = sb("s_dtx", [128, 16])
        dtt = sb("s_dtt", [128, 16])
        dt = sb("s_dt", [128, 16])
        a_t = sb("s_a", [128, 16])
        a_hi = sb("s_ahi", [128, 16], BF16)
        a_lo = sb("s_alo", [128, 16], BF16)
        nacs = sb("s_nacs", [128, 16])
        dec_in = sb("s_decin", [128, 16])
        dec_end = sb("s_decend", [128, 16])
        cdec = sb("s_cdec", [128, 16])
        rhs_hi = sb("s_rhshi", [128, 16, 128], BF16)
        rhs_lo = sb("s_rhslo", [128, 16, 128], BF16)
        LT = [sb("s_LT%d" % i, [128, 128]) for i in range(2)]
        GT = [sb("s_GT%d" % i, [128, 128], BF16) for i in range(2)]
        CBm = sb("s_CBm", [128, 128])
        yo_sb = sb("s_yo", [128, 256])
        y_g = sb("s_yg", [128, 256])
        sz = sb("s_sz", [128, 256], BF16)
        sz_all = sb("s_szall", [128, 1024])
        yn = sb("s_yn", [128, 256], BF16)
        ynT = sb("s_ynT", [128, 2, 128], BF16)
        gss = sb("s_gss", [128, 1])
        grs = sb("s_grs", [128, 1])
        ST = sb("s_ST", [128, 1024])
        ST_bf = sb("s_STbf", [128, 1024], BF16)
        sold = [sb("s_sold%d" % i, [128, 128]) for i in range(4)]
        snew = [sb("s_snew%d" % i, [128, 128]) for i in range(4)]
        soT = [sb("s_soT%d" % i, [128, 256], BF16) for i in range(2)]
        CTm = [sb("s_CTm%d" % i, [128, 128], BF16) for i in range(2)]
        xdm = [sb("s_xdm%d" % i, [128, 256], BF16) for i in range(2)]
        a_bc = sb("s_abc", [128, 1024])
        cd_s = sb("s_cds", [128, 8, 16])
        cst48 = sb("s_cst48", [48, 2048])
        cout = cst48
        colmask_b = sb("s_colmask", [128, 16, 128], BF16)
        C["colmask_b"] = colmask_b
        for hh in range(2):
            self.load(stage[hh][:, :], V["cst"][:, 785 + hh * 1024:785 + (hh + 1) * 1024])
            self.cp(colmask_b[:, hh * 8:(hh + 1) * 8, :], stage[hh][:, :].rearrange("p (b t) -> p b t", b=8))
        if last:
            self.yfin = [sb("s_yfin", [128, D])] * 2
            self.ssq2 = sb("s_ssq2", [128, 1])
            self.rstd2 = sb("s_rstd2", [128, 1])
        h_in, h_new = T["h_in"], T["h_new"]

        ssd_win, ssd_wout = V["ssd_win"], V["ssd_wout"]
        for sw in range(2):
            ls = j * 2 + sw
            self.load(nrm[:, :], V["ssd_nrm"][j, :, :])
            self.load(gnw[:, :], V["ssd_gnw"][ls, :, :])
            self.load(cw[:, :, :], V["ssd_cw"][ls, :, :, :])
            self.load(hp_bc[:, :, :].rearrange("p a b -> p (a b)"),
                      V["ssd_hp"][ls, :, :].rearrange("a b -> (a b)").partition_broadcast(128))
            self.act(A_bc[:, :], hp_bc[:, 1, :], AF.Exp)
            self.ts(A_bc[:, :], A_bc[:, :], -1.0, ALU.mult)
            self.load_weight_bf(w_in, lambda kc, c0, w: ssd_win[ls, kc * 128:(kc + 1) * 128, c0:c0 + w],
                                8, 3088, lambda kc: nrm[:, kc:kc + 1], stage, ["dve", "act"])
            self.load_weight_bf(w_out, lambda kc, c0, w: ssd_wout[ls, kc * 128:(kc + 1) * 128, c0:c0 + w],
                                8, D, lambda kc: gnw[:, kc:kc + 1], stage, ["dve", "act"])
            self.memset(raw[:, :, 0:3], 0.0)
            self.memset(ST[:, :], 0.0)
            self.memset(ST_bf[:, :], 0.0)

            def load_tile(t):
                r0 = t * 128
                self.load(h_in[t % 2][:, :], src_t[r0:r0 + 128, :],
                          rd=([src_b[t]] if src_b is not None else []))
                if sw == 1:
                    self.load(h_acc[t % 2][:, :], mid_t[r0:r0 + 128, :], rd=[mid_b[t]])

            for t in range(NTT):
                samp = t == NT
                load_tile(t)
                hin = h_in[t % 2]
                hacc = hin if sw == 0 else h_acc[t % 2]
                lnT = self.norm_and_transpose(hin[:, :], 0)

                pdt = banks[2]
                for kc in range(8):
                    self.mm(pdt[:, 0:16], lnT[:, kc, :], w_in[:, kc, 3072:3088], start=kc == 0, stop=kc == 7)
                self.tt(dtx[:, :], pdt[:, 0:16], hp_bc[:, 0, :], ALU.add)
                self.act(dtt[:, :], dtx[:, :], AF.Abs)
                self.act(dtt[:, :], dtt[:, :], AF.Exp, scale=-1.0)
                self.act(dtt[:, :], dtt[:, :], AF.Ln, bias=1.0)
                self.stt(dt[:, :], dtx[:, :], 0.0, dtt[:, :], ALU.max, ALU.add)
                self.tt(a_t[:, :], dt[:, :], A_bc[:, :], ALU.mult)
                if samp:
                    stc = V["st_conv"]
                    self.load(cst48[:, 0:1024], stc[j, :, sw * 1024:(sw + 1) * 1024])
                    self.load(cst48[:, 1024:1536], stc[j, :, 2048 + sw * 512:2048 + (sw + 1) * 512])
                    self.load(cst48[:, 1536:2048], stc[j, :, 3072 + sw * 512:3072 + (sw + 1) * 512])
                    for cc in range(16):
                        pb = banks[1]
                        self.tr(pb[:, 0:48], cst48[:, cc * 128:(cc + 1) * 128], C["ident_f"][0:48, 0:48])
                        self.cp(raw_s[:, cc, :, 0:3], pb[:, 0:48].rearrange("p (b k) -> p b k", b=16), eng="act")
                for grp in range(4):
                    pb = banks[1 + grp % 2]
                    ca4 = cacc4[grp % 2]
                    for c4 in range(4):
                        cc = grp * 4 + c4
                        col0 = 1024 + cc * 128
                        for kc in range(8):
                            self.mm(pb[:, c4 * 128:(c4 + 1) * 128], w_in[:, kc, col0:col0 + 128], lnT[:, kc, :],
                                    start=kc == 0, stop=kc == 7)
                    if not samp:
                        self.cp(raw[:, grp * 4:(grp + 1) * 4, 3:131], pb[:, :].rearrange("p (c t) -> p c t", c=4),
                                eng="act")
                        for k in range(4):
                            for c4 in range(4):
                                cc = grp * 4 + c4
                                if k == 0:
                                    self.ts(ca4[:, c4, :], raw[:, cc, 0:128], cw[:, cc, 0:1], ALU.mult,
                                            cw[:, cc, 4:5], ALU.add)
                                else:
                                    self.stt(ca4[:, c4, :], raw[:, cc, k:k + 128], cw[:, cc, k:k + 1], ca4[:, c4, :],
                                             ALU.mult, ALU.add)
                    else:
                        for c4 in range(4):
                            cc = grp * 4 + c4
                            self.cp(raw_s[:, cc, :, 3:11],
                                    pb[:, c4 * 128:(c4 + 1) * 128].rearrange("p (b k) -> p b k", b=16), eng="act")
                        for k in range(4):
                            for c4 in range(4):
                                cc = grp * 4 + c4
                                ca3 = ca4[:, c4, :].rearrange("p (b k) -> p b k", b=16)
                                if k == 0:
                                    self.ts(ca3, raw_s[:, cc, :, 0:8], cw[:, cc, 0:1], ALU.mult, cw[:, cc, 4:5], ALU.add)
                                else:
                                    self.stt(ca3, raw_s[:, cc, :, k:k + 8], cw[:, cc, k:k + 1], ca3, ALU.mult, ALU.add)
                    ca_flat = ca4[:, :, :].rearrange("p c t -> p (c t)")
                    if grp < 2:
                        xa4 = xact4[grp % 2]
                        self.act(xa4[:, :], ca_flat, AF.Silu)
                        pt_ = banks[0]
                        for c4 in range(4):
                            self.tr(pt_[:, c4 * 128:(c4 + 1) * 128], xa4[:, c4 * 128:(c4 + 1) * 128], C["ident_f"])
                        self.cp(x_tok[:, grp * 512:(grp + 1) * 512], pt_[:, :], eng="act")
                    elif grp == 2:
                        self.act(BT[:, :, :].rearrange("p g t -> p (g t)"), ca_flat, AF.Silu)
                        ptb = bankb[0]
                        for g in range(4):
                            self.tr(ptb[:, g * 128:(g + 1) * 128], BT[:, g, :], C["ident_b"][:, :])
                        self.cp(Btok[:, :, :].rearrange("p g t -> p (g t)"), ptb[:, 0:512], eng="act")
                    else:
                        self.act(CT[:, :, :].rearrange("p g t -> p (g t)"), ca_flat, AF.Silu)
                for half in range(2):
                    pz = banks[1 + half]
                    for kc in range(8):
                        self.mm(pz[:, :], lnT[:, kc, :], w_in[:, kc, half * 512:(half + 1) * 512],
                                start=kc == 0, stop=kc == 7)
                    self.act(sz_all[:, half * 512:(half + 1) * 512], pz[:, :], AF.Silu)
                if samp or t == NT - 1:
                    ncol = 48 if samp else 3
                    for cc in range(16):
                        pb = banks[1]
                        if samp:
                            src_ap = raw_s[:, cc, :, 8:11]
                            stg = cacc[cc % 2]
                            self.cp(stg[:, 0:48].rearrange("p (b k) -> p b k", b=16), src_ap)
                            self.tr(pb[0:48, 0:128], stg[:, 0:48], C["ident_f"])
                        else:
                            self.tr(pb[0:3, 0:128], raw[:, cc, 128:131], C["ident_f"])
                        self.cp(cout[0:ncol, cc * 128:(cc + 1) * 128], pb[0:ncol, 0:128], eng="act")
                    oc = V["o_sconv"] if samp else V["o_pconv"]
                    self.store(oc[j, 0:ncol, sw * 1024:(sw + 1) * 1024], cout[0:ncol, 0:1024])
                    self.store(oc[j, 0:ncol, 2048 + sw * 512:2048 + (sw + 1) * 512], cout[0:ncol, 1024:1536])
                    self.store(oc[j, 0:ncol, 3072 + sw * 512:3072 + (sw + 1) * 512], cout[0:ncol, 1536:2048])
                if not samp:
                    self.cp(raw[:, :, 0:3], raw[:, :, 128:131], eng="pool")

                Tm = C["T_s"] if samp else C["T_p"]
                SEGm = C["SEG_s"] if samp else C["SEG_p"]
                self.mm(pdt[:, 16:32], Tm, a_t[:, :], True, True)
                self.mm(pdt[:, 32:48], SEGm, a_t[:, :], True, True)
                self.ts(nacs[:, :], pdt[:, 16:32], -1.0, ALU.mult)
                self.act(dec_in[:, :], pdt[:, 16:32], AF.Exp)
                self.act(cdec[:, :], pdt[:, 32:48], AF.Exp)
                self.tt(dec_end[:, :], pdt[:, 32:48], nacs[:, :], ALU.add)
                self.act(dec_end[:, :], dec_end[:, :], AF.Exp)
                self.cp(a_hi[:, :], a_t[:, :])
                self.tt(a_lo[:, :], a_t[:, :], a_hi[:, :], ALU.subtract)
                Tb = C["Tb_s"] if samp else C["Tb_p"]
                negm = C["negm_s"] if samp else C["negm_p"]
                self.tt(rhs_hi[:, :, :], Tb[:, :].unsqueeze(1).broadcast_to([128, 16, 128]),
                        a_hi[:, :].unsqueeze(2).broadcast_to([128, 16, 128]), ALU.mult)
                self.tt(rhs_lo[:, :, :], Tb[:, :].unsqueeze(1).broadcast_to([128, 16, 128]),
                        a_lo[:, :].unsqueeze(2).broadcast_to([128, 16, 128]), ALU.mult, eng="pool")

                x3 = x_tok[:, :].rearrange("p (h d) -> p h d", h=16)
                self.tt(xdt[:, :].rearrange("p (h d) -> p h d", h=16), x3,
                        dt[:, :].unsqueeze(2).broadcast_to([128, 16, 64]), ALU.mult)
                self.tt(xdd[:, :].rearrange("p (h d) -> p h d", h=16), xdt[:, :].rearrange("p (h d) -> p h d", h=16),
                        dec_end[:, :].unsqueeze(2).broadcast_to([128, 16, 64]), ALU.mult, eng="pool")
                if samp:
                    self.cp(a_bc[:, :].rearrange("p (h d) -> p h d", h=16),
                            a_t[:, :].unsqueeze(2).broadcast_to([128, 16, 64]))
                    pcd = banks[2]
                    for hp in range(8):
                        self.mm(pcd[:, 64 + hp * 16:64 + (hp + 1) * 16], a_bc[:, hp * 128:(hp + 1) * 128],
                                C["rowmask"], True, True)
                    self.act(cd_s[:, :, :].rearrange("p a b -> p (a b)"), pcd[:, 64:192], AF.Exp)

                pout = (banks[6], banks[7])
                for g in range(4):
                    hs = g * 4
                    pL = banks[3]
                    self.mm(pL[:, :], C["ones_b"][:, :], rhs_hi[:, hs:hs + 4, :], True, False)
                    self.mm(pL[:, :], C["ones_b"][:, :], rhs_lo[:, hs:hs + 4, :], False, False)
                    self.mm(pL[:, :], C["ident_b"][:, :], negm[:, :, :], False, True)
                    pC = banks[4]
                    self.mm(pC[:, 0:128], BT[:, g, :], CT[:, g, :], True, True)
                    self.tt(CBm[:, :], pC[:, 0:128], Tm, ALU.mult)
                    for hh in range(4):
                        h = hs + hh
                        lt = LT[h % 2]
                        gt = GT[h % 2]
                        self.act(lt[:, :], pL[:, hh * 128:(hh + 1) * 128], AF.Exp, bias=nacs[:, h:h + 1])
                        self.tt(gt[:, :], lt[:, :], CBm[:, :], ALU.mult)
                        self.mm(pC[:, 128 + hh * 64:128 + (hh + 1) * 64], gt[:, :], xdt[:, h * 64:(h + 1) * 64],
                                True, True)
                    pY = banks[5]
                    if not samp:
                        self.mm(pY[:, 0:256], CT[:, g, :], ST_bf[:, g * 256:(g + 1) * 256], True, True)
                        self.mm(pY[:, 256:512], Btok[:, g, :], xdd[:, g * 256:(g + 1) * 256], True, True)
                        self.cp(yo_sb[:, :], pY[:, 0:256], eng="act")
                        stg_ = ST[:, g * 256:(g + 1) * 256]
                        self.tt(stg_.rearrange("p (h d) -> p h d", h=4), stg_.rearrange("p (h d) -> p h d", h=4),
                                cdec[:, hs:hs + 4].unsqueeze(2).broadcast_to([128, 4, 64]), ALU.mult)
                        self.tt(stg_, stg_, pY[:, 256:512], ALU.add)
                        self.cp(ST_bf[:, g * 256:(g + 1) * 256], stg_, eng="pool")
                    else:
                        sst, osst = V["st_ssm"], V["o_sssm"]
                        for b in range(16):
                            som = soT[b % 2]
                            for hp2 in range(2):
                                hp = g * 2 + hp2
                                so = sold[(b * 2 + hp2) % 4]
                                r0 = (sw * 8 + hp) * 128
                                self.load(so[:, :], sst[j, b, r0:r0 + 128, :])
                                ptr = banks[0]
                                self.tr(ptr[:, 128:256], so[:, :], C["ident_f"])
                                self.cp(som[:, hp2 * 128:(hp2 + 1) * 128], ptr[:, 128:256], eng="act")
                            ctm = CTm[b % 2]
                            self.tt(ctm[:, :], CT[:, g, :], C["colmask_b"][:, b, :], ALU.mult, eng="pool")
                            self.mm(pY[:, 0:256], ctm[:, :], som[:, :], b == 0, b == 15)
                            xm = xdm[b % 2]
                            self.ts(xm[:, :], xdd[:, g * 256:(g + 1) * 256], C["rowmask"][:, b:b + 1], ALU.mult)
                            for hp2 in range(2):
                                hp = g * 2 + hp2
                                so = sold[(b * 2 + hp2) % 4]
                                sn = snew[(b * 2 + hp2) % 4]
                                r0 = (sw * 8 + hp) * 128
                                pS = banks[1]
                                self.mm(pS[:, 256 + hp2 * 128:256 + (hp2 + 1) * 128], xm[:, hp2 * 128:(hp2 + 1) * 128],
                                        Btok[:, g, :], True, True)
                                self.stt(sn[:, :], so[:, :], cd_s[:, hp, b:b + 1],
                                         pS[:, 256 + hp2 * 128:256 + (hp2 + 1) * 128], ALU.mult, ALU.add)
                                self.store(osst[j, b, r0:r0 + 128, :], sn[:, :])
                        self.cp(yo_sb[:, :], pY[:, 0:256], eng="act")
                    for hh in range(4):
                        h = hs + hh
                        ysl = y_g[:, hh * 64:(hh + 1) * 64]
                        self.stt(ysl, yo_sb[:, hh * 64:(hh + 1) * 64], dec_in[:, h:h + 1],
                                 pC[:, 128 + hh * 64:128 + (hh + 1) * 64], ALU.mult, ALU.add)
                        self.stt(ysl, x_tok[:, h * 64:(h + 1) * 64], hp_bc[:, 2, h:h + 1], ysl, ALU.mult, ALU.add)
                    self.tt(y_g[:, :], y_g[:, :], sz_all[:, g * 256:(g + 1) * 256], ALU.mult)
                    self.act(sz[:, :], y_g[:, :], AF.Square, accum=gss[:, :])
                    self.rstd_of(gss[:, :], 256, grs[:, :])
                    self.ts(yn[:, :], y_g[:, :], grs[:, 0:1], ALU.mult)
                    pt2 = bankb[0]
                    for c2 in range(2):
                        self.tr(pt2[:, 512 + c2 * 128:512 + (c2 + 1) * 128], yn[:, c2 * 128:(c2 + 1) * 128],
                                C["ident_b"][:, :])
                    self.cp(ynT[:, :, :], pt2[:, 512:768].rearrange("p (c t) -> p c t", c=2), eng="act")
                    for c2 in range(2):
                        ec = g * 2 + c2
                        for half in range(2):
                            self.mm(pout[half][:, :], ynT[:, c2, :], w_out[:, ec, half * 512:(half + 1) * 512],
                                    start=(ec == 0), stop=(ec == 7))
                hn = h_new[t % 2]
                for half in range(2):
                    self.tt(hn[:, half * 512:(half + 1) * 512], hacc[:, half * 512:(half + 1) * 512],
                            pout[half][:, :], ALU.add)
                r0 = t * 128
                if sw == 0:
                    self.store(mid_t[r0:r0 + 128, :], hn[:, :], wr=[mid_b[t]])
                elif not last:
                    self.store(dst_t[r0:r0 + 128, :], hn[:, :], wr=[dst_b[t]])
                else:
                    self.final_out(hn[:, :], t, V)
            for hp in range(8):
                ptr = banks[0]
                self.tr(ptr[:, 256:384], ST[:, hp * 128:(hp + 1) * 128], C["ident_f"])
                sn = snew[hp % 4]
                self.cp(sn[:, :], ptr[:, 256:384], eng="act")
                r0 = (sw * 8 + hp) * 128
                self.store(V["o_pssm"][j, r0:r0 + 128, :], sn[:, :])
        self.es = old_es
        self._close_layer(es_layer)

    def _close_layer(self, es_layer):
        self.barrier()
        es_layer.close()

    def barrier(self):
        S = self.S
        tags = []
        for e in ("pe", "act", "dve", "pool"):
            if S.count[e] > 0:
                tags.append((e, S.sem[e], S.count[e]))
        for b in S.dma_bufs:
            if b.dlast is not None:
                tags.append(b.dlast)
        for e in S.ENG:
            waits = []
            for (sk, sem, v) in tags:
                if S.waited[e].get(sk, 0) < v:
                    S.waited[e][sk] = v
                    waits.append((sem, v))
            if waits:
                S.ops[e].append((waits, None, None, 0))

    def mla_layer(self, li, j, last, V):
        nc, S, C, T = self.nc, self.S, self.C, self.T
        sb = self.sb
        NT, NTT, SEQ, NPG = self.NT, self.NT + 1, self.SEQ, self.NPG
        banks, bankb = C["banks"], C["bankb"]
        x_in, hb, hbufs = V["x_in"], V["hb"], V["hbufs"]
        es_layer = ExitStack()
        old_es = self.es
        self.es = es_layer
        SCALE = float((64 + 32) ** -0.5)
        if li == 0:
            src_t, src_b = x_in, None
        else:
            src_t, src_b = hb[(li - 1) % 2], hbufs[(li - 1) % 2]
        dst_t, dst_b = hb[li % 2], hbufs[li % 2]

        w_in = sb("m_win", [128, 8, 1856], BF16)
        w_uq = sb("m_wuq", [128, 4, 2048], BF16)
        w_ukz = sb("m_wukz", [128, 16, 256], BF16)
        w_uv = sb("m_wuv", [128, 2, 1024], BF16)
        w_out = sb("m_wout", [128, 8, D], BF16)
        stage = [sb("m_stage%d" % i, [128, 512]) for i in range(2)]
        nrm = sb("m_nrm", [128, 8])
        qnw = sb("m_qnw", [128, 4])
        kvw_bc = sb("m_kvw", [128, 256])
        ckvT = sb("m_ckvT", [128, 2, SEQ], BF16)
        ckv_tok = sb("m_ckvtok", [128, NT, 256], BF16)
        kpeT = sb("m_kpeT", [32, SEQ], BF16)
        ckvT_s = sb("m_ckvTs", [128, 2, 128], BF16)
        ckv_s = sb("m_ckvs", [128, 256], BF16)
        kpeT_s = sb("m_kpeTs", [32, 128], BF16)
        self._ptc = 0
        PTq = [sb("m_PTq%d" % i, [128, 512], BF16) for i in range(3)]
        q_nopeT = sb("m_qnT", [128, 8, 128], BF16)
        q_latT = sb("m_qlT", [128, 2, 16, 128], BF16)
        q_peT = sb("m_qpT", [32, 16, 128], BF16)
        accT = sb("m_accT", [128, 2, 16, 128], BF16)
        cqn = sb("m_cqn", [128, 512], BF16)
        cqT = sb("m_cqT", [128, 4, 128], BF16)
        ckv_f = sb("m_ckvf", [128, 256])
        kpe_f = sb("m_kpef", [128, 32])
        kpe_t = sb("m_kpet", [128, 32])
        kpe_b = sb("m_kpeb", [128, 32], BF16)
        rtok = [sb("m_rtok%d" % i, [128, 64]) for i in range(2)]
        rT = [sb("m_rT%d" % i, [32, 2, 128]) for i in range(2)]
        qp1 = sb("m_qp1", [32, 512])
        sg = sb("m_sg", [128, D])
        og = sb("m_og", [128, D], BF16)
        oT = sb("m_oT", [128, 8, 128], BF16)
        l_tok = sb("m_ltok", [128, 16])
        rl = sb("m_rl", [128, 16])
        ss2 = sb("m_ss2", [128, 1])
        rs2 = sb("m_rs2", [128, 1])
        ss3 = sb("m_ss3", [128, 1])
        rs3 = sb("m_rs3", [128, 1])
        amask_b = sb("m_amask", [128, 16, 128], BF16)
        idx = sb("m_idx", [128, 16 * NPG], I32)
        pg_c = [sb("m_pgc%d" % i, [128, 288]) for i in range(4)]
        pg_b = [sb("m_pgb%d" % i, [128, 256], BF16) for i in range(4)]
        pgT = [sb("m_pgT%d" % i, [128, 2, 128], BF16) for i in range(4)]
        pkT = [sb("m_pkT%d" % i, [32, 128], BF16) for i in range(4)]
        PTs = [sb("m_PTs%d" % i, [128, 128], BF16) for i in range(4)]
        rlb = sb("m_rlb", [128, 128])
        if last:
            self.yfin = [sb("m_yfin", [128, D])] * 2
            self.ssq2 = sb("m_ssq2", [128, 1])
            self.rstd2 = sb("m_rstd2", [128, 1])
        h_in, h_new = T["h_in"], T["h_new"]

        self.load(nrm[:, :], V["mla_nrm"][j, :, :])
        self.load(qnw[:, :], V["mla_qnw"][j, :, :])
        self.load(kvw_bc[:, :], V["mla_kvw"][j, :, :].rearrange("a d -> (a d)").partition_broadcast(128))
        for hh in range(4):
            self.load(stage[hh % 2][:, :], V["cst"][:, 785 + 2048 + hh * 512:785 + 2048 + (hh + 1) * 512])
            self.cp(amask_b[:, hh * 4:(hh + 1) * 4, :], stage[hh % 2][:, :].rearrange("p (b t) -> p b t", b=4))
        mw, wuq, wuk, wuv, wo = V["mla_win"], V["mla_wuq"], V["mla_wuk"], V["mla_wuv"], V["mla_wout"]
        self.load_weight_bf(w_in, lambda kc, c0, w: mw[j, kc * 128:(kc + 1) * 128, c0:c0 + w],
                            8, 1856, lambda kc: nrm[:, kc:kc + 1], stage, ["dve", "act"])
        self.load_weight_bf(w_uq, lambda kc, c0, w: wuq[j, kc * 128:(kc + 1) * 128, c0:c0 + w],
                            4, 2048, lambda kc: qnw[:, kc:kc + 1], stage, ["dve", "act"])
        self.load_weight_bf(w_uv, lambda kc, c0, w: wuv[j, kc * 128:(kc + 1) * 128, c0:c0 + w],
                            2, 1024, None, stage, ["dve", "act"])
        self.load_weight_bf(w_out, lambda kc, c0, w: wo[j, kc * 128:(kc + 1) * 128, c0:c0 + w],
                            8, D, None, stage, ["dve", "act"])
        self.memset(w_ukz[:, :, :].rearrange("p h c -> p (h c)"), 0.0)
        wz4 = w_ukz[:, :, :].rearrange("p (a two) c -> p a two c", two=2)
        for ci, c0 in enumerate(range(0, 2048, 512)):
            st = stage[ci % 2]
            self.load(st[:, :], wuk[j, :, :, :].rearrange("p a c -> p (a c)")[:, c0:c0 + 512])
            self.cp(wz4[0:64, ci * 2:(ci + 1) * 2, 0, :], st[0:64, :].rearrange("p (a c) -> p a c", a=2))
            self.cp(wz4[64:128, ci * 2:(ci + 1) * 2, 1, :], st[64:128, :].rearrange("p (a c) -> p a c", a=2))
        NI = 16 * NPG
        import os as _os
        _dbg = _os.environ.get("K_DBG", "")
        for c0 in range(0, 0 if "noidx" in _dbg else NI, 512):
            w = min(512, NI - c0)
            st = stage[(c0 // 512) % 2]
            sti = st[:, 0:w].bitcast(I32)
            self.load(sti, V["ptab"][:, c0:c0 + w].rearrange("a n -> (a n)").partition_broadcast(128))
            self.cp(idx[:, c0:c0 + w].bitcast(F32), sti)
            self.ts(idx[:, c0:c0 + w].bitcast(F32), idx[:, c0:c0 + w].bitcast(F32), 128.0, ALU.mult,
                    C["iota_p"], ALU.add)
            if j > 0:
                self.ts(idx[:, c0:c0 + w].bitcast(F32), idx[:, c0:c0 + w].bitcast(F32),
                        float(j * self.NPOOL * 128), ALU.add)
            self.cp(idx[:, c0:c0 + w], idx[:, c0:c0 + w].bitcast(F32))
        pool_c = V["pool_ckv"]
        o_ckv, o_kpe = V["o_pckv"], V["o_pkpe"]
        rope_tok, rope_T = V["rope_tok"], V["rope_T"]

        def load_tile(t):
            r0 = t * 128
            self.load(h_in[t % 2][:, :], src_t[r0:r0 + 128, :], rd=([src_b[t]] if src_b is not None else []))
            self.load(rtok[t % 2][:, :], rope_tok[r0:r0 + 128, :])
            self.load(rT[t % 2][:, :, :], rope_T[:, :, r0:r0 + 128])

        _skip = _os.environ.get("K_SKIP", "")
        for t in range(NTT):
            samp = t == NT
            load_tile(t)
            hin = h_in[t % 2]
            r0 = t * 128
            lnT = self.norm_and_transpose(hin[:, :], 0)
            pq, pk, pg0, pg1 = banks[1], banks[2], banks[3], banks[4]
            for kc in range(8):
                self.mm(pq[:, :], lnT[:, kc, :], w_in[:, kc, 0:512], start=kc == 0, stop=kc == 7)
            for kc in range(8):
                self.mm(pk[:, 0:320], lnT[:, kc, :], w_in[:, kc, 512:832], start=kc == 0, stop=kc == 7)
            for kc in range(8):
                self.mm(pg0[:, :], lnT[:, kc, :], w_in[:, kc, 832:1344], start=kc == 0, stop=kc == 7)
            for kc in range(8):
                self.mm(pg1[:, :], lnT[:, kc, :], w_in[:, kc, 1344:1856], start=kc == 0, stop=kc == 7)
            if "A" not in _skip:
                self.act(T["sq_junk"][:, 0:512], pq[:, :], AF.Square, accum=ss2[:, :])
                self.rstd_of(ss2[:, :], 512, rs2[:, :])
                self.ts(cqn[:, :], pq[:, :], rs2[:, 0:1], ALU.mult)
                ptb = bankb[0]
                for kc in range(4):
                    self.tr(ptb[:, kc * 128:(kc + 1) * 128], cqn[:, kc * 128:(kc + 1) * 128], C["ident_b"][:, :])
                self.cp(cqT[:, :, :], ptb[:, 0:512].rearrange("p (k t) -> p k t", k=4), eng="act")
            if "B" not in _skip:
                self.act(T["sq_junk"][:, 0:256], pk[:, 0:256], AF.Square, accum=ss3[:, :])
                self.rstd_of(ss3[:, :], 256, rs3[:, :])
                self.stt(ckv_f[:, :], pk[:, 0:256], rs3[:, 0:1], kvw_bc[:, :], ALU.mult, ALU.mult)
                self.store(o_ckv[j, r0:r0 + 128, :], ckv_f[:, :])
                self.tt(kpe_f[:, :], pk[:, 256:288], rtok[t % 2][:, 0:32], ALU.mult)
                self.tt(kpe_t[:, :], pk[:, 288:320], rtok[t % 2][:, 32:64], ALU.mult)
                self.tt(kpe_f[:, :], kpe_f[:, :], kpe_t[:, :], ALU.add)
                self.store(o_kpe[j, r0:r0 + 128, :], kpe_f[:, :])
                self.cp(kpe_b[:, :], kpe_f[:, :])
                kv_dst = ckv_s[:, :] if samp else ckv_tok[:, t, :]
                self.cp(kv_dst, ckv_f[:, :], eng="pool")
                for cc in range(2):
                    self.tr(ptb[:, 512 + cc * 128:512 + (cc + 1) * 128], kv_dst[:, cc * 128:(cc + 1) * 128]
                            if False else (ckv_s[:, cc * 128:(cc + 1) * 128] if samp else ckv_tok[:, t, cc * 128:(cc + 1) * 128]),
                            C["ident_b"][:, :])
                kT_dst = ckvT_s[:, :, :] if samp else ckvT[:, :, r0:r0 + 128]
                self.cp(kT_dst, ptb[:, 512:768].rearrange("p (c t) -> p c t", c=2), eng="act")
                self.tr(ptb[0:32, 768:896], kpe_b[:, :], C["ident_b"][:, :])
                kp_dst = kpeT_s[:, :] if samp else kpeT[:, r0:r0 + 128]
                self.cp(kp_dst, ptb[0:32, 768:896], eng="act")
            if "C" not in _skip:
                self.act(sg[:, 0:512], pg0[:, :], AF.Silu)
                self.act(sg[:, 512:1024], pg1[:, :], AF.Silu)
            if "D" not in _skip:
                for hp in range(8):
                    pb = banks[1]
                    for kc in range(4):
                        self.mm(pb[:, 0:128], w_uq[:, kc, hp * 128:(hp + 1) * 128], cqT[:, kc, :],
                                start=kc == 0, stop=kc == 3)
                    self.cp(q_nopeT[:, hp, :], pb[:, 0:128], eng="act")
            if "E" not in _skip:
                for q4 in range(4):
                    pa, pbk = banks[2], banks[3]
                    for hh in range(4):
                        h = q4 * 4 + hh
                        for kc in range(4):
                            self.mm(pa[0:32, hh * 128:(hh + 1) * 128], w_uq[:, kc, 1024 + h * 32:1024 + (h + 1) * 32],
                                    cqT[:, kc, :], start=kc == 0, stop=kc == 3)
                        for kc in range(4):
                            self.mm(pbk[0:32, hh * 128:(hh + 1) * 128], w_uq[:, kc, 1536 + h * 32:1536 + (h + 1) * 32],
                                    cqT[:, kc, :], start=kc == 0, stop=kc == 3)
                    cosb = rT[t % 2][:, 0, :].unsqueeze(1).broadcast_to([32, 4, 128])
                    sinb = rT[t % 2][:, 1, :].unsqueeze(1).broadcast_to([32, 4, 128])
                    self.tt(qp1[:, :].rearrange("p (h t) -> p h t", h=4), pa[0:32, :].rearrange("p (h t) -> p h t", h=4),
                            cosb, ALU.mult)
                    self.tt(pbk[0:32, :].rearrange("p (h t) -> p h t", h=4), pbk[0:32, :].rearrange("p (h t) -> p h t", h=4),
                            sinb, ALU.mult)
                    self.tt(q_peT[:, q4 * 4:(q4 + 1) * 4, :].rearrange("p h t -> p (h t)"), qp1[:, :], pbk[0:32, :], ALU.add)
            if "F" not in _skip:
                for q4 in range(4):
                    for cc in range(2):
                        pb = banks[1 + (q4 * 2 + cc) % 2]
                        for hh in range(4):
                            h = q4 * 4 + hh
                            hp = h // 2
                            self.mm(pb[:, hh * 128:(hh + 1) * 128], w_ukz[:, h, cc * 128:(cc + 1) * 128],
                                    q_nopeT[:, hp, :], True, True)
                        self.cp(q_latT[:, cc, q4 * 4:(q4 + 1) * 4, :].rearrange("p h t -> p (h t)"), pb[:, :], eng="act")
            if not samp:
                nk = t + 1
                for q4 in range(4):
                    h0 = q4 * 4
                    pacc = (banks[5], banks[6]) if q4 % 2 == 0 else (banks[1], banks[2])
                    pl = banks[7]

                    def scores(jk, h0=h0):
                        ps = banks[3 + jk % 2]
                        k0 = jk * 128
                        self.mm(ps[:, :], ckvT[:, 0, k0:k0 + 128], q_latT[:, 0, h0:h0 + 4, :], True, False)
                        self.mm(ps[:, :], ckvT[:, 1, k0:k0 + 128], q_latT[:, 1, h0:h0 + 4, :], False, False)
                        self.mm(ps[:, :], kpeT[:, k0:k0 + 128], q_peT[:, h0:h0 + 4, :], False, True)

                    scores(0)
                    for jk in range(nk):
                        if jk + 1 < nk:
                            scores(jk + 1)
                        pt = PTq[self._ptc % 3]
                        self._ptc += 1
                        self.act(pt[:, :], banks[3 + jk % 2][:, :], AF.Exp, scale=SCALE)
                        if jk == t:
                            self.tt(pt[:, :].rearrange("p (h t) -> p h t", h=4), pt[:, :].rearrange("p (h t) -> p h t", h=4),
                                    C["Tb_p"][:, :].unsqueeze(1).broadcast_to([128, 4, 128]), ALU.mult)
                        for cc in range(2):
                            self.mm(pacc[cc][:, :], ckv_tok[:, jk, cc * 128:(cc + 1) * 128], pt[:, :],
                                    jk == 0, jk == nk - 1)
                        for hh in range(4):
                            self.mm(pl[:, h0 + hh:h0 + hh + 1], pt[:, hh * 128:(hh + 1) * 128], C["ones_b"][:, 0:1],
                                    jk == 0 and hh == 0, jk == nk - 1 and hh == 3)
                    for cc in range(2):
                        self.cp(accT[:, cc, h0:h0 + 4, :].rearrange("p h t -> p (h t)"), pacc[cc][:, :], eng="act")
                self.cp(l_tok[:, :], banks[7][:, 0:16])
            else:
                self.memset(l_tok[:, :], 1.0)
                steps = [(b, pgi) for b in range(16) for pgi in range(NPG)]
                NST = len(steps)

                def gather(i):
                    b_, pg_ = steps[i]
                    col = b_ * NPG + pg_
                    dst = pg_c[i % 4]
                    self.S.dma("pool", (lambda e, o_=dst[:, :], i_=pool_c[:, :], x_=idx[:, col:col + 1]:
                                        e.indirect_dma_start(out=o_, out_offset=None, in_=i_,
                                                             in_offset=bass.IndirectOffsetOnAxis(ap=x_, axis=0))),
                               S.tbuf(dst[:, :]), reads=[S.tbuf(idx[:, :])], writes=[S.tbuf(dst[:, :])])

                for i in range(min(3, NST)):
                    gather(i)
                pacc = (banks[5], banks[6])
                pl = banks[7]
                si = 0
                for b in range(16):
                    tsl = slice(b * 8, b * 8 + 8)
                    for pgi in range(NPG + 1):
                        new = pgi == NPG
                        if not new:
                            if si + 3 < NST:
                                gather(si + 3)
                            s4 = si % 4
                            pgf = pg_c[s4]
                            self.cp(pg_b[s4][:, :], pgf[:, 0:256])
                            ptf = banks[1 + si % 2]
                            for cc in range(2):
                                self.tr(ptf[:, cc * 128:(cc + 1) * 128], pgf[:, cc * 128:(cc + 1) * 128], C["ident_f"])
                            self.tr(ptf[0:32, 256:384], pgf[:, 256:288], C["ident_f"])
                            self.cp(pgT[s4][:, :, :], ptf[:, 0:256].rearrange("p (c t) -> p c t", c=2), eng="act")
                            self.cp(pkT[s4][:, :], ptf[0:32, 256:384], eng="act")
                            kT0, kT1, kP, kV = pgT[s4][:, 0, :], pgT[s4][:, 1, :], pkT[s4][:, :], pg_b[s4]
                            si += 1
                        else:
                            kT0, kT1, kP, kV = ckvT_s[:, 0, :], ckvT_s[:, 1, :], kpeT_s[:, :], ckv_s
                        ps = banks[3 + self._ptc % 2]
                        self.mm(ps[:, 0:128], kT0, q_latT[:, 0, :, tsl], True, False)
                        self.mm(ps[:, 0:128], kT1, q_latT[:, 1, :, tsl], False, False)
                        self.mm(ps[:, 0:128], kP, q_peT[:, :, tsl], False, True)
                        pt = PTs[self._ptc % 4]
                        self._ptc += 1
                        self.act(pt[:, :], ps[:, 0:128], AF.Exp, scale=SCALE)
                        if new:
                            self.tt(pt[:, :], pt[:, :], amask_b[:, b, :], ALU.mult)
                        for cc in range(2):
                            self.mm(pacc[cc][:, 0:128], kV[:, cc * 128:(cc + 1) * 128], pt[:, :], pgi == 0, new)
                        self.mm(pl[:, 128:256], C["ones_b"][:, :], pt[:, :], pgi == 0, new)
                    self.recip(rlb[:, :], pl[:, 128:256])
                    for cc in range(2):
                        self.tt(accT[:, cc, :, tsl], pacc[cc][:, 0:128].rearrange("p (h k) -> p h k", h=16),
                                rlb[:, :].rearrange("p (h k) -> p h k", h=16), ALU.mult)
            po = (banks[1], banks[2])
            for h in range(16):
                for cc in range(2):
                    self.mm(po[h // 8][:, (h % 8) * 64:(h % 8 + 1) * 64], accT[:, cc, h, :],
                            w_uv[:, cc, h * 64:(h + 1) * 64], cc == 0, cc == 1)
            self.recip(rl[:, :], l_tok[:, :])
            for half in range(2):
                self.tt(sg[:, half * 512:(half + 1) * 512], po[half][:, :], sg[:, half * 512:(half + 1) * 512], ALU.mult)
                self.tt(og[:, half * 512:(half + 1) * 512].rearrange("p (h v) -> p h v", h=8),
                        sg[:, half * 512:(half + 1) * 512].rearrange("p (h v) -> p h v", h=8),
                        rl[:, half * 8:(half + 1) * 8].unsqueeze(2).broadcast_to([128, 8, 64]), ALU.mult)
            for kc in range(8):
                self.tr(ptb[:, kc * 128:(kc + 1) * 128], og[:, kc * 128:(kc + 1) * 128], C["ident_b"][:, :])
            self.cp(oT[:, :, :], ptb[:, 0:1024].rearrange("p (k t) -> p k t", k=8), eng="act")
            pout = (banks[5], banks[6])
            for half in range(2):
                for kc in range(8):
                    self.mm(pout[half][:, :], oT[:, kc, :], w_out[:, kc, half * 512:(half + 1) * 512],
                            start=kc == 0, stop=kc == 7)
            hn = h_new[t % 2]
            for half in range(2):
                self.tt(hn[:, half * 512:(half + 1) * 512], hin[:, half * 512:(half + 1) * 512],
                        pout[half][:, :], ALU.add)
            if not last:
                self.store(dst_t[r0:r0 + 128, :], hn[:, :], wr=[dst_b[t]])
            else:
                self.final_out(hn[:, :], t, V)
        self.es = old_es
        self._close_layer(es_layer)


def make_consts(SEQ):
    P = 128
    idx = np.arange(P)
    ident = np.eye(P, dtype=np.float32)
    T_p = (idx[:, None] <= idx[None, :]).astype(np.float32)
    same = (idx[:, None] // 8 == idx[None, :] // 8)
    T_s = (T_p.astype(bool) & same).astype(np.float32)
    SEG_p = np.ones((P, P), np.float32)
    SEG_s = same.astype(np.float32)
    ones = np.ones((P, P), np.float32)
    rowmask = (idx[:, None] // 8 == np.arange(16)[None, :]).astype(np.float32)
    colmask = np.broadcast_to((np.arange(16)[:, None] == (idx[None, :] // 8))[None], (P, 16, P))
    colmask = colmask.astype(np.float32).reshape(P, 2048)
    kb, kt = idx // 8, idx % 8
    tok = np.arange(128) % 8
    am = (kb[:, None, None] == np.arange(16)[None, :, None]) & (kt[:, None, None] <= tok[None, None, :])
    am = am.astype(np.float32).reshape(P, 2048)
    iota = idx.astype(np.float32)[:, None]
    cst = np.concatenate([ident, T_p, T_s, SEG_p, SEG_s, ones, rowmask, iota, colmask, am], axis=1)
    return np.ascontiguousarray(cst.astype(np.float32))


def rope_tables(SEQ, past_len):
    pos = np.concatenate([np.arange(SEQ, dtype=np.float64),
                          np.tile(past_len + np.arange(8, dtype=np.float64), 16)])
    inv = 1.0 / (10000.0 ** (np.arange(0, 32, 2, dtype=np.float64) / 32))
    ang = pos[:, None] * inv[None, :]
    cos, sin = np.cos(ang), np.sin(ang)
    tok = np.concatenate([cos, cos, -sin, sin], axis=1).astype(np.float32)
    rt = np.stack([np.concatenate([cos, cos], 1).T, np.concatenate([-sin, sin], 1).T], axis=1)
    return np.ascontiguousarray(tok), np.ascontiguousarray(rt.astype(np.float32))


_PROG_CACHE = {}


def _get_prog(cfg):
    key = (cfg["SEQ"], cfg["NPG"], cfg["NPOOL"], tuple(cfg["LAYERS"]))
    if key not in _PROG_CACHE:
        p = Prog(cfg)
        p.build()
        _PROG_CACHE[key] = p
    return _PROG_CACHE[key]


def run_cfg(cfg, inputs, n_cores, past_len):
    f = lambda a: np.ascontiguousarray(np.asarray(a))
    SEQ, NPG, NPOOL = cfg["SEQ"], cfg["NPG"], cfg["NPOOL"]
    LAYERS = cfg["LAYERS"]
    prog = _get_prog(cfg)
    nS, nM = prog.n_ssd, prog.n_mla
    ROWS = SEQ + 128
    x_prompt, x_sample = f(inputs["x_prompt"]), f(inputs["x_sample"])
    cst = make_consts(SEQ)
    rope_tok, rope_T = rope_tables(SEQ, past_len)
    shared = {"cst": cst, "rope_tok": rope_tok, "rope_T": rope_T,
              "fnw": f(inputs["final_norm_w"]).reshape(1, D)}
    norm_w = f(inputs["norm_w"])
    ssd_idx = [i for i, k in enumerate(LAYERS) if k == "ssd"]
    mla_idx = [i for i, k in enumerate(LAYERS) if k == "mla"]
    if nS:
        w_in = f(inputs["ssd_w_in"]); w_out = f(inputs["ssd_w_out"])
        cwv = f(inputs["ssd_conv_w"]); cbv = f(inputs["ssd_conv_b"])
        dtb = f(inputs["ssd_dt_bias"]); alog = f(inputs["ssd_a_log"]); dsk = f(inputs["ssd_d"])
        gnw = f(inputs["ssd_norm_w"])
        win_l, wout_l, gnw_l, cw_l, hp_l = [], [], [], [], []
        for j in range(nS):
            for sw in range(2):
                z = w_in[j][:, sw * 1024:(sw + 1) * 1024]
                x = w_in[j][:, 2048 + sw * 1024:2048 + (sw + 1) * 1024]
                Bc = w_in[j][:, 4096 + sw * 512:4096 + (sw + 1) * 512]
                Cc = w_in[j][:, 5120 + sw * 512:5120 + (sw + 1) * 512]
                dtc = w_in[j][:, 6144 + sw * 16:6144 + (sw + 1) * 16]
                win_l.append(np.concatenate([z, x, Bc, Cc, dtc], axis=1))
                wout_l.append(w_out[j][sw * 1024:(sw + 1) * 1024, :])
                gnw_l.append(gnw[j][sw * 1024:(sw + 1) * 1024].reshape(8, 128).T)
                ch = np.concatenate([np.arange(sw * 1024, (sw + 1) * 1024),
                                     2048 + np.arange(sw * 512, (sw + 1) * 512),
                                     3072 + np.arange(sw * 512, (sw + 1) * 512)])
                cwb = np.concatenate([cwv[j][:, ch], cbv[j][None, ch]], axis=0)
                cw_l.append(cwb.reshape(5, 16, 128).transpose(2, 1, 0))
                hp_l.append(np.stack([dtb[j][sw * 16:(sw + 1) * 16], alog[j][sw * 16:(sw + 1) * 16],
                                      dsk[j][sw * 16:(sw + 1) * 16]]))
        shared["ssd_win"] = f(np.stack(win_l)); shared["ssd_wout"] = f(np.stack(wout_l))
        shared["ssd_gnw"] = f(np.stack(gnw_l)); shared["ssd_cw"] = f(np.stack(cw_l))
        shared["ssd_hp"] = f(np.stack(hp_l))
        shared["ssd_nrm"] = f(np.stack([norm_w[i].reshape(8, 128).T for i in ssd_idx]))
    if nM:
        mw = f(inputs["mla_w_in"])
        perm = np.concatenate([np.arange(16, 32), np.arange(0, 16)])
        win_l = []
        for jm in range(nM):
            kpe = mw[jm][:, 768:800]
            win_l.append(np.concatenate([mw[jm][:, 0:768], kpe, kpe[:, perm], mw[jm][:, 800:1824]], axis=1))
        shared["mla_win"] = f(np.stack(win_l))
        shared["mla_nrm"] = f(np.stack([norm_w[i].reshape(8, 128).T for i in mla_idx]))
        shared["mla_qnw"] = f(np.stack([f(inputs["mla_q_norm_w"])[jm].reshape(4, 128).T for jm in range(nM)]))
        shared["mla_kvw"] = f(f(inputs["mla_kv_norm_w"])[:nM].reshape(nM, 1, 256))
        wuq = f(inputs["mla_w_uq"])
        nope = wuq[:nM, :, :, 0:64].reshape(nM, 512, 1024)
        pe = wuq[:nM, :, :, 64:96]
        shared["mla_wuq"] = f(np.concatenate([nope, pe.reshape(nM, 512, 512),
                                              pe[..., perm].reshape(nM, 512, 512)], axis=2))
        wuk = f(inputs["mla_w_uk"])[:nM]
        t_ = wuk.transpose(0, 2, 3, 1).reshape(nM, 8, 2, 64, 256)
        shared["mla_wuk"] = f(t_.transpose(0, 2, 3, 1, 4).reshape(nM, 128, 8, 256))
        shared["mla_wuv"] = f(f(inputs["mla_w_uv"])[:nM].reshape(nM, 256, 1024))
        shared["mla_wout"] = f(inputs["mla_w_out"])[:nM]
        shared["pool_all"] = np.concatenate([f(inputs["cache_ckv"])[:nM].reshape(nM * NPOOL * 128, 256),
                                             f(inputs["cache_kpe"])[:nM].reshape(nM * NPOOL * 128, 32)], axis=1)
    in_maps = []
    for c in range(n_cores):
        k = c // 2
        m = dict(shared)
        m["x_in"] = f(np.concatenate([x_prompt[k], x_sample[16 * c:16 * c + 16].reshape(128, D)], axis=0))
        if nS:
            m["st_ssm"] = f(inputs["state_ssm"])[:nS, 16 * c:16 * c + 16].reshape(nS, 16, 2048, 128)
            m["st_conv"] = f(inputs["state_conv"])[:nS, 16 * c:16 * c + 16].reshape(nS, 48, 4096)
        if nM:
            m["ptab"] = f(inputs["page_table"])[16 * c:16 * c + 16].reshape(1, 16 * NPG).astype(np.int32)
        in_maps.append(m)
    import os as _os
    if _os.environ.get("K_TRACE"):
        res = run_bass_kernel_spmd(prog.nc, in_maps, core_ids=list(range(n_cores)), trace=True)
        print("EXEC_TIME_NS", res.exec_time_ns, {e: len(v) for e, v in prog.S.ops.items()})
    else:
        res = run_bass_kernel_spmd(prog.nc, in_maps, core_ids=list(range(n_cores)))
    R = res.results
    nseq = n_cores // 2
    out = {}
    out["y_prompt"] = np.stack([R[2 * k]["y_out"][:SEQ] for k in range(nseq)])
    out["y_sample"] = np.concatenate([R[c]["y_out"][SEQ:].reshape(16, 8, D) for c in range(n_cores)])
    if nS:
        out["p_ssm"] = np.stack([R[2 * k]["o_pssm"].reshape(nS, 32, 64, 128) for k in range(nseq)], axis=1)
        out["p_conv"] = np.stack([R[2 * k]["o_pconv"] for k in range(nseq)], axis=1)
        out["s_ssm"] = np.concatenate([R[c]["o_sssm"].reshape(nS, 16, 32, 64, 128) for c in range(n_cores)], axis=1)
        out["s_conv"] = np.concatenate([R[c]["o_sconv"].reshape(nS, 16, 3, 4096) for c in range(n_cores)], axis=1)
    if nM:
        out["p_ckv"] = np.stack([R[2 * k]["o_ckv"][:, :SEQ] for k in range(nseq)], axis=1)
        out["p_kpe"] = np.stack([R[2 * k]["o_kpe"][:, :SEQ] for k in range(nseq)], axis=1)
        out["s_ckv"] = np.concatenate([R[c]["o_ckv"][:, SEQ:].reshape(nM, 16, 8, 256) for c in range(n_cores)], axis=1)
        out["s_kpe"] = np.concatenate([R[c]["o_kpe"][:, SEQ:].reshape(nM, 16, 8, 32) for c in range(n_cores)], axis=1)
    return out


def kernel(**inputs):
    out = run_cfg(FULL_CFG, inputs, 8, 8192)
    return (out["y_prompt"], out["y_sample"], out["p_ssm"], out["p_conv"], out["p_ckv"], out["p_kpe"],
            out["s_ssm"], out["s_conv"], out["s_ckv"], out["s_kpe"])
```

```python
import math
from contextlib import ExitStack

import numpy as np
import concourse.bass as bass
import concourse.mybir as mybir
from concourse.bass_utils import run_bass_kernel_spmd

F32 = mybir.dt.float32
BF16 = mybir.dt.bfloat16
I32 = mybir.dt.int32
AF = mybir.ActivationFunctionType
ALU = mybir.AluOpType

D = 1024
EPS = 1e-6
NEG = -1.0e5

FULL_CFG = dict(SEQ=4096, NPG=64, NPOOL=10240, LAYERS=("ssd", "mla", "ssd", "mla"))


class Buf:
    __slots__ = ("name", "w", "r", "dsem", "dcount", "dlast", "excl")

    def __init__(self, name):
        self.name = name
        self.w = None
        self.r = []
        self.dsem = None
        self.dcount = 0
        self.dlast = None
        self.excl = False


class Sched:
    ENG = ("pe", "act", "dve", "pool", "sp")

    def __init__(self, nc, es):
        self.nc = nc
        self.es = es
        self.ops = {e: [] for e in self.ENG}
        self.sem = {}
        for e in ("pe", "act", "dve", "pool"):
            self.sem[e] = es.enter_context(nc.semaphore("sem_" + e))
        self.count = {e: 0 for e in self.ENG}
        self.waited = {e: {} for e in self.ENG}
        self.bufs = {}
        self.dma_bufs = []
        self.nsem = 4

    def buf(self, name):
        b = Buf(name)
        return b

    def tbuf(self, ap):
        n = ap.tensor.name
        if n not in self.bufs:
            self.bufs[n] = Buf(n)
        return self.bufs[n]

    def _deps(self, eng, reads, writes):
        deps = []
        for b in reads:
            if b.w is not None:
                deps.append(b.w)
        for b in writes:
            if b.w is not None:
                deps.append(b.w)
            deps.extend(b.r)
        waits = []
        wd = self.waited[eng]
        best = {}
        for (sk, sem, v) in deps:
            if sk == eng and eng == "pe":
                continue
            if wd.get(sk, 0) >= v:
                continue
            if sk not in best or best[sk][1] < v:
                best[sk] = (sem, v)
        for sk, (sem, v) in best.items():
            wd[sk] = v
            waits.append((sem, v))
        return waits

    def op(self, eng, fn, reads=(), writes=()):
        writes = list(dict.fromkeys(list(writes) + [b for b in reads if b.excl]))
        reads = list(dict.fromkeys(b for b in reads if not b.excl))
        waits = self._deps(eng, reads, writes)
        self.count[eng] += 1
        tag = (eng, self.sem[eng], self.count[eng])
        self.ops[eng].append((waits, fn, self.sem[eng], 1))
        for b in reads:
            b.r.append(tag)
        for b in writes:
            b.w = tag
            b.r = []
        return tag

    def dma(self, queue, fn, sb, reads=(), writes=()):
        if sb.dsem is None:
            sb.dsem = self.es.enter_context(self.nc.semaphore("d_" + sb.name))
            self.dma_bufs.append(sb)
            self.nsem += 1
        reads = list(dict.fromkeys(reads))
        writes = list(dict.fromkeys(writes))
        waits = self._deps(queue, reads, writes)
        if sb.dlast is not None:
            sk, sem, v = sb.dlast
            if self.waited[queue].get(sk, 0) < v:
                self.waited[queue][sk] = v
                waits.append((sem, v))
        sb.dcount += 1
        tag = ("d_" + sb.name, sb.dsem, 16 * sb.dcount)
        sb.dlast = tag
        self.ops[queue].append((waits, fn, sb.dsem, 16))
        for b in reads:
            b.r.append(tag)
        for b in writes:
            b.w = tag
            b.r = []
        return tag

    def finish(self):
        waits = []
        for b in self.dma_bufs:
            waits.append((b.dsem, 16 * b.dcount))
        self.ops["sp"].append((waits, None, None, 0))

    def emit(self):
        nc = self.nc
        ops = self.ops

        def run(e, lst):
            for (waits, fn, sem, amt) in lst:
                for (s, v) in waits:
                    e.wait_ge(s, v)
                if fn is not None:
                    inst = fn(e)
                    inst.then_inc(sem, amt)

        with nc.Block() as block:
            @block.sync
            def _(e):
                run(e, ops["sp"])

            @block.scalar
            def _(e):
                run(e, ops["act"])

            @block.vector
            def _(e):
                run(e, ops["dve"])

            @block.gpsimd
            def _(e):
                run(e, ops["pool"])

            @block.tensor
            def _(e):
                run(e, ops["pe"])


class Prog:
    def __init__(self, cfg):
        self.cfg = cfg
        self.SEQ = cfg["SEQ"]
        self.NT = self.SEQ // 128
        self.NPG = cfg["NPG"]
        self.NPOOL = cfg["NPOOL"]
        self.LAYERS = cfg["LAYERS"]
        self.n_ssd = sum(1 for l in self.LAYERS if l == "ssd")
        self.n_mla = sum(1 for l in self.LAYERS if l == "mla")
        self.nc = bass.Bass("TRN2", target_bir_lowering=False)
        self.es = ExitStack()
        self.S = None
        self.dram = {}

    def din(self, name, shape, dt=F32):
        t = self.nc.dram_tensor(name, list(shape), dt, kind="ExternalInput")
        self.dram[name] = t
        return t

    def dout(self, name, shape, dt=F32):
        t = self.nc.dram_tensor(name, list(shape), dt, kind="ExternalOutput")
        self.dram[name] = t
        return t

    def dint(self, name, shape, dt=F32):
        t = self.nc.dram_tensor(name, list(shape), dt, kind="Internal")
        self.dram[name] = t
        return t

    def sb(self, name, shape, dt=F32):
        self._uid = getattr(self, "_uid", 0) + 1
        return self.es.enter_context(self.nc.sbuf_tensor("%s_%d" % (name, self._uid), list(shape), dt))

    def _rw(self, ins, outs, rd, wr):
        S = self.S
        reads = list(rd) if rd is not None else []
        writes = list(wr) if wr is not None else []
        if rd is None:
            reads = [S.tbuf(a) for a in ins if a is not None and not isinstance(a, (int, float))]
        if wr is None:
            writes = [S.tbuf(a) for a in outs if a is not None]
        return reads, writes

    def mm(self, out, lhsT, rhs, start=True, stop=True, rd=None, wr=None, xrd=()):
        reads, writes = self._rw([lhsT, rhs], [out], rd, wr)
        reads += list(xrd)
        self.S.op("pe", lambda e: e.matmul(out, lhsT, rhs, start=start, stop=stop,
                                           skip_group_check=True), reads, writes)

    def tr(self, out, in_, ident, rd=None, wr=None):
        reads, writes = self._rw([in_, ident], [out], rd, wr)
        self.S.op("pe", lambda e: e.transpose(out, in_, ident), reads, writes)

    def act(self, out, in_, func, bias=0.0, scale=1.0, accum=None, eng="act", rd=None, wr=None):
        ins = [in_]
        if not isinstance(bias, (int, float)):
            ins.append(bias)
        if not isinstance(scale, (int, float)):
            ins.append(scale)
        outs = [out] + ([accum] if accum is not None else [])
        reads, writes = self._rw(ins, outs, rd, wr)
        if accum is None:
            self.S.op("act", lambda e: e.activation(out, in_, func, bias=bias, scale=scale), reads, writes)
        else:
            self.S.op("act", lambda e: e.activation(out, in_, func, bias=bias, scale=scale,
                                                    accum_out=accum), reads, writes)

    def tt(self, out, in0, in1, op, eng="dve", rd=None, wr=None):
        reads, writes = self._rw([in0, in1], [out], rd, wr)
        self.S.op(eng, lambda e: e.tensor_tensor(out=out, in0=in0, in1=in1, op=op), reads, writes)

    def ts(self, out, in0, s1, op0, s2=None, op1=None, eng="dve", rd=None, wr=None):
        ins = [in0] + [s for s in (s1, s2) if s is not None and not isinstance(s, (int, float))]
        reads, writes = self._rw(ins, [out], rd, wr)
        if op1 is None:
            self.S.op(eng, lambda e: e.tensor_scalar(out=out, in0=in0, scalar1=s1, scalar2=None, op0=op0),
                      reads, writes)
        else:
            self.S.op(eng, lambda e: e.tensor_scalar(out=out, in0=in0, scalar1=s1, scalar2=s2, op0=op0,
                                                     op1=op1), reads, writes)

    def stt(self, out, in0, scalar, in1, op0, op1, rd=None, wr=None):
        ins = [in0, in1] + ([scalar] if not isinstance(scalar, (int, float)) else [])
        reads, writes = self._rw(ins, [out], rd, wr)
        self.S.op("dve", lambda e: e.scalar_tensor_tensor(out=out, in0=in0, scalar=scalar, in1=in1,
                                                          op0=op0, op1=op1), reads, writes)

    def cp(self, out, in_, eng="dve", rd=None, wr=None):
        reads, writes = self._rw([in_], [out], rd, wr)
        if eng == "act":
            self.S.op("act", lambda e: e.copy(out, in_), reads, writes)
        else:
            self.S.op(eng, lambda e: e.tensor_copy(out=out, in_=in_), reads, writes)

    def memset(self, ap, val, eng="dve", wr=None):
        reads, writes = self._rw([], [ap], None, wr)
        self.S.op(eng, lambda e: e.memset(ap, val), reads, writes)

    def recip(self, out, in_, rd=None, wr=None):
        reads, writes = self._rw([in_], [out], rd, wr)
        self.S.op("dve", lambda e: e.reciprocal(out=out, in_=in_), reads, writes)

    def load(self, out, in_, sbbuf=None, rd=(), wr=None, q="sp", nc_ok=False):
        S = self.S
        sbb = sbbuf if sbbuf is not None else S.tbuf(out)
        writes = [sbb] if wr is None else list(wr)
        if nc_ok:
            fn = lambda e: e.dma_start(out=out, in_=in_, allow_slow_non_contiguous=True)
        else:
            fn = lambda e: e.dma_start(out=out, in_=in_)
        S.dma(q, fn, sbb, reads=list(rd), writes=writes)

    def store(self, out, in_, sbbuf=None, rd=None, wr=(), q="sp", nc_ok=False):
        S = self.S
        sbb = sbbuf if sbbuf is not None else S.tbuf(in_)
        reads = [sbb] if rd is None else list(rd)
        if nc_ok:
            fn = lambda e: e.dma_start(out=out, in_=in_, allow_slow_non_contiguous=True)
        else:
            fn = lambda e: e.dma_start(out=out, in_=in_)
        S.dma(q, fn, sbb, reads=reads, writes=list(wr))

    def rstd_of(self, ssq, n, out):
        self.ts(out, ssq, 1.0 / n, ALU.mult, EPS, ALU.add)
        self.act(out, out, AF.Ln)
        self.act(out, out, AF.Exp, scale=-0.5)

    def build(self):
        nc = self.nc
        cfg = self.cfg
        SEQ, NT, NPG, NPOOL = self.SEQ, self.NT, self.NPG, self.NPOOL
        nS, nM = self.n_ssd, self.n_mla
        NTT = NT + 1
        ROWS = SEQ + 128
        es = self.es
        self.S = S = Sched(nc, es)

        x_in = self.din("x_in", [ROWS, D])
        cst = self.din("cst", [128, 128 * 6 + 16 + 1 + 4096])
        rope_tok = self.din("rope_tok", [ROWS, 64])
        rope_T = self.din("rope_T", [32, 2, ROWS])
        fnw = self.din("fnw", [1, D])
        y_out = self.dout("y_out", [ROWS, D])
        hb = [self.dint("hbA", [ROWS, D]), self.dint("hbB", [ROWS, D])]
        if nS:
            ssd_win = self.din("ssd_win", [nS * 2, D, 3088])
            ssd_wout = self.din("ssd_wout", [nS * 2, 1024, D])
            ssd_nrm = self.din("ssd_nrm", [nS, 128, 8])
            ssd_gnw = self.din("ssd_gnw", [nS * 2, 128, 8])
            ssd_cw = self.din("ssd_cw", [nS * 2, 128, 16, 5])
            ssd_hp = self.din("ssd_hp", [nS * 2, 3, 16])
            st_ssm = self.din("st_ssm", [nS, 16, 2048, 128])
            st_conv = self.din("st_conv", [nS, 48, 4096])
            o_pssm = self.dout("o_pssm", [nS, 2048, 128])
            o_pconv = self.dout("o_pconv", [nS, 3, 4096])
            o_sssm = self.dout("o_sssm", [nS, 16, 2048, 128])
            o_sconv = self.dout("o_sconv", [nS, 48, 4096])
        if nM:
            mla_win = self.din("mla_win", [nM, D, 1856])
            mla_nrm = self.din("mla_nrm", [nM, 128, 8])
            mla_qnw = self.din("mla_qnw", [nM, 128, 4])
            mla_kvw = self.din("mla_kvw", [nM, 1, 256])
            mla_wuq = self.din("mla_wuq", [nM, 512, 2048])
            mla_wuk = self.din("mla_wuk", [nM, 128, 8, 256])
            mla_wuv = self.din("mla_wuv", [nM, 256, 1024])
            mla_wout = self.din("mla_wout", [nM, 1024, D])
            pool_ckv = self.din("pool_all", [nM * NPOOL * 128, 288])
            ptab = self.din("ptab", [1, 16 * NPG], I32)
            o_pckv = self.dout("o_ckv", [nM, ROWS, 256])
            o_pkpe = self.dout("o_kpe", [nM, ROWS, 32])

        sb = self.sb
        NCS = 128 * 6 + 16 + 1
        cst_sb = sb("cst_sb", [128, NCS])
        self.load(cst_sb[:, :], cst[:, 0:NCS])
        o = 0
        ident_f = cst_sb[:, o:o + 128]; o += 128
        T_p = cst_sb[:, o:o + 128]; o += 128
        T_s = cst_sb[:, o:o + 128]; o += 128
        SEG_p = cst_sb[:, o:o + 128]; o += 128
        SEG_s = cst_sb[:, o:o + 128]; o += 128
        ones_f = cst_sb[:, o:o + 128]; o += 128
        rowmask = cst_sb[:, o:o + 16]; o += 16
        iota_p = cst_sb[:, o:o + 1]; o += 1
        ident_b = sb("ident_b", [128, 128], BF16)
        self.cp(ident_b[:, :], ident_f)
        ones_b = sb("ones_b", [128, 128], BF16)
        self.cp(ones_b[:, :], ones_f)
        Tb_p = sb("Tb_p", [128, 128], BF16)
        self.cp(Tb_p[:, :], T_p)
        Tb_s = sb("Tb_s", [128, 128], BF16)
        self.cp(Tb_s[:, :], T_s)
        negm_p = sb("negm_p", [128, 4, 128], BF16)
        negm_s = sb("negm_s", [128, 4, 128], BF16)
        for q4 in range(4):
            self.ts(negm_p[:, q4, :], T_p, -1.0, ALU.add, -NEG, ALU.mult)
            self.ts(negm_s[:, q4, :], T_s, -1.0, ALU.add, -NEG, ALU.mult)
        fnw_bc = sb("fnw_bc", [128, D])
        self.load(fnw_bc[:, :], fnw.ap().rearrange("a d -> (a d)").partition_broadcast(128))

        banks = [self.es.enter_context(nc.psum_tensor("bank%d" % i, [128, 512], F32)) for i in range(8)]
        bankb = [b.bitcast(BF16) for b in banks]
        for i in range(8):
            S.bufs[bankb[i][:, :].tensor.name] = S.tbuf(banks[i][:, :])
            S.tbuf(banks[i][:, :]).excl = True

        C = dict(ident_f=ident_f, ident_b=ident_b, ones_b=ones_b, ones_f=ones_f, T_p=T_p, T_s=T_s,
                 SEG_p=SEG_p, SEG_s=SEG_s, Tb_p=Tb_p, Tb_s=Tb_s, negm_p=negm_p, negm_s=negm_s,
                 rowmask=rowmask, iota_p=iota_p, fnw_bc=fnw_bc,
                 banks=banks, bankb=bankb)
        self.C = C

        hbufs = [[S.buf("hb%d_%d" % (k, t)) for t in range(NTT)] for k in range(3)]

        h_in = [sb("h_in", [128, D])] * 2
        h_acc = None
        h_new = [sb("h_new", [128, D])] * 2
        xn = sb("xn", [128, D], BF16)
        lnT = sb("lnT", [128, 8, 128], BF16)
        sq_junk = sb("sq_junk", [128, D], BF16)
        ssq = sb("ssq", [128, 1])
        rstd = sb("rstd", [128, 1])
        self.T = dict(h_in=h_in, h_acc=h_acc, h_new=h_new, xn=xn, lnT=lnT, sq_junk=sq_junk, ssq=ssq,
                      rstd=rstd)

        src = (x_in, None)
        li_s = li_m = 0
        nL = len(self.LAYERS)
        cur = None
        for li, kind in enumerate(self.LAYERS):
            last = li == nL - 1
            if kind == "ssd":
                self.ssd_layer(li, li_s, last, locals())
                li_s += 1
            else:
                self.mla_layer(li, li_m, last, locals())
                li_m += 1
        S.finish()
        S.emit()
        return nc

    def stream_src(self, li, sweep=0):
        raise NotImplementedError

    def norm_and_transpose(self, h_tile, fold_bank):
        T, C = self.T, self.C
        self.act(T["sq_junk"][:, :], h_tile, AF.Square, accum=T["ssq"][:, :])
        self.rstd_of(T["ssq"][:, :], D, T["rstd"][:, :])
        self.ts(T["xn"][:, :], h_tile, T["rstd"][:, 0:1], ALU.mult)
        pb = C["bankb"][fold_bank]
        for kc in range(8):
            self.tr(pb[:, kc * 128:(kc + 1) * 128], T["xn"][:, kc * 128:(kc + 1) * 128], C["ident_b"][:, :])
        self.cp(T["lnT"][:, :, :], pb[:, 0:1024].rearrange("p (k t) -> p k t", k=8), eng="act")
        return T["lnT"]

    def final_out(self, hn_ap, t, V):
        T, C = self.T, self.C
        y_out = V["y_out"]
        yt = self.yfin[t % 2]
        self.act(T["sq_junk"][:, :], hn_ap, AF.Square, accum=self.ssq2[:, :])
        self.rstd_of(self.ssq2[:, :], D, self.rstd2[:, :])
        self.stt(yt[:, :], hn_ap, self.rstd2[:, 0:1], C["fnw_bc"][:, :], ALU.mult, ALU.mult)
        self.store(y_out[t * 128:(t + 1) * 128, :], yt[:, :])

    def load_weight_bf(self, dst_bf, src_ap_fn, nk, ncols, scale_col_fn, stage, eng_cycle):
        CB = stage[0].shape[1]
        i = 0
        for kc in range(nk):
            for c0 in range(0, ncols, CB):
                w = min(CB, ncols - c0)
                st = stage[i % 2]
                self.load(st[:, 0:w], src_ap_fn(kc, c0, w))
                sc = scale_col_fn(kc) if scale_col_fn is not None else None
                eng = eng_cycle[i % len(eng_cycle)]
                if sc is None:
                    self.cp(dst_bf[:, kc, c0:c0 + w], st[:, 0:w], eng=eng)
                elif eng == "act":
                    self.act(dst_bf[:, kc, c0:c0 + w], st[:, 0:w], AF.Copy, scale=sc)
                else:
                    self.ts(dst_bf[:, kc, c0:c0 + w], st[:, 0:w], sc, ALU.mult, eng=eng)
                i += 1

    def ssd_layer(self, li, j, last, V):
        nc, S, C, T = self.nc, self.S, self.C, self.T
        sb = self.sb
        NT, NTT, SEQ = self.NT, self.NT + 1, self.SEQ
        banks, bankb = C["banks"], C["bankb"]
        x_in, hb, hbufs = V["x_in"], V["hb"], V["hbufs"]
        es_layer = ExitStack()
        old_es = self.es
        self.es = es_layer

        if li == 0:
            src_t, src_b = x_in, None
        else:
            src_t, src_b = hb[(li - 1) % 2], hbufs[(li - 1) % 2]
        mid_t, mid_b = hb[2 - 2] if False else None, None
        if "hbC" not in self.dram:
            self.dint("hbC", [SEQ + 128, D])
        mid_t, mid_b = self.dram["hbC"], hbufs[2]
        dst_t, dst_b = hb[li % 2], hbufs[li % 2]

        w_in = sb("s_win", [128, 8, 3088], BF16)
        w_out = sb("s_wout", [128, 8, D], BF16)
        stage = [sb("s_stage%d" % i, [128, 1024]) for i in range(2)]
        h_acc = [sb("s_hacc%d" % i, [128, D]) for i in range(2)]
        nrm = sb("s_nrm", [128, 8])
        gnw = sb("s_gnw", [128, 8])
        cw = sb("s_cw", [128, 16, 5])
        hp_bc = sb("s_hp", [128, 3, 16])
        A_bc = sb("s_A", [128, 16])
        raw = sb("s_raw", [128, 16, 131])
        raw_s = sb("s_raws", [128, 16, 16, 11])
        cacc4 = [sb("s_cacc%d" % i, [128, 4, 128]) for i in range(2)]
        cacc = [cacc4[0][:, 0, :], cacc4[1][:, 0, :]]
        xact4 = [sb("s_xact%d" % i, [128, 512]) for i in range(2)]
        x_tok = sb("s_xtok", [128, 1024])
        xdt = sb("s_xdt", [128, 1024], BF16)
        xdd = sb("s_xdd", [128, 1024], BF16)
        BT = sb("s_BT", [128, 4, 128], BF16)
        CT = sb("s_CT", [128, 4, 128], BF16)
        Btok = sb("s_Btok", [128, 4, 128], BF16)
        dtx = sb("s_dtx", [128, 16])
        dtt = sb("s_dtt", [128, 16])
        dt = sb("s_dt", [128, 16])
        a_t = sb("s_a", [128, 16])
        a_hi = sb("s_ahi", [128, 16], BF16)
        a_lo = sb("s_alo", [128, 16], BF16)
        nacs = sb("s_nacs", [128, 16])
        dec_in = sb("s_decin", [128, 16])
        dec_end = sb("s_decend", [128, 16])
        cdec = sb("s_cdec", [128, 16])
        rhs_hi = sb("s_rhshi", [128, 16, 128], BF16)
        rhs_lo = sb("s_rhslo", [128, 16, 128], BF16)
        LT = [sb("s_LT%d" % i, [128, 128]) for i in range(4)]
        GT = [sb("s_GT%d" % i, [128, 128], BF16) for i in range(4)]
        CBm_2 = [sb("s_CBm%d" % i, [128, 128]) for i in range(2)]
        yo_sb_2 = [sb("s_yo%d" % i, [128, 256]) for i in range(2)]
        y_g_2 = [sb("s_yg%d" % i, [128, 256]) for i in range(2)]
        sz_2 = [sb("s_sz%d" % i, [128, 256], BF16) for i in range(2)]
        sz_all = sb("s_szall", [128, 1024])
        yn_2 = [sb("s_yn%d" % i, [128, 256], BF16) for i in range(2)]
        ynT_2 = [sb("s_ynT%d" % i, [128, 2, 128], BF16) for i in range(2)]
        gss_2 = [sb("s_gss%d" % i, [128, 1]) for i in range(2)]
        grs_2 = [sb("s_grs%d" % i, [128, 1]) for i in range(2)]
        ST = sb("s_ST", [128, 1024])
        ST_bf = sb("s_STbf", [128, 1024], BF16)
        sold = [sb("s_sold%d" % i, [128, 128]) for i in range(8)]
        snew = [sb("s_snew%d" % i, [128, 128]) for i in range(8)]
        soT = [sb("s_soT%d" % i, [128, 256], BF16) for i in range(4)]
        CTm = [sb("s_CTm%d" % i, [128, 128], BF16) for i in range(4)]
        xdm = [sb("s_xdm%d" % i, [128, 256], BF16) for i in range(4)]
        a_bc = sb("s_abc", [128, 1024])
        cd_s = sb("s_cds", [128, 8, 16])
        cst48 = sb("s_cst48", [48, 2048])
        cout = cst48
        colmask_b = sb("s_colmask", [128, 16, 128], BF16)
        C["colmask_b"] = colmask_b
        for hh in range(2):
            self.load(stage[hh][:, :], V["cst"][:, 785 + hh * 1024:785 + (hh + 1) * 1024])
            self.cp(colmask_b[:, hh * 8:(hh + 1) * 8, :], stage[hh][:, :].rearrange("p (b t) -> p b t", b=8))
        if last:
            self.yfin = [sb("s_yfin", [128, D])] * 2
            self.ssq2 = sb("s_ssq2", [128, 1])
            self.rstd2 = sb("s_rstd2", [128, 1])
        h_in, h_new = T["h_in"], T["h_new"]

        ssd_win, ssd_wout = V["ssd_win"], V["ssd_wout"]
        for sw in range(2):
            ls = j * 2 + sw
            self.load(nrm[:, :], V["ssd_nrm"][j, :, :])
            self.load(gnw[:, :], V["ssd_gnw"][ls, :, :])
            self.load(cw[:, :, :], V["ssd_cw"][ls, :, :, :])
            self.load(hp_bc[:, :, :].rearrange("p a b -> p (a b)"),
                      V["ssd_hp"][ls, :, :].rearrange("a b -> (a b)").partition_broadcast(128))
            self.act(A_bc[:, :], hp_bc[:, 1, :], AF.Exp)
            self.ts(A_bc[:, :], A_bc[:, :], -1.0, ALU.mult)
            self.load_weight_bf(w_in, lambda kc, c0, w: ssd_win[ls, kc * 128:(kc + 1) * 128, c0:c0 + w],
                                8, 3088, lambda kc: nrm[:, kc:kc + 1], stage, ["dve", "act"])
            self.load_weight_bf(w_out, lambda kc, c0, w: ssd_wout[ls, kc * 128:(kc + 1) * 128, c0:c0 + w],
                                8, D, lambda kc: gnw[:, kc:kc + 1], stage, ["dve", "act"])
            self.memset(raw[:, :, 0:3], 0.0)
            self.memset(ST[:, :], 0.0)
            self.memset(ST_bf[:, :], 0.0)

            def load_tile(t):
                r0 = t * 128
                self.load(h_in[t % 2][:, :], src_t[r0:r0 + 128, :],
                          rd=([src_b[t]] if src_b is not None else []))
                if sw == 1:
                    self.load(h_acc[t % 2][:, :], mid_t[r0:r0 + 128, :], rd=[mid_b[t]])

            for t in range(NTT):
                samp = t == NT
                load_tile(t)
                hin = h_in[t % 2]
                hacc = hin if sw == 0 else h_acc[t % 2]
                lnT = self.norm_and_transpose(hin[:, :], 0)

                pdt = banks[2]
                for kc in range(8):
                    self.mm(pdt[:, 0:16], lnT[:, kc, :], w_in[:, kc, 3072:3088], start=kc == 0, stop=kc == 7)
                self.tt(dtx[:, :], pdt[:, 0:16], hp_bc[:, 0, :], ALU.add)
                self.act(dtt[:, :], dtx[:, :], AF.Abs)
                self.act(dtt[:, :], dtt[:, :], AF.Exp, scale=-1.0)
                self.act(dtt[:, :], dtt[:, :], AF.Ln, bias=1.0)
                self.stt(dt[:, :], dtx[:, :], 0.0, dtt[:, :], ALU.max, ALU.add)
                self.tt(a_t[:, :], dt[:, :], A_bc[:, :], ALU.mult)
                if samp:
                    stc = V["st_conv"]
                    self.load(cst48[:, 0:1024], stc[j, :, sw * 1024:(sw + 1) * 1024])
                    self.load(cst48[:, 1024:1536], stc[j, :, 2048 + sw * 512:2048 + (sw + 1) * 512])
                    self.load(cst48[:, 1536:2048], stc[j, :, 3072 + sw * 512:3072 + (sw + 1) * 512])
                    for cc in range(16):
                        pb = banks[1]
                        self.tr(pb[:, 0:48], cst48[:, cc * 128:(cc + 1) * 128], C["ident_f"][0:48, 0:48])
                        self.cp(raw_s[:, cc, :, 0:3], pb[:, 0:48].rearrange("p (b k) -> p b k", b=16), eng="act")
                for grp in range(4):
                    pb = banks[1 + grp % 2]
                    ca4 = cacc4[grp % 2]
                    for c4 in range(4):
                        cc = grp * 4 + c4
                        col0 = 1024 + cc * 128
                        for kc in range(8):
                            self.mm(pb[:, c4 * 128:(c4 + 1) * 128], w_in[:, kc, col0:col0 + 128], lnT[:, kc, :],
                                    start=kc == 0, stop=kc == 7)
                    if not samp:
                        self.cp(raw[:, grp * 4:(grp + 1) * 4, 3:131], pb[:, :].rearrange("p (c t) -> p c t", c=4),
                                eng="act")
                        for k in range(4):
                            for c4 in range(4):
                                cc = grp * 4 + c4
                                if k == 0:
                                    self.ts(ca4[:, c4, :], raw[:, cc, 0:128], cw[:, cc, 0:1], ALU.mult,
                                            cw[:, cc, 4:5], ALU.add)
                                else:
                                    self.stt(ca4[:, c4, :], raw[:, cc, k:k + 128], cw[:, cc, k:k + 1], ca4[:, c4, :],
                                             ALU.mult, ALU.add)
                    else:
                        for c4 in range(4):
                            cc = grp * 4 + c4
                            self.cp(raw_s[:, cc, :, 3:11],
                                    pb[:, c4 * 128:(c4 + 1) * 128].rearrange("p (b k) -> p b k", b=16), eng="act")
                        for k in range(4):
                            for c4 in range(4):
                                cc = grp * 4 + c4
                                ca3 = ca4[:, c4, :].rearrange("p (b k) -> p b k", b=16)
                                if k == 0:
                                    self.ts(ca3, raw_s[:, cc, :, 0:8], cw[:, cc, 0:1], ALU.mult, cw[:, cc, 4:5], ALU.add)
                                else:
                                    self.stt(ca3, raw_s[:, cc, :, k:k + 8], cw[:, cc, k:k + 1], ca3, ALU.mult, ALU.add)
                    ca_flat = ca4[:, :, :].rearrange("p c t -> p (c t)")
                    if grp < 2:
                        xa4 = xact4[grp % 2]
                        self.act(xa4[:, :], ca_flat, AF.Silu)
                        pt_ = banks[0]
                        for c4 in range(4):
                            self.tr(pt_[:, c4 * 128:(c4 + 1) * 128], xa4[:, c4 * 128:(c4 + 1) * 128], C["ident_f"])
                        self.cp(x_tok[:, grp * 512:(grp + 1) * 512], pt_[:, :], eng="act")
                    elif grp == 2:
                        self.act(BT[:, :, :].rearrange("p g t -> p (g t)"), ca_flat, AF.Silu)
                        ptb = bankb[0]
                        for g in range(4):
                            self.tr(ptb[:, g * 128:(g + 1) * 128], BT[:, g, :], C["ident_b"][:, :])
                        self.cp(Btok[:, :, :].rearrange("p g t -> p (g t)"), ptb[:, 0:512], eng="act")
                    else:
                        self.act(CT[:, :, :].rearrange("p g t -> p (g t)"), ca_flat, AF.Silu)
                for half in range(2):
                    pz = banks[1 + half]
                    for kc in range(8):
                        self.mm(pz[:, :], lnT[:, kc, :], w_in[:, kc, half * 512:(half + 1) * 512],
                                start=kc == 0, stop=kc == 7)
                    self.act(sz_all[:, half * 512:(half + 1) * 512], pz[:, :], AF.Silu)
                if samp or t == NT - 1:
                    ncol = 48 if samp else 3
                    for cc in range(16):
                        pb = banks[1]
                        if samp:
                            src_ap = raw_s[:, cc, :, 8:11]
                            stg = cacc[cc % 2]
                            self.cp(stg[:, 0:48].rearrange("p (b k) -> p b k", b=16), src_ap)
                            self.tr(pb[0:48, 0:128], stg[:, 0:48], C["ident_f"])
                        else:
                            self.tr(pb[0:3, 0:128], raw[:, cc, 128:131], C["ident_f"])
                        self.cp(cout[0:ncol, cc * 128:(cc + 1) * 128], pb[0:ncol, 0:128], eng="act")
                    oc = V["o_sconv"] if samp else V["o_pconv"]
                    self.store(oc[j, 0:ncol, sw * 1024:(sw + 1) * 1024], cout[0:ncol, 0:1024])
                    self.store(oc[j, 0:ncol, 2048 + sw * 512:2048 + (sw + 1) * 512], cout[0:ncol, 1024:1536])
                    self.store(oc[j, 0:ncol, 3072 + sw * 512:3072 + (sw + 1) * 512], cout[0:ncol, 1536:2048])
                if not samp:
                    self.cp(raw[:, :, 0:3], raw[:, :, 128:131], eng="pool")

                Tm = C["T_s"] if samp else C["T_p"]
                SEGm = C["SEG_s"] if samp else C["SEG_p"]
                self.mm(pdt[:, 16:32], Tm, a_t[:, :], True, True)
                self.mm(pdt[:, 32:48], SEGm, a_t[:, :], True, True)
                self.ts(nacs[:, :], pdt[:, 16:32], -1.0, ALU.mult)
                self.act(dec_in[:, :], pdt[:, 16:32], AF.Exp)
                self.act(cdec[:, :], pdt[:, 32:48], AF.Exp)
                self.tt(dec_end[:, :], pdt[:, 32:48], nacs[:, :], ALU.add)
                self.act(dec_end[:, :], dec_end[:, :], AF.Exp)
                self.cp(a_hi[:, :], a_t[:, :])
                self.tt(a_lo[:, :], a_t[:, :], a_hi[:, :], ALU.subtract)
                Tb = C["Tb_s"] if samp else C["Tb_p"]
                negm = C["negm_s"] if samp else C["negm_p"]
                self.tt(rhs_hi[:, :, :], Tb[:, :].unsqueeze(1).broadcast_to([128, 16, 128]),
                        a_hi[:, :].unsqueeze(2).broadcast_to([128, 16, 128]), ALU.mult)
                self.tt(rhs_lo[:, :, :], Tb[:, :].unsqueeze(1).broadcast_to([128, 16, 128]),
                        a_lo[:, :].unsqueeze(2).broadcast_to([128, 16, 128]), ALU.mult)

                x3 = x_tok[:, :].rearrange("p (h d) -> p h d", h=16)
                self.tt(xdt[:, :].rearrange("p (h d) -> p h d", h=16), x3,
                        dt[:, :].unsqueeze(2).broadcast_to([128, 16, 64]), ALU.mult)
                self.tt(xdd[:, :].rearrange("p (h d) -> p h d", h=16), xdt[:, :].rearrange("p (h d) -> p h d", h=16),
                        dec_end[:, :].unsqueeze(2).broadcast_to([128, 16, 64]), ALU.mult)
                if samp:
                    self.cp(a_bc[:, :].rearrange("p (h d) -> p h d", h=16),
                            a_t[:, :].unsqueeze(2).broadcast_to([128, 16, 64]))
                    pcd = banks[2]
                    for hp in range(8):
                        self.mm(pcd[:, 64 + hp * 16:64 + (hp + 1) * 16], a_bc[:, hp * 128:(hp + 1) * 128],
                                C["rowmask"], True, True)
                    self.act(cd_s[:, :, :].rearrange("p a b -> p (a b)"), pcd[:, 64:192], AF.Exp)

                pout = (banks[6], banks[7])
                def group_gen(g, samp=samp, Tm=Tm, negm=negm):
                    p_ = g % 2
                    hs = g * 4
                    pL = banks[3] if p_ == 0 else banks[0]
                    self.mm(pL[:, :], C["ones_b"][:, :], rhs_hi[:, hs:hs + 4, :], True, False)
                    yield
                    self.mm(pL[:, :], C["ones_b"][:, :], rhs_lo[:, hs:hs + 4, :], False, False)
                    yield
                    self.mm(pL[:, :], C["ident_b"][:, :], negm[:, :, :], False, True)
                    yield
                    pC = banks[4] if p_ == 0 else banks[1]
                    self.mm(pC[:, 0:128], BT[:, g, :], CT[:, g, :], True, True)
                    yield
                    self.tt(CBm_2[p_][:, :], pC[:, 0:128], Tm, ALU.mult)
                    yield
                    for hh in range(4):
                        h = hs + hh
                        lt = LT[p_ * 2 + h % 2]
                        gt = GT[p_ * 2 + h % 2]
                        self.act(lt[:, :], pL[:, hh * 128:(hh + 1) * 128], AF.Exp, bias=nacs[:, h:h + 1])
                        yield
                        self.tt(gt[:, :], lt[:, :], CBm_2[p_][:, :], ALU.mult)
                        yield
                        self.mm(pC[:, 128 + hh * 64:128 + (hh + 1) * 64], gt[:, :], xdt[:, h * 64:(h + 1) * 64],
                                True, True)
                        yield
                    pY = banks[5] if p_ == 0 else banks[2]
                    if not samp:
                        self.mm(pY[:, 0:256], CT[:, g, :], ST_bf[:, g * 256:(g + 1) * 256], True, True)
                        yield
                        self.mm(pY[:, 256:512], Btok[:, g, :], xdd[:, g * 256:(g + 1) * 256], True, True)
                        yield
                        self.cp(yo_sb_2[p_][:, :], pY[:, 0:256], eng="act")
                        yield
                        stg_ = ST[:, g * 256:(g + 1) * 256]
                        self.tt(stg_.rearrange("p (h d) -> p h d", h=4), stg_.rearrange("p (h d) -> p h d", h=4),
                                cdec[:, hs:hs + 4].unsqueeze(2).broadcast_to([128, 4, 64]), ALU.mult)
                        yield
                        self.tt(stg_, stg_, pY[:, 256:512], ALU.add)
                        yield
                        self.cp(ST_bf[:, g * 256:(g + 1) * 256], stg_, eng="pool")
                        yield
                    else:
                        sst, osst = V["st_ssm"], V["o_sssm"]
                        for b in range(16):
                            som = soT[p_ * 2 + b % 2]
                            for hp2 in range(2):
                                hp = g * 2 + hp2
                                so = sold[p_ * 4 + (b * 2 + hp2) % 4]
                                r0 = (sw * 8 + hp) * 128
                                self.load(so[:, :], sst[j, b, r0:r0 + 128, :])
                                yield
                                ptr = pL
                                self.tr(ptr[:, 128:256], so[:, :], C["ident_f"])
                                yield
                                self.cp(som[:, hp2 * 128:(hp2 + 1) * 128], ptr[:, 128:256], eng="act")
                                yield
                            ctm = CTm[p_ * 2 + b % 2]
                            self.tt(ctm[:, :], CT[:, g, :], C["colmask_b"][:, b, :], ALU.mult, eng="pool")
                            yield
                            self.mm(pY[:, 0:256], ctm[:, :], som[:, :], b == 0, b == 15)
                            yield
                            xm = xdm[p_ * 2 + b % 2]
                            self.ts(xm[:, :], xdd[:, g * 256:(g + 1) * 256], C["rowmask"][:, b:b + 1], ALU.mult)
                            yield
                            for hp2 in range(2):
                                hp = g * 2 + hp2
                                so = sold[p_ * 4 + (b * 2 + hp2) % 4]
                                sn = snew[p_ * 4 + (b * 2 + hp2) % 4]
                                r0 = (sw * 8 + hp) * 128
                                pS = pL
                                self.mm(pS[:, 256 + hp2 * 128:256 + (hp2 + 1) * 128], xm[:, hp2 * 128:(hp2 + 1) * 128],
                                        Btok[:, g, :], True, True)
                                yield
                                self.stt(sn[:, :], so[:, :], cd_s[:, hp, b:b + 1],
                                         pS[:, 256 + hp2 * 128:256 + (hp2 + 1) * 128], ALU.mult, ALU.add)
                                yield
                                self.store(osst[j, b, r0:r0 + 128, :], sn[:, :])
                                yield
                        self.cp(yo_sb_2[p_][:, :], pY[:, 0:256], eng="act")
                        yield
                    for hh in range(4):
                        h = hs + hh
                        ysl = y_g_2[p_][:, hh * 64:(hh + 1) * 64]
                        self.stt(ysl, yo_sb_2[p_][:, hh * 64:(hh + 1) * 64], dec_in[:, h:h + 1],
                                 pC[:, 128 + hh * 64:128 + (hh + 1) * 64], ALU.mult, ALU.add)
                        yield
                        self.stt(ysl, x_tok[:, h * 64:(h + 1) * 64], hp_bc[:, 2, h:h + 1], ysl, ALU.mult, ALU.add)
                        yield
                    self.tt(y_g_2[p_][:, :], y_g_2[p_][:, :], sz_all[:, g * 256:(g + 1) * 256], ALU.mult)
                    yield
                    self.act(sz_2[p_][:, :], y_g_2[p_][:, :], AF.Square, accum=gss_2[p_][:, :])
                    yield
                    self.rstd_of(gss_2[p_][:, :], 256, grs_2[p_][:, :])
                    yield
                    self.ts(yn_2[p_][:, :], y_g_2[p_][:, :], grs_2[p_][:, 0:1], ALU.mult)
                    yield
                    pt2 = bankb[3] if p_ == 0 else bankb[0]
                    for c2 in range(2):
                        self.tr(pt2[:, 512 + c2 * 128:512 + (c2 + 1) * 128], yn_2[p_][:, c2 * 128:(c2 + 1) * 128],
                                C["ident_b"][:, :])
                        yield
                    self.cp(ynT_2[p_][:, :, :], pt2[:, 512:768].rearrange("p (c t) -> p c t", c=2), eng="act")
                    yield
                    for c2 in range(2):
                        ec = g * 2 + c2
                        for half in range(2):
                            self.mm(pout[half][:, :], ynT_2[p_][:, c2, :], w_out[:, ec, half * 512:(half + 1) * 512],
                                    start=(ec == 0), stop=(ec == 7))
                            yield

                def _interleave(*gens):
                    gens = list(gens)
                    while gens:
                        for g_ in list(gens):
                            try:
                                next(g_)
                            except StopIteration:
                                gens.remove(g_)

                _interleave(group_gen(0), group_gen(1))
                _interleave(group_gen(2), group_gen(3))
                hn = h_new[t % 2]
                for half in range(2):
                    self.tt(hn[:, half * 512:(half + 1) * 512], hacc[:, half * 512:(half + 1) * 512],
                            pout[half][:, :], ALU.add)
                r0 = t * 128
                if sw == 0:
                    self.store(mid_t[r0:r0 + 128, :], hn[:, :], wr=[mid_b[t]])
                elif not last:
                    self.store(dst_t[r0:r0 + 128, :], hn[:, :], wr=[dst_b[t]])
                else:
                    self.final_out(hn[:, :], t, V)
            for hp in range(8):
                ptr = banks[0]
                self.tr(ptr[:, 256:384], ST[:, hp * 128:(hp + 1) * 128], C["ident_f"])
                sn = snew[hp % 4]
                self.cp(sn[:, :], ptr[:, 256:384], eng="act")
                r0 = (sw * 8 + hp) * 128
                self.store(V["o_pssm"][j, r0:r0 + 128, :], sn[:, :])
        self.es = old_es
        self._close_layer(es_layer)

    def _close_layer(self, es_layer):
        self.barrier()
        es_layer.close()

    def barrier(self):
        S = self.S
        tags = []
        for e in ("pe", "act", "dve", "pool"):
            if S.count[e] > 0:
                tags.append((e, S.sem[e], S.count[e]))
        for b in S.dma_bufs:
            if b.dlast is not None:
                tags.append(b.dlast)
        for e in S.ENG:
            waits = []
            for (sk, sem, v) in tags:
                if S.waited[e].get(sk, 0) < v:
                    S.waited[e][sk] = v
                    waits.append((sem, v))
            if waits:
                S.ops[e].append((waits, None, None, 0))

    def mla_layer(self, li, j, last, V):
        nc, S, C, T = self.nc, self.S, self.C, self.T
        sb = self.sb
        NT, NTT, SEQ, NPG = self.NT, self.NT + 1, self.SEQ, self.NPG
        banks, bankb = C["banks"], C["bankb"]
        x_in, hb, hbufs = V["x_in"], V["hb"], V["hbufs"]
        es_layer = ExitStack()
        old_es = self.es
        self.es = es_layer
        SCALE = float((64 + 32) ** -0.5)
        if li == 0:
            src_t, src_b = x_in, None
        else:
            src_t, src_b = hb[(li - 1) % 2], hbufs[(li - 1) % 2]
        dst_t, dst_b = hb[li % 2], hbufs[li % 2]

        w_in = sb("m_win", [128, 8, 1856], BF16)
        w_uq = sb("m_wuq", [128, 4, 2048], BF16)
        w_ukz = sb("m_wukz", [128, 16, 256], BF16)
        w_uv = sb("m_wuv", [128, 2, 1024], BF16)
        w_out = sb("m_wout", [128, 8, D], BF16)
        stage = [sb("m_stage%d" % i, [128, 512]) for i in range(2)]
        nrm = sb("m_nrm", [128, 8])
        qnw = sb("m_qnw", [128, 4])
        kvw_bc = sb("m_kvw", [128, 256])
        ckvT = sb("m_ckvT", [128, 2, SEQ], BF16)
        ckv_tok = sb("m_ckvtok", [128, NT, 256], BF16)
        kpeT = sb("m_kpeT", [32, SEQ], BF16)
        ckvT_s = sb("m_ckvTs", [128, 2, 128], BF16)
        ckv_s = sb("m_ckvs", [128, 256], BF16)
        kpeT_s = sb("m_kpeTs", [32, 128], BF16)
        self._ptc = 0
        PTq = [sb("m_PTq%d" % i, [128, 512], BF16) for i in range(3)]
        q_nopeT = sb("m_qnT", [128, 8, 128], BF16)
        q_latT = sb("m_qlT", [128, 2, 16, 128], BF16)
        q_peT = sb("m_qpT", [32, 16, 128], BF16)
        accT = sb("m_accT", [128, 2, 16, 128], BF16)
        cqn = sb("m_cqn", [128, 512], BF16)
        cqT = sb("m_cqT", [128, 4, 128], BF16)
        ckv_f = sb("m_ckvf", [128, 256])
        kpe_f = sb("m_kpef", [128, 32])
        kpe_t = sb("m_kpet", [128, 32])
        kpe_b = sb("m_kpeb", [128, 32], BF16)
        rtok = [sb("m_rtok%d" % i, [128, 64]) for i in range(2)]
        rT = [sb("m_rT%d" % i, [32, 2, 128]) for i in range(2)]
        qp1 = sb("m_qp1", [32, 512])
        sg = sb("m_sg", [128, D])
        og = sb("m_og", [128, D], BF16)
        oT = sb("m_oT", [128, 8, 128], BF16)
        l_tok = sb("m_ltok", [128, 16])
        rl = sb("m_rl", [128, 16])
        ss2 = sb("m_ss2", [128, 1])
        rs2 = sb("m_rs2", [128, 1])
        ss3 = sb("m_ss3", [128, 1])
        rs3 = sb("m_rs3", [128, 1])
        amask_b = sb("m_amask", [128, 16, 128], BF16)
        idx = sb("m_idx", [128, 16 * NPG], I32)
        pg_c = [sb("m_pgc%d" % i, [128, 288]) for i in range(4)]
        pg_b = [sb("m_pgb%d" % i, [128, 256], BF16) for i in range(4)]
        pgT = [sb("m_pgT%d" % i, [128, 2, 128], BF16) for i in range(4)]
        pkT = [sb("m_pkT%d" % i, [32, 128], BF16) for i in range(4)]
        PTs = [sb("m_PTs%d" % i, [128, 128], BF16) for i in range(4)]
        rlb = sb("m_rlb", [128, 128])
        if last:
            self.yfin = [sb("m_yfin", [128, D])] * 2
            self.ssq2 = sb("m_ssq2", [128, 1])
            self.rstd2 = sb("m_rstd2", [128, 1])
        h_in, h_new = T["h_in"], T["h_new"]

        self.load(nrm[:, :], V["mla_nrm"][j, :, :])
        self.load(qnw[:, :], V["mla_qnw"][j, :, :])
        self.load(kvw_bc[:, :], V["mla_kvw"][j, :, :].rearrange("a d -> (a d)").partition_broadcast(128))
        for hh in range(4):
            self.load(stage[hh % 2][:, :], V["cst"][:, 785 + 2048 + hh * 512:785 + 2048 + (hh + 1) * 512])
            self.cp(amask_b[:, hh * 4:(hh + 1) * 4, :], stage[hh % 2][:, :].rearrange("p (b t) -> p b t", b=4))
        mw, wuq, wuk, wuv, wo = V["mla_win"], V["mla_wuq"], V["mla_wuk"], V["mla_wuv"], V["mla_wout"]
        self.load_weight_bf(w_in, lambda kc, c0, w: mw[j, kc * 128:(kc + 1) * 128, c0:c0 + w],
                            8, 1856, lambda kc: nrm[:, kc:kc + 1], stage, ["dve", "act"])
        self.load_weight_bf(w_uq, lambda kc, c0, w: wuq[j, kc * 128:(kc + 1) * 128, c0:c0 + w],
                            4, 2048, lambda kc: qnw[:, kc:kc + 1], stage, ["dve", "act"])
        self.load_weight_bf(w_uv, lambda kc, c0, w: wuv[j, kc * 128:(kc + 1) * 128, c0:c0 + w],
                            2, 1024, None, stage, ["dve", "act"])
        self.load_weight_bf(w_out, lambda kc, c0, w: wo[j, kc * 128:(kc + 1) * 128, c0:c0 + w],
                            8, D, None, stage, ["dve", "act"])
        self.memset(w_ukz[:, :, :].rearrange("p h c -> p (h c)"), 0.0)
        wz4 = w_ukz[:, :, :].rearrange("p (a two) c -> p a two c", two=2)
        for ci, c0 in enumerate(range(0, 2048, 512)):
            st = stage[ci % 2]
            self.load(st[:, :], wuk[j, :, :, :].rearrange("p a c -> p (a c)")[:, c0:c0 + 512])
            self.cp(wz4[0:64, ci * 2:(ci + 1) * 2, 0, :], st[0:64, :].rearrange("p (a c) -> p a c", a=2))
            self.cp(wz4[64:128, ci * 2:(ci + 1) * 2, 1, :], st[64:128, :].rearrange("p (a c) -> p a c", a=2))
        NI = 16 * NPG
        import os as _os
        _dbg = _os.environ.get("K_DBG", "")
        for c0 in range(0, 0 if "noidx" in _dbg else NI, 512):
            w = min(512, NI - c0)
            st = stage[(c0 // 512) % 2]
            sti = st[:, 0:w].bitcast(I32)
            self.load(sti, V["ptab"][:, c0:c0 + w].rearrange("a n -> (a n)").partition_broadcast(128))
            self.cp(idx[:, c0:c0 + w].bitcast(F32), sti)
            self.ts(idx[:, c0:c0 + w].bitcast(F32), idx[:, c0:c0 + w].bitcast(F32), 128.0, ALU.mult,
                    C["iota_p"], ALU.add)
            if j > 0:
                self.ts(idx[:, c0:c0 + w].bitcast(F32), idx[:, c0:c0 + w].bitcast(F32),
                        float(j * self.NPOOL * 128), ALU.add)
            self.cp(idx[:, c0:c0 + w], idx[:, c0:c0 + w].bitcast(F32))
        pool_c = V["pool_ckv"]
        o_ckv, o_kpe = V["o_pckv"], V["o_pkpe"]
        rope_tok, rope_T = V["rope_tok"], V["rope_T"]

        def load_tile(t):
            r0 = t * 128
            self.load(h_in[t % 2][:, :], src_t[r0:r0 + 128, :], rd=([src_b[t]] if src_b is not None else []))
            self.load(rtok[t % 2][:, :], rope_tok[r0:r0 + 128, :])
            self.load(rT[t % 2][:, :, :], rope_T[:, :, r0:r0 + 128])

        _skip = _os.environ.get("K_SKIP", "")
        for t in range(NTT):
            samp = t == NT
            load_tile(t)
            hin = h_in[t % 2]
            r0 = t * 128
            lnT = self.norm_and_transpose(hin[:, :], 0)
            pq, pk, pg0, pg1 = banks[1], banks[2], banks[3], banks[4]
            for kc in range(8):
                self.mm(pq[:, :], lnT[:, kc, :], w_in[:, kc, 0:512], start=kc == 0, stop=kc == 7)
            for kc in range(8):
                self.mm(pk[:, 0:320], lnT[:, kc, :], w_in[:, kc, 512:832], start=kc == 0, stop=kc == 7)
            for kc in range(8):
                self.mm(pg0[:, :], lnT[:, kc, :], w_in[:, kc, 832:1344], start=kc == 0, stop=kc == 7)
            for kc in range(8):
                self.mm(pg1[:, :], lnT[:, kc, :], w_in[:, kc, 1344:1856], start=kc == 0, stop=kc == 7)
            if "A" not in _skip:
                self.act(T["sq_junk"][:, 0:512], pq[:, :], AF.Square, accum=ss2[:, :])
                self.rstd_of(ss2[:, :], 512, rs2[:, :])
                self.ts(cqn[:, :], pq[:, :], rs2[:, 0:1], ALU.mult)
                ptb = bankb[0]
                for kc in range(4):
                    self.tr(ptb[:, kc * 128:(kc + 1) * 128], cqn[:, kc * 128:(kc + 1) * 128], C["ident_b"][:, :])
                self.cp(cqT[:, :, :], ptb[:, 0:512].rearrange("p (k t) -> p k t", k=4), eng="act")
            if "B" not in _skip:
                self.act(T["sq_junk"][:, 0:256], pk[:, 0:256], AF.Square, accum=ss3[:, :])
                self.rstd_of(ss3[:, :], 256, rs3[:, :])
                self.stt(ckv_f[:, :], pk[:, 0:256], rs3[:, 0:1], kvw_bc[:, :], ALU.mult, ALU.mult)
                self.store(o_ckv[j, r0:r0 + 128, :], ckv_f[:, :])
                self.tt(kpe_f[:, :], pk[:, 256:288], rtok[t % 2][:, 0:32], ALU.mult)
                self.tt(kpe_t[:, :], pk[:, 288:320], rtok[t % 2][:, 32:64], ALU.mult)
                self.tt(kpe_f[:, :], kpe_f[:, :], kpe_t[:, :], ALU.add)
                self.store(o_kpe[j, r0:r0 + 128, :], kpe_f[:, :])
                self.cp(kpe_b[:, :], kpe_f[:, :])
                kv_dst = ckv_s[:, :] if samp else ckv_tok[:, t, :]
                self.cp(kv_dst, ckv_f[:, :], eng="pool")
                for cc in range(2):
                    self.tr(ptb[:, 512 + cc * 128:512 + (cc + 1) * 128], kv_dst[:, cc * 128:(cc + 1) * 128]
                            if False else (ckv_s[:, cc * 128:(cc + 1) * 128] if samp else ckv_tok[:, t, cc * 128:(cc + 1) * 128]),
                            C["ident_b"][:, :])
                kT_dst = ckvT_s[:, :, :] if samp else ckvT[:, :, r0:r0 + 128]
                self.cp(kT_dst, ptb[:, 512:768].rearrange("p (c t) -> p c t", c=2), eng="act")
                self.tr(ptb[0:32, 768:896], kpe_b[:, :], C["ident_b"][:, :])
                kp_dst = kpeT_s[:, :] if samp else kpeT[:, r0:r0 + 128]
                self.cp(kp_dst, ptb[0:32, 768:896], eng="act")
            if "C" not in _skip:
                self.act(sg[:, 0:512], pg0[:, :], AF.Silu)
                self.act(sg[:, 512:1024], pg1[:, :], AF.Silu)
            if "D" not in _skip:
                for hp in range(8):
                    pb = banks[1]
                    for kc in range(4):
                        self.mm(pb[:, 0:128], w_uq[:, kc, hp * 128:(hp + 1) * 128], cqT[:, kc, :],
                                start=kc == 0, stop=kc == 3)
                    self.cp(q_nopeT[:, hp, :], pb[:, 0:128], eng="act")
            if "E" not in _skip:
                for q4 in range(4):
                    pa, pbk = banks[2], banks[3]
                    for hh in range(4):
                        h = q4 * 4 + hh
                        for kc in range(4):
                            self.mm(pa[0:32, hh * 128:(hh + 1) * 128], w_uq[:, kc, 1024 + h * 32:1024 + (h + 1) * 32],
                                    cqT[:, kc, :], start=kc == 0, stop=kc == 3)
                        for kc in range(4):
                            self.mm(pbk[0:32, hh * 128:(hh + 1) * 128], w_uq[:, kc, 1536 + h * 32:1536 + (h + 1) * 32],
                                    cqT[:, kc, :], start=kc == 0, stop=kc == 3)
                    cosb = rT[t % 2][:, 0, :].unsqueeze(1).broadcast_to([32, 4, 128])
                    sinb = rT[t % 2][:, 1, :].unsqueeze(1).broadcast_to([32, 4, 128])
                    self.tt(qp1[:, :].rearrange("p (h t) -> p h t", h=4), pa[0:32, :].rearrange("p (h t) -> p h t", h=4),
                            cosb, ALU.mult)
                    self.tt(pbk[0:32, :].rearrange("p (h t) -> p h t", h=4), pbk[0:32, :].rearrange("p (h t) -> p h t", h=4),
                            sinb, ALU.mult)
                    self.tt(q_peT[:, q4 * 4:(q4 + 1) * 4, :].rearrange("p h t -> p (h t)"), qp1[:, :], pbk[0:32, :], ALU.add)
            if "F" not in _skip:
                for q4 in range(4):
                    for cc in range(2):
                        pb = banks[1 + (q4 * 2 + cc) % 2]
                        for hh in range(4):
                            h = q4 * 4 + hh
                            hp = h // 2
                            self.mm(pb[:, hh * 128:(hh + 1) * 128], w_ukz[:, h, cc * 128:(cc + 1) * 128],
                                    q_nopeT[:, hp, :], True, True)
                        self.cp(q_latT[:, cc, q4 * 4:(q4 + 1) * 4, :].rearrange("p h t -> p (h t)"), pb[:, :], eng="act")
            if not samp:
                nk = t + 1
                for q4 in range(4):
                    h0 = q4 * 4
                    pacc = (banks[5], banks[6]) if q4 % 2 == 0 else (banks[1], banks[2])
                    pl = banks[7]

                    def scores(jk, h0=h0):
                        ps = banks[3 + jk % 2]
                        k0 = jk * 128
                        self.mm(ps[:, :], ckvT[:, 0, k0:k0 + 128], q_latT[:, 0, h0:h0 + 4, :], True, False)
                        self.mm(ps[:, :], ckvT[:, 1, k0:k0 + 128], q_latT[:, 1, h0:h0 + 4, :], False, False)
                        self.mm(ps[:, :], kpeT[:, k0:k0 + 128], q_peT[:, h0:h0 + 4, :], False, True)

                    scores(0)
                    for jk in range(nk):
                        if jk + 1 < nk:
                            scores(jk + 1)
                        pt = PTq[self._ptc % 3]
                        self._ptc += 1
                        self.act(pt[:, :], banks[3 + jk % 2][:, :], AF.Exp, scale=SCALE)
                        if jk == t:
                            self.tt(pt[:, :].rearrange("p (h t) -> p h t", h=4), pt[:, :].rearrange("p (h t) -> p h t", h=4),
                                    C["Tb_p"][:, :].unsqueeze(1).broadcast_to([128, 4, 128]), ALU.mult)
                        for cc in range(2):
                            self.mm(pacc[cc][:, :], ckv_tok[:, jk, cc * 128:(cc + 1) * 128], pt[:, :],
                                    jk == 0, jk == nk - 1)
                        for hh in range(4):
                            self.mm(pl[:, h0 + hh:h0 + hh + 1], pt[:, hh * 128:(hh + 1) * 128], C["ones_b"][:, 0:1],
                                    jk == 0 and hh == 0, jk == nk - 1 and hh == 3)
                    for cc in range(2):
                        self.cp(accT[:, cc, h0:h0 + 4, :].rearrange("p h t -> p (h t)"), pacc[cc][:, :], eng="act")
                self.cp(l_tok[:, :], banks[7][:, 0:16])
            else:
                self.memset(l_tok[:, :], 1.0)
                steps = [(b, pgi) for b in range(16) for pgi in range(NPG)]
                NST = len(steps)

                def gather(i):
                    b_, pg_ = steps[i]
                    col = b_ * NPG + pg_
                    dst = pg_c[i % 4]
                    self.S.dma("pool", (lambda e, o_=dst[:, :], i_=pool_c[:, :], x_=idx[:, col:col + 1]:
                                        e.indirect_dma_start(out=o_, out_offset=None, in_=i_,
                                                             in_offset=bass.IndirectOffsetOnAxis(ap=x_, axis=0))),
                               S.tbuf(dst[:, :]), reads=[S.tbuf(idx[:, :])], writes=[S.tbuf(dst[:, :])])

                for i in range(min(3, NST)):
                    gather(i)
                pacc = (banks[5], banks[6])
                pl = banks[7]
                si = 0
                for b in range(16):
                    tsl = slice(b * 8, b * 8 + 8)
                    for pgi in range(NPG + 1):
                        new = pgi == NPG
                        if not new:
                            if si + 3 < NST:
                                gather(si + 3)
                            s4 = si % 4
                            pgf = pg_c[s4]
                            self.cp(pg_b[s4][:, :], pgf[:, 0:256])
                            ptf = banks[1 + si % 2]
                            for cc in range(2):
                                self.tr(ptf[:, cc * 128:(cc + 1) * 128], pgf[:, cc * 128:(cc + 1) * 128], C["ident_f"])
                            self.tr(ptf[0:32, 256:384], pgf[:, 256:288], C["ident_f"])
                            self.cp(pgT[s4][:, :, :], ptf[:, 0:256].rearrange("p (c t) -> p c t", c=2), eng="act")
                            self.cp(pkT[s4][:, :], ptf[0:32, 256:384], eng="act")
                            kT0, kT1, kP, kV = pgT[s4][:, 0, :], pgT[s4][:, 1, :], pkT[s4][:, :], pg_b[s4]
                            si += 1
                        else:
                            kT0, kT1, kP, kV = ckvT_s[:, 0, :], ckvT_s[:, 1, :], kpeT_s[:, :], ckv_s
                        ps = banks[3 + self._ptc % 2]
                        self.mm(ps[:, 0:128], kT0, q_latT[:, 0, :, tsl], True, False)
                        self.mm(ps[:, 0:128], kT1, q_latT[:, 1, :, tsl], False, False)
                        self.mm(ps[:, 0:128], kP, q_peT[:, :, tsl], False, True)
                        pt = PTs[self._ptc % 4]
                        self._ptc += 1
                        self.act(pt[:, :], ps[:, 0:128], AF.Exp, scale=SCALE)
                        if new:
                            self.tt(pt[:, :], pt[:, :], amask_b[:, b, :], ALU.mult)
                        for cc in range(2):
                            self.mm(pacc[cc][:, 0:128], kV[:, cc * 128:(cc + 1) * 128], pt[:, :], pgi == 0, new)
                        self.mm(pl[:, 128:256], C["ones_b"][:, :], pt[:, :], pgi == 0, new)
                    self.recip(rlb[:, :], pl[:, 128:256])
                    for cc in range(2):
                        self.tt(accT[:, cc, :, tsl], pacc[cc][:, 0:128].rearrange("p (h k) -> p h k", h=16),
                                rlb[:, :].rearrange("p (h k) -> p h k", h=16), ALU.mult)
            po = (banks[1], banks[2])
            for h in range(16):
                for cc in range(2):
                    self.mm(po[h // 8][:, (h % 8) * 64:(h % 8 + 1) * 64], accT[:, cc, h, :],
                            w_uv[:, cc, h * 64:(h + 1) * 64], cc == 0, cc == 1)
            self.recip(rl[:, :], l_tok[:, :])
            for half in range(2):
                self.tt(sg[:, half * 512:(half + 1) * 512], po[half][:, :], sg[:, half * 512:(half + 1) * 512], ALU.mult)
                self.tt(og[:, half * 512:(half + 1) * 512].rearrange("p (h v) -> p h v", h=8),
                        sg[:, half * 512:(half + 1) * 512].rearrange("p (h v) -> p h v", h=8),
                        rl[:, half * 8:(half + 1) * 8].unsqueeze(2).broadcast_to([128, 8, 64]), ALU.mult)
            for kc in range(8):
                self.tr(ptb[:, kc * 128:(kc + 1) * 128], og[:, kc * 128:(kc + 1) * 128], C["ident_b"][:, :])
            self.cp(oT[:, :, :], ptb[:, 0:1024].rearrange("p (k t) -> p k t", k=8), eng="act")
            pout = (banks[5], banks[6])
            for half in range(2):
                for kc in range(8):
                    self.mm(pout[half][:, :], oT[:, kc, :], w_out[:, kc, half * 512:(half + 1) * 512],
                            start=kc == 0, stop=kc == 7)
            hn = h_new[t % 2]
            for half in range(2):
                self.tt(hn[:, half * 512:(half + 1) * 512], hin[:, half * 512:(half + 1) * 512],
                        pout[half][:, :], ALU.add)
            if not last:
                self.store(dst_t[r0:r0 + 128, :], hn[:, :], wr=[dst_b[t]])
            else:
                self.final_out(hn[:, :], t, V)
        self.es = old_es
        self._close_layer(es_layer)


def make_consts(SEQ):
    P = 128
    idx = np.arange(P)
    ident = np.eye(P, dtype=np.float32)
    T_p = (idx[:, None] <= idx[None, :]).astype(np.float32)
    same = (idx[:, None] // 8 == idx[None, :] // 8)
    T_s = (T_p.astype(bool) & same).astype(np.float32)
    SEG_p = np.ones((P, P), np.float32)
    SEG_s = same.astype(np.float32)
    ones = np.ones((P, P), np.float32)
    rowmask = (idx[:, None] // 8 == np.arange(16)[None, :]).astype(np.float32)
    colmask = np.broadcast_to((np.arange(16)[:, None] == (idx[None, :] // 8))[None], (P, 16, P))
    colmask = colmask.astype(np.float32).reshape(P, 2048)
    kb, kt = idx // 8, idx % 8
    tok = np.arange(128) % 8
    am = (kb[:, None, None] == np.arange(16)[None, :, None]) & (kt[:, None, None] <= tok[None, None, :])
    am = am.astype(np.float32).reshape(P, 2048)
    iota = idx.astype(np.float32)[:, None]
    cst = np.concatenate([ident, T_p, T_s, SEG_p, SEG_s, ones, rowmask, iota, colmask, am], axis=1)
    return np.ascontiguousarray(cst.astype(np.float32))


def rope_tables(SEQ, past_len):
    pos = np.concatenate([np.arange(SEQ, dtype=np.float64),
                          np.tile(past_len + np.arange(8, dtype=np.float64), 16)])
    inv = 1.0 / (10000.0 ** (np.arange(0, 32, 2, dtype=np.float64) / 32))
    ang = pos[:, None] * inv[None, :]
    cos, sin = np.cos(ang), np.sin(ang)
    tok = np.concatenate([cos, cos, -sin, sin], axis=1).astype(np.float32)
    rt = np.stack([np.concatenate([cos, cos], 1).T, np.concatenate([-sin, sin], 1).T], axis=1)
    return np.ascontiguousarray(tok), np.ascontiguousarray(rt.astype(np.float32))


_PROG_CACHE = {}


def _get_prog(cfg):
    key = (cfg["SEQ"], cfg["NPG"], cfg["NPOOL"], tuple(cfg["LAYERS"]))
    if key not in _PROG_CACHE:
        p = Prog(cfg)
        p.build()
        _PROG_CACHE[key] = p
    return _PROG_CACHE[key]


def run_cfg(cfg, inputs, n_cores, past_len):
    f = lambda a: np.ascontiguousarray(np.asarray(a))
    SEQ, NPG, NPOOL = cfg["SEQ"], cfg["NPG"], cfg["NPOOL"]
    LAYERS = cfg["LAYERS"]
    prog = _get_prog(cfg)
    nS, nM = prog.n_ssd, prog.n_mla
    ROWS = SEQ + 128
    x_prompt, x_sample = f(inputs["x_prompt"]), f(inputs["x_sample"])
    cst = make_consts(SEQ)
    rope_tok, rope_T = rope_tables(SEQ, past_len)
    shared = {"cst": cst, "rope_tok": rope_tok, "rope_T": rope_T,
              "fnw": f(inputs["final_norm_w"]).reshape(1, D)}
    norm_w = f(inputs["norm_w"])
    ssd_idx = [i for i, k in enumerate(LAYERS) if k == "ssd"]
    mla_idx = [i for i, k in enumerate(LAYERS) if k == "mla"]
    if nS:
        w_in = f(inputs["ssd_w_in"]); w_out = f(inputs["ssd_w_out"])
        cwv = f(inputs["ssd_conv_w"]); cbv = f(inputs["ssd_conv_b"])
        dtb = f(inputs["ssd_dt_bias"]); alog = f(inputs["ssd_a_log"]); dsk = f(inputs["ssd_d"])
        gnw = f(inputs["ssd_norm_w"])
        win_l, wout_l, gnw_l, cw_l, hp_l = [], [], [], [], []
        for j in range(nS):
            for sw in range(2):
                z = w_in[j][:, sw * 1024:(sw + 1) * 1024]
                x = w_in[j][:, 2048 + sw * 1024:2048 + (sw + 1) * 1024]
                Bc = w_in[j][:, 4096 + sw * 512:4096 + (sw + 1) * 512]
                Cc = w_in[j][:, 5120 + sw * 512:5120 + (sw + 1) * 512]
                dtc = w_in[j][:, 6144 + sw * 16:6144 + (sw + 1) * 16]
                win_l.append(np.concatenate([z, x, Bc, Cc, dtc], axis=1))
                wout_l.append(w_out[j][sw * 1024:(sw + 1) * 1024, :])
                gnw_l.append(gnw[j][sw * 1024:(sw + 1) * 1024].reshape(8, 128).T)
                ch = np.concatenate([np.arange(sw * 1024, (sw + 1) * 1024),
                                     2048 + np.arange(sw * 512, (sw + 1) * 512),
                                     3072 + np.arange(sw * 512, (sw + 1) * 512)])
                cwb = np.concatenate([cwv[j][:, ch], cbv[j][None, ch]], axis=0)
                cw_l.append(cwb.reshape(5, 16, 128).transpose(2, 1, 0))
                hp_l.append(np.stack([dtb[j][sw * 16:(sw + 1) * 16], alog[j][sw * 16:(sw + 1) * 16],
                                      dsk[j][sw * 16:(sw + 1) * 16]]))
        shared["ssd_win"] = f(np.stack(win_l)); shared["ssd_wout"] = f(np.stack(wout_l))
        shared["ssd_gnw"] = f(np.stack(gnw_l)); shared["ssd_cw"] = f(np.stack(cw_l))
        shared["ssd_hp"] = f(np.stack(hp_l))
        shared["ssd_nrm"] = f(np.stack([norm_w[i].reshape(8, 128).T for i in ssd_idx]))
    if nM:
        mw = f(inputs["mla_w_in"])
        perm = np.concatenate([np.arange(16, 32), np.arange(0, 16)])
        win_l = []
        for jm in range(nM):
            kpe = mw[jm][:, 768:800]
            win_l.append(np.concatenate([mw[jm][:, 0:768], kpe, kpe[:, perm], mw[jm][:, 800:1824]], axis=1))
        shared["mla_win"] = f(np.stack(win_l))
        shared["mla_nrm"] = f(np.stack([norm_w[i].reshape(8, 128).T for i in mla_idx]))
        shared["mla_qnw"] = f(np.stack([f(inputs["mla_q_norm_w"])[jm].reshape(4, 128).T for jm in range(nM)]))
        shared["mla_kvw"] = f(f(inputs["mla_kv_norm_w"])[:nM].reshape(nM, 1, 256))
        wuq = f(inputs["mla_w_uq"])
        nope = wuq[:nM, :, :, 0:64].reshape(nM, 512, 1024)
        pe = wuq[:nM, :, :, 64:96]
        shared["mla_wuq"] = f(np.concatenate([nope, pe.reshape(nM, 512, 512),
                                              pe[..., perm].reshape(nM, 512, 512)], axis=2))
        wuk = f(inputs["mla_w_uk"])[:nM]
        t_ = wuk.transpose(0, 2, 3, 1).reshape(nM, 8, 2, 64, 256)
        shared["mla_wuk"] = f(t_.transpose(0, 2, 3, 1, 4).reshape(nM, 128, 8, 256))
        shared["mla_wuv"] = f(f(inputs["mla_w_uv"])[:nM].reshape(nM, 256, 1024))
        shared["mla_wout"] = f(inputs["mla_w_out"])[:nM]
        shared["pool_all"] = np.concatenate([f(inputs["cache_ckv"])[:nM].reshape(nM * NPOOL * 128, 256),
                                             f(inputs["cache_kpe"])[:nM].reshape(nM * NPOOL * 128, 32)], axis=1)
    in_maps = []
    for c in range(n_cores):
        k = c // 2
        m = dict(shared)
        m["x_in"] = f(np.concatenate([x_prompt[k], x_sample[16 * c:16 * c + 16].reshape(128, D)], axis=0))
        if nS:
            m["st_ssm"] = f(inputs["state_ssm"])[:nS, 16 * c:16 * c + 16].reshape(nS, 16, 2048, 128)
            m["st_conv"] = f(inputs["state_conv"])[:nS, 16 * c:16 * c + 16].reshape(nS, 48, 4096)
        if nM:
            m["ptab"] = f(inputs["page_table"])[16 * c:16 * c + 16].reshape(1, 16 * NPG).astype(np.int32)
        in_maps.append(m)
    import os as _os
    if _os.environ.get("K_TRACE"):
        res = run_bass_kernel_spmd(prog.nc, in_maps, core_ids=list(range(n_cores)), trace=True)
        print("EXEC_TIME_NS", res.exec_time_ns, {e: len(v) for e, v in prog.S.ops.items()})
    else:
        res = run_bass_kernel_spmd(prog.nc, in_maps, core_ids=list(range(n_cores)))
    R = res.results
    nseq = n_cores // 2
    out = {}
    out["y_prompt"] = np.stack([R[2 * k]["y_out"][:SEQ] for k in range(nseq)])
    out["y_sample"] = np.concatenate([R[c]["y_out"][SEQ:].reshape(16, 8, D) for c in range(n_cores)])
    if nS:
        out["p_ssm"] = np.stack([R[2 * k]["o_pssm"].reshape(nS, 32, 64, 128) for k in range(nseq)], axis=1)
        out["p_conv"] = np.stack([R[2 * k]["o_pconv"] for k in range(nseq)], axis=1)
        out["s_ssm"] = np.concatenate([R[c]["o_sssm"].reshape(nS, 16, 32, 64, 128) for c in range(n_cores)], axis=1)
        out["s_conv"] = np.concatenate([R[c]["o_sconv"].reshape(nS, 16, 3, 4096) for c in range(n_cores)], axis=1)
    if nM:
        out["p_ckv"] = np.stack([R[2 * k]["o_ckv"][:, :SEQ] for k in range(nseq)], axis=1)
        out["p_kpe"] = np.stack([R[2 * k]["o_kpe"][:, :SEQ] for k in range(nseq)], axis=1)
        out["s_ckv"] = np.concatenate([R[c]["o_ckv"][:, SEQ:].reshape(nM, 16, 8, 256) for c in range(n_cores)], axis=1)
        out["s_kpe"] = np.concatenate([R[c]["o_kpe"][:, SEQ:].reshape(nM, 16, 8, 32) for c in range(n_cores)], axis=1)
    return out


def kernel(**inputs):
    out = run_cfg(FULL_CFG, inputs, 8, 8192)
    return (out["y_prompt"], out["y_sample"], out["p_ssm"], out["p_conv"], out["p_ckv"], out["p_kpe"],
            out["s_ssm"], out["s_conv"], out["s_ckv"], out["s_kpe"])
```

```python
import math
from contextlib import ExitStack

import numpy as np
import concourse.bass as bass
import concourse.mybir as mybir
from concourse.bass_utils import run_bass_kernel_spmd

F32 = mybir.dt.float32
BF16 = mybir.dt.bfloat16
I32 = mybir.dt.int32
AF = mybir.ActivationFunctionType
ALU = mybir.AluOpType

D = 1024
EPS = 1e-6
NEG = -1.0e5

FULL_CFG = dict(SEQ=4096, NPG=64, NPOOL=10240, LAYERS=("ssd", "mla", "ssd", "mla"))


class Buf:
    __slots__ = ("name", "w", "r", "dsem", "dcount", "dlast", "excl")

    def __init__(self, name):
        self.name = name
        self.w = None
        self.r = []
        self.dsem = None
        self.dcount = 0
        self.dlast = None
        self.excl = False


class Sched:
    ENG = ("pe", "act", "dve", "pool", "sp")

    def __init__(self, nc, es):
        self.nc = nc
        self.es = es
        self.ops = {e: [] for e in self.ENG}
        self.sem = {}
        for e in ("pe", "act", "dve", "pool"):
            self.sem[e] = es.enter_context(nc.semaphore("sem_" + e))
        self.count = {e: 0 for e in self.ENG}
        self.waited = {e: {} for e in self.ENG}
        self.bufs = {}
        self.dma_bufs = []
        self.nsem = 4

    def buf(self, name):
        b = Buf(name)
        return b

    def tbuf(self, ap):
        n = ap.tensor.name
        if n not in self.bufs:
            self.bufs[n] = Buf(n)
        return self.bufs[n]

    def _deps(self, eng, reads, writes):
        deps = []
        for b in reads:
            if b.w is not None:
                deps.append(b.w)
        for b in writes:
            if b.w is not None:
                deps.append(b.w)
            deps.extend(b.r)
        waits = []
        wd = self.waited[eng]
        best = {}
        for (sk, sem, v) in deps:
            if sk == eng and eng == "pe":
                continue
            if wd.get(sk, 0) >= v:
                continue
            if sk not in best or best[sk][1] < v:
                best[sk] = (sem, v)
        for sk, (sem, v) in best.items():
            wd[sk] = v
            waits.append((sem, v))
        return waits

    def op(self, eng, fn, reads=(), writes=()):
        writes = list(dict.fromkeys(list(writes) + [b for b in reads if b.excl]))
        reads = list(dict.fromkeys(b for b in reads if not b.excl))
        waits = self._deps(eng, reads, writes)
        self.count[eng] += 1
        tag = (eng, self.sem[eng], self.count[eng])
        self.ops[eng].append((waits, fn, self.sem[eng], 1))
        for b in reads:
            b.r.append(tag)
        for b in writes:
            b.w = tag
            b.r = []
        return tag

    def dma(self, queue, fn, sb, reads=(), writes=()):
        if sb.dsem is None:
            sb.dsem = self.es.enter_context(self.nc.semaphore("d_" + sb.name))
            self.dma_bufs.append(sb)
            self.nsem += 1
        reads = list(dict.fromkeys(reads))
        writes = list(dict.fromkeys(writes))
        waits = self._deps(queue, reads, writes)
        if sb.dlast is not None:
            sk, sem, v = sb.dlast
            if self.waited[queue].get(sk, 0) < v:
                self.waited[queue][sk] = v
                waits.append((sem, v))
        sb.dcount += 1
        tag = ("d_" + sb.name, sb.dsem, 16 * sb.dcount)
        sb.dlast = tag
        self.ops[queue].append((waits, fn, sb.dsem, 16))
        for b in reads:
            b.r.append(tag)
        for b in writes:
            b.w = tag
            b.r = []
        return tag

    def finish(self):
        waits = []
        for b in self.dma_bufs:
            waits.append((b.dsem, 16 * b.dcount))
        self.ops["sp"].append((waits, None, None, 0))

    def emit(self):
        nc = self.nc
        ops = self.ops

        def run(e, lst):
            for (waits, fn, sem, amt) in lst:
                for (s, v) in waits:
                    e.wait_ge(s, v)
                if fn is not None:
                    inst = fn(e)
                    inst.then_inc(sem, amt)

        with nc.Block() as block:
            @block.sync
            def _(e):
                run(e, ops["sp"])

            @block.scalar
            def _(e):
                run(e, ops["act"])

            @block.vector
            def _(e):
                run(e, ops["dve"])

            @block.gpsimd
            def _(e):
                run(e, ops["pool"])

            @block.tensor
            def _(e):
                run(e, ops["pe"])


class Prog:
    def __init__(self, cfg):
        self.cfg = cfg
        self.SEQ = cfg["SEQ"]
        self.NT = self.SEQ // 128
        self.NPG = cfg["NPG"]
        self.NPOOL = cfg["NPOOL"]
        self.LAYERS = cfg["LAYERS"]
        self.n_ssd = sum(1 for l in self.LAYERS if l == "ssd")
        self.n_mla = sum(1 for l in self.LAYERS if l == "mla")
        self.nc = bass.Bass("TRN2", target_bir_lowering=False)
        self.es = ExitStack()
        self.S = None
        self.dram = {}

    def din(self, name, shape, dt=F32):
        t = self.nc.dram_tensor(name, list(shape), dt, kind="ExternalInput")
        self.dram[name] = t
        return t

    def dout(self, name, shape, dt=F32):
        t = self.nc.dram_tensor(name, list(shape), dt, kind="ExternalOutput")
        self.dram[name] = t
        return t

    def dint(self, name, shape, dt=F32):
        t = self.nc.dram_tensor(name, list(shape), dt, kind="Internal")
        self.dram[name] = t
        return t

    def sb(self, name, shape, dt=F32):
        self._uid = getattr(self, "_uid", 0) + 1
        return self.es.enter_context(self.nc.sbuf_tensor("%s_%d" % (name, self._uid), list(shape), dt))

    def _rw(self, ins, outs, rd, wr):
        S = self.S
        reads = list(rd) if rd is not None else []
        writes = list(wr) if wr is not None else []
        if rd is None:
            reads = [S.tbuf(a) for a in ins if a is not None and not isinstance(a, (int, float))]
        if wr is None:
            writes = [S.tbuf(a) for a in outs if a is not None]
        return reads, writes

    def mm(self, out, lhsT, rhs, start=True, stop=True, rd=None, wr=None, xrd=()):
        reads, writes = self._rw([lhsT, rhs], [out], rd, wr)
        reads += list(xrd)
        self.S.op("pe", lambda e: e.matmul(out, lhsT, rhs, start=start, stop=stop,
                                           skip_group_check=True), reads, writes)

    def tr(self, out, in_, ident, rd=None, wr=None):
        reads, writes = self._rw([in_, ident], [out], rd, wr)
        self.S.op("pe", lambda e: e.transpose(out, in_, ident), reads, writes)

    def act(self, out, in_, func, bias=0.0, scale=1.0, accum=None, eng="act", rd=None, wr=None):
        ins = [in_]
        if not isinstance(bias, (int, float)):
            ins.append(bias)
        if not isinstance(scale, (int, float)):
            ins.append(scale)
        outs = [out] + ([accum] if accum is not None else [])
        reads, writes = self._rw(ins, outs, rd, wr)
        if accum is None:
            self.S.op("act", lambda e: e.activation(out, in_, func, bias=bias, scale=scale), reads, writes)
        else:
            self.S.op("act", lambda e: e.activation(out, in_, func, bias=bias, scale=scale,
                                                    accum_out=accum), reads, writes)

    def tt(self, out, in0, in1, op, eng="dve", rd=None, wr=None):
        reads, writes = self._rw([in0, in1], [out], rd, wr)
        self.S.op(eng, lambda e: e.tensor_tensor(out=out, in0=in0, in1=in1, op=op), reads, writes)

    def ts(self, out, in0, s1, op0, s2=None, op1=None, eng="dve", rd=None, wr=None):
        ins = [in0] + [s for s in (s1, s2) if s is not None and not isinstance(s, (int, float))]
        reads, writes = self._rw(ins, [out], rd, wr)
        if op1 is None:
            self.S.op(eng, lambda e: e.tensor_scalar(out=out, in0=in0, scalar1=s1, scalar2=None, op0=op0),
                      reads, writes)
        else:
            self.S.op(eng, lambda e: e.tensor_scalar(out=out, in0=in0, scalar1=s1, scalar2=s2, op0=op0,
                                                     op1=op1), reads, writes)

    def stt(self, out, in0, scalar, in1, op0, op1, rd=None, wr=None):
        ins = [in0, in1] + ([scalar] if not isinstance(scalar, (int, float)) else [])
        reads, writes = self._rw(ins, [out], rd, wr)
        self.S.op("dve", lambda e: e.scalar_tensor_tensor(out=out, in0=in0, scalar=scalar, in1=in1,
                                                          op0=op0, op1=op1), reads, writes)

    def cp(self, out, in_, eng="dve", rd=None, wr=None):
        reads, writes = self._rw([in_], [out], rd, wr)
        if eng == "act":
            self.S.op("act", lambda e: e.copy(out, in_), reads, writes)
        else:
            self.S.op(eng, lambda e: e.tensor_copy(out=out, in_=in_), reads, writes)

    def memset(self, ap, val, eng="dve", wr=None):
        reads, writes = self._rw([], [ap], None, wr)
        self.S.op(eng, lambda e: e.memset(ap, val), reads, writes)

    def recip(self, out, in_, rd=None, wr=None):
        reads, writes = self._rw([in_], [out], rd, wr)
        self.S.op("dve", lambda e: e.reciprocal(out=out, in_=in_), reads, writes)

    def load(self, out, in_, sbbuf=None, rd=(), wr=None, q="sp", nc_ok=False):
        S = self.S
        sbb = sbbuf if sbbuf is not None else S.tbuf(out)
        writes = [sbb] if wr is None else list(wr)
        if nc_ok:
            fn = lambda e: e.dma_start(out=out, in_=in_, allow_slow_non_contiguous=True)
        else:
            fn = lambda e: e.dma_start(out=out, in_=in_)
        S.dma(q, fn, sbb, reads=list(rd), writes=writes)

    def store(self, out, in_, sbbuf=None, rd=None, wr=(), q="sp", nc_ok=False):
        S = self.S
        sbb = sbbuf if sbbuf is not None else S.tbuf(in_)
        reads = [sbb] if rd is None else list(rd)
        if nc_ok:
            fn = lambda e: e.dma_start(out=out, in_=in_, allow_slow_non_contiguous=True)
        else:
            fn = lambda e: e.dma_start(out=out, in_=in_)
        S.dma(q, fn, sbb, reads=reads, writes=list(wr))

    def rstd_of(self, ssq, n, out):
        self.ts(out, ssq, 1.0 / n, ALU.mult, EPS, ALU.add)
        self.act(out, out, AF.Ln)
        self.act(out, out, AF.Exp, scale=-0.5)

    def build(self):
        nc = self.nc
        cfg = self.cfg
        SEQ, NT, NPG, NPOOL = self.SEQ, self.NT, self.NPG, self.NPOOL
        nS, nM = self.n_ssd, self.n_mla
        NTT = NT + 1
        ROWS = SEQ + 128
        es = self.es
        self.S = S = Sched(nc, es)

        x_in = self.din("x_in", [ROWS, D])
        cst = self.din("cst", [128, 128 * 6 + 16 + 1 + 4096])
        rope_tok = self.din("rope_tok", [ROWS, 64])
        rope_T = self.din("rope_T", [32, 2, ROWS])
        fnw = self.din("fnw", [1, D])
        y_out = self.dout("y_out", [ROWS, D])
        hb = [self.dint("hbA", [ROWS, D]), self.dint("hbB", [ROWS, D])]
        if nS:
            ssd_win = self.din("ssd_win", [nS * 2, D, 3088])
            ssd_wout = self.din("ssd_wout", [nS * 2, 1024, D])
            ssd_nrm = self.din("ssd_nrm", [nS, 128, 8])
            ssd_gnw = self.din("ssd_gnw", [nS * 2, 128, 8])
            ssd_cw = self.din("ssd_cw", [nS * 2, 128, 16, 5])
            ssd_hp = self.din("ssd_hp", [nS * 2, 3, 16])
            st_ssm = self.din("st_ssm", [nS, 16, 2048, 128])
            st_conv = self.din("st_conv", [nS, 48, 4096])
            o_pssm = self.dout("o_pssm", [nS, 2048, 128])
            o_pconv = self.dout("o_pconv", [nS, 3, 4096])
            o_sssm = self.dout("o_sssm", [nS, 16, 2048, 128])
            o_sconv = self.dout("o_sconv", [nS, 48, 4096])
        if nM:
            mla_win = self.din("mla_win", [nM, D, 1856])
            mla_nrm = self.din("mla_nrm", [nM, 128, 8])
            mla_qnw = self.din("mla_qnw", [nM, 128, 4])
            mla_kvw = self.din("mla_kvw", [nM, 1, 256])
            mla_wuq = self.din("mla_wuq", [nM, 512, 2048])
            mla_wuk = self.din("mla_wuk", [nM, 128, 8, 256])
            mla_wuv = self.din("mla_wuv", [nM, 256, 1024])
            mla_wout = self.din("mla_wout", [nM, 1024, D])
            pool_ckv = self.din("pool_all", [nM * NPOOL * 128, 288])
            ptab = self.din("ptab", [1, 16 * NPG], I32)
            o_pckv = self.dout("o_ckv", [nM, ROWS, 256])
            o_pkpe = self.dout("o_kpe", [nM, ROWS, 32])

        sb = self.sb
        NCS = 128 * 6 + 16 + 1
        cst_sb = sb("cst_sb", [128, NCS])
        self.load(cst_sb[:, :], cst[:, 0:NCS])
        o = 0
        ident_f = cst_sb[:, o:o + 128]; o += 128
        T_p = cst_sb[:, o:o + 128]; o += 128
        T_s = cst_sb[:, o:o + 128]; o += 128
        SEG_p = cst_sb[:, o:o + 128]; o += 128
        SEG_s = cst_sb[:, o:o + 128]; o += 128
        ones_f = cst_sb[:, o:o + 128]; o += 128
        rowmask = cst_sb[:, o:o + 16]; o += 16
        iota_p = cst_sb[:, o:o + 1]; o += 1
        ident_b = sb("ident_b", [128, 128], BF16)
        self.cp(ident_b[:, :], ident_f)
        ones_b = sb("ones_b", [128, 128], BF16)
        self.cp(ones_b[:, :], ones_f)
        Tb_p = sb("Tb_p", [128, 128], BF16)
        self.cp(Tb_p[:, :], T_p)
        Tb_s = sb("Tb_s", [128, 128], BF16)
        self.cp(Tb_s[:, :], T_s)
        negm_p = sb("negm_p", [128, 4, 128], BF16)
        negm_s = sb("negm_s", [128, 4, 128], BF16)
        for q4 in range(4):
            self.ts(negm_p[:, q4, :], T_p, -1.0, ALU.add, -NEG, ALU.mult)
            self.ts(negm_s[:, q4, :], T_s, -1.0, ALU.add, -NEG, ALU.mult)
        fnw_bc = sb("fnw_bc", [128, D])
        self.load(fnw_bc[:, :], fnw.ap().rearrange("a d -> (a d)").partition_broadcast(128))

        banks = [self.es.enter_context(nc.psum_tensor("bank%d" % i, [128, 512], F32)) for i in range(8)]
        bankb = [b.bitcast(BF16) for b in banks]
        for i in range(8):
            S.bufs[bankb[i][:, :].tensor.name] = S.tbuf(banks[i][:, :])
            S.tbuf(banks[i][:, :]).excl = True

        C = dict(ident_f=ident_f, ident_b=ident_b, ones_b=ones_b, ones_f=ones_f, T_p=T_p, T_s=T_s,
                 SEG_p=SEG_p, SEG_s=SEG_s, Tb_p=Tb_p, Tb_s=Tb_s, negm_p=negm_p, negm_s=negm_s,
                 rowmask=rowmask, iota_p=iota_p, fnw_bc=fnw_bc,
                 banks=banks, bankb=bankb)
        self.C = C

        hbufs = [[S.buf("hb%d_%d" % (k, t)) for t in range(NTT)] for k in range(3)]

        h_in = [sb("h_in", [128, D])] * 2
        h_acc = None
        h_new = [sb("h_new", [128, D])] * 2
        xn = sb("xn", [128, D], BF16)
        lnT = sb("lnT", [128, 8, 128], BF16)
        sq_junk = sb("sq_junk", [128, D], BF16)
        ssq = sb("ssq", [128, 1])
        rstd = sb("rstd", [128, 1])
        self.T = dict(h_in=h_in, h_acc=h_acc, h_new=h_new, xn=xn, lnT=lnT, sq_junk=sq_junk, ssq=ssq,
                      rstd=rstd)

        src = (x_in, None)
        li_s = li_m = 0
        nL = len(self.LAYERS)
        cur = None
        for li, kind in enumerate(self.LAYERS):
            last = li == nL - 1
            if kind == "ssd":
                self.ssd_layer(li, li_s, last, locals())
                li_s += 1
            else:
                self.mla_layer(li, li_m, last, locals())
                li_m += 1
        S.finish()
        S.emit()
        return nc

    def stream_src(self, li, sweep=0):
        raise NotImplementedError

    def norm_and_transpose(self, h_tile, fold_bank):
        T, C = self.T, self.C
        self.act(T["sq_junk"][:, :], h_tile, AF.Square, accum=T["ssq"][:, :])
        self.rstd_of(T["ssq"][:, :], D, T["rstd"][:, :])
        self.ts(T["xn"][:, :], h_tile, T["rstd"][:, 0:1], ALU.mult)
        pb = C["bankb"][fold_bank]
        for kc in range(8):
            self.tr(pb[:, kc * 128:(kc + 1) * 128], T["xn"][:, kc * 128:(kc + 1) * 128], C["ident_b"][:, :])
        self.cp(T["lnT"][:, :, :], pb[:, 0:1024].rearrange("p (k t) -> p k t", k=8), eng="act")
        return T["lnT"]

    def final_out(self, hn_ap, t, V):
        T, C = self.T, self.C
        y_out = V["y_out"]
        yt = self.yfin[t % 2]
        self.act(T["sq_junk"][:, :], hn_ap, AF.Square, accum=self.ssq2[:, :])
        self.rstd_of(self.ssq2[:, :], D, self.rstd2[:, :])
        self.stt(yt[:, :], hn_ap, self.rstd2[:, 0:1], C["fnw_bc"][:, :], ALU.mult, ALU.mult)
        self.store(y_out[t * 128:(t + 1) * 128, :], yt[:, :])

    def load_weight_bf(self, dst_bf, src_ap_fn, nk, ncols, scale_col_fn, stage, eng_cycle):
        CB = stage[0].shape[1]
        i = 0
        for kc in range(nk):
            for c0 in range(0, ncols, CB):
                w = min(CB, ncols - c0)
                st = stage[i % 2]
                self.load(st[:, 0:w], src_ap_fn(kc, c0, w))
                sc = scale_col_fn(kc) if scale_col_fn is not None else None
                eng = eng_cycle[i % len(eng_cycle)]
                if sc is None:
                    self.cp(dst_bf[:, kc, c0:c0 + w], st[:, 0:w], eng=eng)
                elif eng == "act":
                    self.act(dst_bf[:, kc, c0:c0 + w], st[:, 0:w], AF.Copy, scale=sc)
                else:
                    self.ts(dst_bf[:, kc, c0:c0 + w], st[:, 0:w], sc, ALU.mult, eng=eng)
                i += 1

    def ssd_layer(self, li, j, last, V):
        nc, S, C, T = self.nc, self.S, self.C, self.T
        sb = self.sb
        NT, NTT, SEQ = self.NT, self.NT + 1, self.SEQ
        banks, bankb = C["banks"], C["bankb"]
        x_in, hb, hbufs = V["x_in"], V["hb"], V["hbufs"]
        es_layer = ExitStack()
        old_es = self.es
        self.es = es_layer

        if li == 0:
            src_t, src_b = x_in, None
        else:
            src_t, src_b = hb[(li - 1) % 2], hbufs[(li - 1) % 2]
        mid_t, mid_b = hb[2 - 2] if False else None, None
        if "hbC" not in self.dram:
            self.dint("hbC", [SEQ + 128, D])
        mid_t, mid_b = self.dram["hbC"], hbufs[2]
        dst_t, dst_b = hb[li % 2], hbufs[li % 2]

        w_in = sb("s_win", [128, 8, 3088], BF16)
        w_out = sb("s_wout", [128, 8, D], BF16)
        stage = [sb("s_stage%d" % i, [128, 1024]) for i in range(2)]
        h_acc = [sb("s_hacc%d" % i, [128, D]) for i in range(2)]
        nrm = sb("s_nrm", [128, 8])
        gnw = sb("s_gnw", [128, 8])
        cw = sb("s_cw", [128, 16, 5])
        hp_bc = sb("s_hp", [128, 3, 16])
        A_bc = sb("s_A", [128, 16])
        raw = sb("s_raw", [128, 16, 131])
        raw_s = sb("s_raws", [128, 16, 16, 11])
        cacc4 = [sb("s_cacc%d" % i, [128, 4, 128]) for i in range(2)]
        cacc = [cacc4[0][:, 0, :], cacc4[1][:, 0, :]]
        xact4 = [sb("s_xact%d" % i, [128, 512]) for i in range(2)]
        x_tok = sb("s_xtok", [128, 1024])
        xdt = sb("s_xdt", [128, 1024], BF16)
        xdd = sb("s_xdd", [128, 1024], BF16)
        BT = sb("s_BT", [128, 4, 128], BF16)
        CT = sb("s_CT", [128, 4, 128], BF16)
        Btok = sb("s_Btok", [128, 4, 128], BF16)
        dtx = sb("s_dtx", [128, 16])
        dtt = sb("s_dtt", [128, 16])
        dt = sb("s_dt", [128, 16])
        a_t = sb("s_a", [128, 16])
        a_hi = sb("s_ahi", [128, 16], BF16)
        a_lo = sb("s_alo", [128, 16], BF16)
        nacs = sb("s_nacs", [128, 16])
        dec_in = sb("s_decin", [128, 16])
        dec_end = sb("s_decend", [128, 16])
        cdec = sb("s_cdec", [128, 16])
        rhs_hi = sb("s_rhshi", [128, 16, 128], BF16)
        rhs_lo = sb("s_rhslo", [128, 16, 128], BF16)
        LT = [sb("s_LT%d" % i, [128, 128]) for i in range(4)]
        GT = [sb("s_GT%d" % i, [128, 128], BF16) for i in range(4)]
        CBm_2 = [sb("s_CBm%d" % i, [128, 128]) for i in range(2)]
        yo_sb_2 = [sb("s_yo%d" % i, [128, 256]) for i in range(2)]
        y_g_2 = [sb("s_yg%d" % i, [128, 256]) for i in range(2)]
        sz_2 = [sb("s_sz%d" % i, [128, 256], BF16) for i in range(2)]
        sz_all = sb("s_szall", [128, 1024])
        yn_2 = [sb("s_yn%d" % i, [128, 256], BF16) for i in range(2)]
        ynT_2 = [sb("s_ynT%d" % i, [128, 2, 128], BF16) for i in range(2)]
        gss_2 = [sb("s_gss%d" % i, [128, 1]) for i in range(2)]
        grs_2 = [sb("s_grs%d" % i, [128, 1]) for i in range(2)]
        ST = sb("s_ST", [128, 1024])
        ST_bf = sb("s_STbf", [128, 1024], BF16)
        sold = [sb("s_sold%d" % i, [128, 128]) for i in range(8)]
        snew = [sb("s_snew%d" % i, [128, 128]) for i in range(8)]
        soT = [sb("s_soT%d" % i, [128, 256], BF16) for i in range(4)]
        CTm = [sb("s_CTm%d" % i, [128, 128], BF16) for i in range(4)]
        xdm = [sb("s_xdm%d" % i, [128, 256], BF16) for i in range(4)]
        a_bc = sb("s_abc", [128, 1024])
        cd_s = sb("s_cds", [128, 8, 16])
        cst48 = sb("s_cst48", [48, 2048])
        cout = cst48
        colmask_b = sb("s_colmask", [128, 16, 128], BF16)
        C["colmask_b"] = colmask_b
        for hh in range(2):
            self.load(stage[hh][:, :], V["cst"][:, 785 + hh * 1024:785 + (hh + 1) * 1024])
            self.cp(colmask_b[:, hh * 8:(hh + 1) * 8, :], stage[hh][:, :].rearrange("p (b t) -> p b t", b=8))
        if last:
            self.yfin = [sb("s_yfin", [128, D])] * 2
            self.ssq2 = sb("s_ssq2", [128, 1])
            self.rstd2 = sb("s_rstd2", [128, 1])
        h_in, h_new = T["h_in"], T["h_new"]

        ssd_win, ssd_wout = V["ssd_win"], V["ssd_wout"]
        for sw in range(2):
            ls = j * 2 + sw
            self.load(nrm[:, :], V["ssd_nrm"][j, :, :])
            self.load(gnw[:, :], V["ssd_gnw"][ls, :, :])
            self.load(cw[:, :, :], V["ssd_cw"][ls, :, :, :])
            self.load(hp_bc[:, :, :].rearrange("p a b -> p (a b)"),
                      V["ssd_hp"][ls, :, :].rearrange("a b -> (a b)").partition_broadcast(128))
            self.act(A_bc[:, :], hp_bc[:, 1, :], AF.Exp)
            self.ts(A_bc[:, :], A_bc[:, :], -1.0, ALU.mult)
            self.load_weight_bf(w_in, lambda kc, c0, w: ssd_win[ls, kc * 128:(kc + 1) * 128, c0:c0 + w],
                                8, 3088, lambda kc: nrm[:, kc:kc + 1], stage, ["dve", "act"])
            self.load_weight_bf(w_out, lambda kc, c0, w: ssd_wout[ls, kc * 128:(kc + 1) * 128, c0:c0 + w],
                                8, D, lambda kc: gnw[:, kc:kc + 1], stage, ["dve", "act"])
            self.memset(raw[:, :, 0:3], 0.0)
            self.memset(ST[:, :], 0.0)
            self.memset(ST_bf[:, :], 0.0)

            def load_tile(t):
                r0 = t * 128
                self.load(h_in[t % 2][:, :], src_t[r0:r0 + 128, :],
                          rd=([src_b[t]] if src_b is not None else []))
                if sw == 1:
                    self.load(h_acc[t % 2][:, :], mid_t[r0:r0 + 128, :], rd=[mid_b[t]])

            for t in range(NTT):
                samp = t == NT
                load_tile(t)
                hin = h_in[t % 2]
                hacc = hin if sw == 0 else h_acc[t % 2]
                lnT = self.norm_and_transpose(hin[:, :], 0)

                pdt = banks[2]
                for kc in range(8):
                    self.mm(pdt[:, 0:16], lnT[:, kc, :], w_in[:, kc, 3072:3088], start=kc == 0, stop=kc == 7)
                self.tt(dtx[:, :], pdt[:, 0:16], hp_bc[:, 0, :], ALU.add)
                self.act(dtt[:, :], dtx[:, :], AF.Abs)
                self.act(dtt[:, :], dtt[:, :], AF.Exp, scale=-1.0)
                self.act(dtt[:, :], dtt[:, :], AF.Ln, bias=1.0)
                self.stt(dt[:, :], dtx[:, :], 0.0, dtt[:, :], ALU.max, ALU.add)
                self.tt(a_t[:, :], dt[:, :], A_bc[:, :], ALU.mult)
                if samp:
                    stc = V["st_conv"]
                    self.load(cst48[:, 0:1024], stc[j, :, sw * 1024:(sw + 1) * 1024])
                    self.load(cst48[:, 1024:1536], stc[j, :, 2048 + sw * 512:2048 + (sw + 1) * 512])
                    self.load(cst48[:, 1536:2048], stc[j, :, 3072 + sw * 512:3072 + (sw + 1) * 512])
                    for cc in range(16):
                        pb = banks[1]
                        self.tr(pb[:, 0:48], cst48[:, cc * 128:(cc + 1) * 128], C["ident_f"][0:48, 0:48])
                        self.cp(raw_s[:, cc, :, 0:3], pb[:, 0:48].rearrange("p (b k) -> p b k", b=16), eng="act")
                for grp in range(4):
                    pb = banks[1 + grp % 2]
                    ca4 = cacc4[grp % 2]
                    for c4 in range(4):
                        cc = grp * 4 + c4
                        col0 = 1024 + cc * 128
                        for kc in range(8):
                            self.mm(pb[:, c4 * 128:(c4 + 1) * 128], w_in[:, kc, col0:col0 + 128], lnT[:, kc, :],
                                    start=kc == 0, stop=kc == 7)
                    if not samp:
                        self.cp(raw[:, grp * 4:(grp + 1) * 4, 3:131], pb[:, :].rearrange("p (c t) -> p c t", c=4),
                                eng="act")
                        for k in range(4):
                            for c4 in range(4):
                                cc = grp * 4 + c4
                                if k == 0:
                                    self.ts(ca4[:, c4, :], raw[:, cc, 0:128], cw[:, cc, 0:1], ALU.mult,
                                            cw[:, cc, 4:5], ALU.add)
                                else:
                                    self.stt(ca4[:, c4, :], raw[:, cc, k:k + 128], cw[:, cc, k:k + 1], ca4[:, c4, :],
                                             ALU.mult, ALU.add)
                    else:
                        for c4 in range(4):
                            cc = grp * 4 + c4
                            self.cp(raw_s[:, cc, :, 3:11],
                                    pb[:, c4 * 128:(c4 + 1) * 128].rearrange("p (b k) -> p b k", b=16), eng="act")
                        for k in range(4):
                            for c4 in range(4):
                                cc = grp * 4 + c4
                                ca3 = ca4[:, c4, :].rearrange("p (b k) -> p b k", b=16)
                                if k == 0:
                                    self.ts(ca3, raw_s[:, cc, :, 0:8], cw[:, cc, 0:1], ALU.mult, cw[:, cc, 4:5], ALU.add)
                                else:
                                    self.stt(ca3, raw_s[:, cc, :, k:k + 8], cw[:, cc, k:k + 1], ca3, ALU.mult, ALU.add)
                    ca_flat = ca4[:, :, :].rearrange("p c t -> p (c t)")
                    if grp < 2:
                        xa4 = xact4[grp % 2]
                        self.act(xa4[:, :], ca_flat, AF.Silu)
                        pt_ = banks[0]
                        for c4 in range(4):
                            self.tr(pt_[:, c4 * 128:(c4 + 1) * 128], xa4[:, c4 * 128:(c4 + 1) * 128], C["ident_f"])
                        self.cp(x_tok[:, grp * 512:(grp + 1) * 512], pt_[:, :], eng="act")
                    elif grp == 2:
                        self.act(BT[:, :, :].rearrange("p g t -> p (g t)"), ca_flat, AF.Silu)
                        ptb = bankb[0]
                        for g in range(4):
                            self.tr(ptb[:, g * 128:(g + 1) * 128], BT[:, g, :], C["ident_b"][:, :])
                        self.cp(Btok[:, :, :].rearrange("p g t -> p (g t)"), ptb[:, 0:512], eng="act")
                    else:
                        self.act(CT[:, :, :].rearrange("p g t -> p (g t)"), ca_flat, AF.Silu)
                for half in range(2):
                    pz = banks[1 + half]
                    for kc in range(8):
                        self.mm(pz[:, :], lnT[:, kc, :], w_in[:, kc, half * 512:(half + 1) * 512],
                                start=kc == 0, stop=kc == 7)
                    self.act(sz_all[:, half * 512:(half + 1) * 512], pz[:, :], AF.Silu)
                if samp or t == NT - 1:
                    ncol = 48 if samp else 3
                    for cc in range(16):
                        pb = banks[1]
                        if samp:
                            src_ap = raw_s[:, cc, :, 8:11]
                            stg = cacc[cc % 2]
                            self.cp(stg[:, 0:48].rearrange("p (b k) -> p b k", b=16), src_ap)
                            self.tr(pb[0:48, 0:128], stg[:, 0:48], C["ident_f"])
                        else:
                            self.tr(pb[0:3, 0:128], raw[:, cc, 128:131], C["ident_f"])
                        self.cp(cout[0:ncol, cc * 128:(cc + 1) * 128], pb[0:ncol, 0:128], eng="act")
                    oc = V["o_sconv"] if samp else V["o_pconv"]
                    self.store(oc[j, 0:ncol, sw * 1024:(sw + 1) * 1024], cout[0:ncol, 0:1024])
                    self.store(oc[j, 0:ncol, 2048 + sw * 512:2048 + (sw + 1) * 512], cout[0:ncol, 1024:1536])
                    self.store(oc[j, 0:ncol, 3072 + sw * 512:3072 + (sw + 1) * 512], cout[0:ncol, 1536:2048])
                if not samp:
                    self.cp(raw[:, :, 0:3], raw[:, :, 128:131], eng="pool")

                Tm = C["T_s"] if samp else C["T_p"]
                SEGm = C["SEG_s"] if samp else C["SEG_p"]
                self.mm(pdt[:, 16:32], Tm, a_t[:, :], True, True)
                self.mm(pdt[:, 32:48], SEGm, a_t[:, :], True, True)
                self.ts(nacs[:, :], pdt[:, 16:32], -1.0, ALU.mult)
                self.act(dec_in[:, :], pdt[:, 16:32], AF.Exp)
                self.act(cdec[:, :], pdt[:, 32:48], AF.Exp)
                self.tt(dec_end[:, :], pdt[:, 32:48], nacs[:, :], ALU.add)
                self.act(dec_end[:, :], dec_end[:, :], AF.Exp)
                self.cp(a_hi[:, :], a_t[:, :])
                self.tt(a_lo[:, :], a_t[:, :], a_hi[:, :], ALU.subtract)
                Tb = C["Tb_s"] if samp else C["Tb_p"]
                negm = C["negm_s"] if samp else C["negm_p"]
                self.tt(rhs_hi[:, :, :], Tb[:, :].unsqueeze(1).broadcast_to([128, 16, 128]),
                        a_hi[:, :].unsqueeze(2).broadcast_to([128, 16, 128]), ALU.mult)
                self.tt(rhs_lo[:, :, :], Tb[:, :].unsqueeze(1).broadcast_to([128, 16, 128]),
                        a_lo[:, :].unsqueeze(2).broadcast_to([128, 16, 128]), ALU.mult)

                x3 = x_tok[:, :].rearrange("p (h d) -> p h d", h=16)
                self.tt(xdt[:, :].rearrange("p (h d) -> p h d", h=16), x3,
                        dt[:, :].unsqueeze(2).broadcast_to([128, 16, 64]), ALU.mult)
                self.tt(xdd[:, :].rearrange("p (h d) -> p h d", h=16), xdt[:, :].rearrange("p (h d) -> p h d", h=16),
                        dec_end[:, :].unsqueeze(2).broadcast_to([128, 16, 64]), ALU.mult)
                if samp:
                    self.cp(a_bc[:, :].rearrange("p (h d) -> p h d", h=16),
                            a_t[:, :].unsqueeze(2).broadcast_to([128, 16, 64]))
                    pcd = banks[2]
                    for hp in range(8):
                        self.mm(pcd[:, 64 + hp * 16:64 + (hp + 1) * 16], a_bc[:, hp * 128:(hp + 1) * 128],
                                C["rowmask"], True, True)
                    self.act(cd_s[:, :, :].rearrange("p a b -> p (a b)"), pcd[:, 64:192], AF.Exp)

                pout = (banks[6], banks[7])
                def group_gen(g, samp=samp, Tm=Tm, negm=negm):
                    p_ = g % 2
                    hs = g * 4
                    pL = banks[3] if p_ == 0 else banks[0]
                    self.mm(pL[:, :], C["ones_b"][:, :], rhs_hi[:, hs:hs + 4, :], True, False)
                    yield
                    self.mm(pL[:, :], C["ones_b"][:, :], rhs_lo[:, hs:hs + 4, :], False, False)
                    yield
                    self.mm(pL[:, :], C["ident_b"][:, :], negm[:, :, :], False, True)
                    yield
                    pC = banks[4] if p_ == 0 else banks[1]
                    self.mm(pC[:, 0:128], BT[:, g, :], CT[:, g, :], True, True)
                    yield
                    self.tt(CBm_2[p_][:, :], pC[:, 0:128], Tm, ALU.mult)
                    yield
                    for hh in range(4):
                        h = hs + hh
                        lt = LT[p_ * 2 + h % 2]
                        gt = GT[p_ * 2 + h % 2]
                        self.act(lt[:, :], pL[:, hh * 128:(hh + 1) * 128], AF.Exp, bias=nacs[:, h:h + 1])
                        yield
                        self.tt(gt[:, :], lt[:, :], CBm_2[p_][:, :], ALU.mult)
                        yield
                        self.mm(pC[:, 128 + hh * 64:128 + (hh + 1) * 64], gt[:, :], xdt[:, h * 64:(h + 1) * 64],
                                True, True)
                        yield
                    pY = banks[5] if p_ == 0 else banks[2]
                    if not samp:
                        self.mm(pY[:, 0:256], CT[:, g, :], ST_bf[:, g * 256:(g + 1) * 256], True, True)
                        yield
                        self.mm(pY[:, 256:512], Btok[:, g, :], xdd[:, g * 256:(g + 1) * 256], True, True)
                        yield
                        self.cp(yo_sb_2[p_][:, :], pY[:, 0:256], eng="act")
                        yield
                        stg_ = ST[:, g * 256:(g + 1) * 256]
                        self.tt(stg_.rearrange("p (h d) -> p h d", h=4), stg_.rearrange("p (h d) -> p h d", h=4),
                                cdec[:, hs:hs + 4].unsqueeze(2).broadcast_to([128, 4, 64]), ALU.mult)
                        yield
                        self.tt(stg_, stg_, pY[:, 256:512], ALU.add)
                        yield
                        self.cp(ST_bf[:, g * 256:(g + 1) * 256], stg_, eng="pool")
                        yield
                    else:
                        sst, osst = V["st_ssm"], V["o_sssm"]
                        for b in range(16):
                            som = soT[p_ * 2 + b % 2]
                            for hp2 in range(2):
                                hp = g * 2 + hp2
                                so = sold[p_ * 4 + (b * 2 + hp2) % 4]
                                r0 = (sw * 8 + hp) * 128
                                self.load(so[:, :], sst[j, b, r0:r0 + 128, :])
                                yield
                                ptr = pL
                                self.tr(ptr[:, 128:256], so[:, :], C["ident_f"])
                                yield
                                self.cp(som[:, hp2 * 128:(hp2 + 1) * 128], ptr[:, 128:256], eng="act")
                                yield
                            ctm = CTm[p_ * 2 + b % 2]
                            self.tt(ctm[:, :], CT[:, g, :], C["colmask_b"][:, b, :], ALU.mult, eng="pool")
                            yield
                            self.mm(pY[:, 0:256], ctm[:, :], som[:, :], b == 0, b == 15)
                            yield
                            xm = xdm[p_ * 2 + b % 2]
                            self.ts(xm[:, :], xdd[:, g * 256:(g + 1) * 256], C["rowmask"][:, b:b + 1], ALU.mult)
                            yield
                            for hp2 in range(2):
                                hp = g * 2 + hp2
                                so = sold[p_ * 4 + (b * 2 + hp2) % 4]
                                sn = snew[p_ * 4 + (b * 2 + hp2) % 4]
                                r0 = (sw * 8 + hp) * 128
                                pS = pL
                                self.mm(pS[:, 256 + hp2 * 128:256 + (hp2 + 1) * 128], xm[:, hp2 * 128:(hp2 + 1) * 128],
                                        Btok[:, g, :], True, True)
                                yield
                                self.stt(sn[:, :], so[:, :], cd_s[:, hp, b:b + 1],
                                         pS[:, 256 + hp2 * 128:256 + (hp2 + 1) * 128], ALU.mult, ALU.add)
                                yield
                                self.store(osst[j, b, r0:r0 + 128, :], sn[:, :])
                                yield
                        self.cp(yo_sb_2[p_][:, :], pY[:, 0:256], eng="act")
                        yield
                    for hh in range(4):
                        h = hs + hh
                        ysl = y_g_2[p_][:, hh * 64:(hh + 1) * 64]
                        self.stt(ysl, yo_sb_2[p_][:, hh * 64:(hh + 1) * 64], dec_in[:, h:h + 1],
                                 pC[:, 128 + hh * 64:128 + (hh + 1) * 64], ALU.mult, ALU.add)
                        yield
                        self.stt(ysl, x_tok[:, h * 64:(h + 1) * 64], hp_bc[:, 2, h:h + 1], ysl, ALU.mult, ALU.add)
                        yield
                    self.tt(y_g_2[p_][:, :], y_g_2[p_][:, :], sz_all[:, g * 256:(g + 1) * 256], ALU.mult)
                    yield
                    self.act(sz_2[p_][:, :], y_g_2[p_][:, :], AF.Square, accum=gss_2[p_][:, :])
                    yield
                    self.rstd_of(gss_2[p_][:, :], 256, grs_2[p_][:, :])
                    yield
                    self.ts(yn_2[p_][:, :], y_g_2[p_][:, :], grs_2[p_][:, 0:1], ALU.mult)
                    yield
                    pt2 = bankb[3] if p_ == 0 else bankb[0]
                    for c2 in range(2):
                        self.tr(pt2[:, 512 + c2 * 128:512 + (c2 + 1) * 128], yn_2[p_][:, c2 * 128:(c2 + 1) * 128],
                                C["ident_b"][:, :])
                        yield
                    self.cp(ynT_2[p_][:, :, :], pt2[:, 512:768].rearrange("p (c t) -> p c t", c=2), eng="act")
                    yield
                    for c2 in range(2):
                        ec = g * 2 + c2
                        for half in range(2):
                            self.mm(pout[half][:, :], ynT_2[p_][:, c2, :], w_out[:, ec, half * 512:(half + 1) * 512],
                                    start=(ec == 0), stop=(ec == 7))
                            yield

                def _interleave(*gens):
                    gens = list(gens)
                    while gens:
                        for g_ in list(gens):
                            try:
                                next(g_)
                            except StopIteration:
                                gens.remove(g_)

                _interleave(group_gen(0), group_gen(1))
                _interleave(group_gen(2), group_gen(3))
                hn = h_new[t % 2]
                for half in range(2):
                    self.tt(hn[:, half * 512:(half + 1) * 512], hacc[:, half * 512:(half + 1) * 512],
                            pout[half][:, :], ALU.add)
                r0 = t * 128
                if sw == 0:
                    self.store(mid_t[r0:r0 + 128, :], hn[:, :], wr=[mid_b[t]])
                elif not last:
                    self.store(dst_t[r0:r0 + 128, :], hn[:, :], wr=[dst_b[t]])
                else:
                    self.final_out(hn[:, :], t, V)
            for hp in range(8):
                ptr = banks[0]
                self.tr(ptr[:, 256:384], ST[:, hp * 128:(hp + 1) * 128], C["ident_f"])
                sn = snew[hp % 4]
                self.cp(sn[:, :], ptr[:, 256:384], eng="act")
                r0 = (sw * 8 + hp) * 128
                self.store(V["o_pssm"][j, r0:r0 + 128, :], sn[:, :])
        self.es = old_es
        self._close_layer(es_layer)

    def _close_layer(self, es_layer):
        self.barrier()
        es_layer.close()

    def barrier(self):
        S = self.S
        tags = []
        for e in ("pe", "act", "dve", "pool"):
            if S.count[e] > 0:
                tags.append((e, S.sem[e], S.count[e]))
        for b in S.dma_bufs:
            if b.dlast is not None:
                tags.append(b.dlast)
        for e in S.ENG:
            waits = []
            for (sk, sem, v) in tags:
                if S.waited[e].get(sk, 0) < v:
                    S.waited[e][sk] = v
                    waits.append((sem, v))
            if waits:
                S.ops[e].append((waits, None, None, 0))

    def mla_layer(self, li, j, last, V):
        nc, S, C, T = self.nc, self.S, self.C, self.T
        sb = self.sb
        NT, NTT, SEQ, NPG = self.NT, self.NT + 1, self.SEQ, self.NPG
        banks, bankb = C["banks"], C["bankb"]
        x_in, hb, hbufs = V["x_in"], V["hb"], V["hbufs"]
        es_layer = ExitStack()
        old_es = self.es
        self.es = es_layer
        SCALE = float((64 + 32) ** -0.5)
        if li == 0:
            src_t, src_b = x_in, None
        else:
            src_t, src_b = hb[(li - 1) % 2], hbufs[(li - 1) % 2]
        dst_t, dst_b = hb[li % 2], hbufs[li % 2]

        w_in = sb("m_win", [128, 8, 1856], BF16)
        w_uq = sb("m_wuq", [128, 4, 2048], BF16)
        w_ukz = sb("m_wukz", [128, 16, 256], BF16)
        w_uv = sb("m_wuv", [128, 2, 1024], BF16)
        w_out = sb("m_wout", [128, 8, D], BF16)
        stage = [sb("m_stage%d" % i, [128, 512]) for i in range(2)]
        nrm = sb("m_nrm", [128, 8])
        qnw = sb("m_qnw", [128, 4])
        kvw_bc = sb("m_kvw", [128, 256])
        ckvT = sb("m_ckvT", [128, 2, SEQ], BF16)
        ckv_tok = sb("m_ckvtok", [128, NT, 256], BF16)
        kpeT = sb("m_kpeT", [32, SEQ], BF16)
        ckvT_s = sb("m_ckvTs", [128, 2, 128], BF16)
        ckv_s = sb("m_ckvs", [128, 256], BF16)
        kpeT_s = sb("m_kpeTs", [32, 128], BF16)
        self._ptc = 0
        PTq = [sb("m_PTq%d" % i, [128, 512], BF16) for i in range(3)]
        q_nopeT = sb("m_qnT", [128, 8, 128], BF16)
        q_latT = sb("m_qlT", [128, 2, 16, 128], BF16)
        q_peT = sb("m_qpT", [32, 16, 128], BF16)
        accT = sb("m_accT", [128, 2, 16, 128], BF16)
        cqn = sb("m_cqn", [128, 512], BF16)
        cqT = sb("m_cqT", [128, 4, 128], BF16)
        ckv_f = sb("m_ckvf", [128, 256])
        kpe_f = sb("m_kpef", [128, 32])
        kpe_t = sb("m_kpet", [128, 32])
        kpe_b = sb("m_kpeb", [128, 32], BF16)
        rtok = [sb("m_rtok%d" % i, [128, 64]) for i in range(2)]
        rT = [sb("m_rT%d" % i, [32, 2, 128]) for i in range(2)]
        qp1 = sb("m_qp1", [128, 512])
        sg = sb("m_sg", [128, D])
        og = sb("m_og", [128, D], BF16)
        oT = sb("m_oT", [128, 8, 128], BF16)
        l_tok = sb("m_ltok", [128, 16])
        rl = sb("m_rl", [128, 16])
        ss2 = sb("m_ss2", [128, 1])
        rs2 = sb("m_rs2", [128, 1])
        ss3 = sb("m_ss3", [128, 1])
        rs3 = sb("m_rs3", [128, 1])
        amask_b = sb("m_amask", [128, 16, 128], BF16)
        idx = sb("m_idx", [128, 16 * NPG], I32)
        pg_c = [sb("m_pgc%d" % i, [128, 288]) for i in range(4)]
        pg_b = [sb("m_pgb%d" % i, [128, 288], BF16) for i in range(4)]
        pgT = [sb("m_pgT%d" % i, [128, 2, 128], BF16) for i in range(4)]
        pkT = [sb("m_pkT%d" % i, [32, 128], BF16) for i in range(4)]
        PTs = [sb("m_PTs%d" % i, [128, 128], BF16) for i in range(4)]
        rlb = sb("m_rlb", [128, 128])
        if last:
            self.yfin = [sb("m_yfin", [128, D])] * 2
            self.ssq2 = sb("m_ssq2", [128, 1])
            self.rstd2 = sb("m_rstd2", [128, 1])
        h_in, h_new = T["h_in"], T["h_new"]

        self.load(nrm[:, :], V["mla_nrm"][j, :, :])
        self.load(qnw[:, :], V["mla_qnw"][j, :, :])
        self.load(kvw_bc[:, :], V["mla_kvw"][j, :, :].rearrange("a d -> (a d)").partition_broadcast(128))
        for hh in range(4):
            self.load(stage[hh % 2][:, :], V["cst"][:, 785 + 2048 + hh * 512:785 + 2048 + (hh + 1) * 512])
            self.cp(amask_b[:, hh * 4:(hh + 1) * 4, :], stage[hh % 2][:, :].rearrange("p (b t) -> p b t", b=4))
        mw, wuq, wuk, wuv, wo = V["mla_win"], V["mla_wuq"], V["mla_wuk"], V["mla_wuv"], V["mla_wout"]
        self.load_weight_bf(w_in, lambda kc, c0, w: mw[j, kc * 128:(kc + 1) * 128, c0:c0 + w],
                            8, 1856, lambda kc: nrm[:, kc:kc + 1], stage, ["dve", "act"])
        self.load_weight_bf(w_uq, lambda kc, c0, w: wuq[j, kc * 128:(kc + 1) * 128, c0:c0 + w],
                            4, 2048, lambda kc: qnw[:, kc:kc + 1], stage, ["dve", "act"])
        self.load_weight_bf(w_uv, lambda kc, c0, w: wuv[j, kc * 128:(kc + 1) * 128, c0:c0 + w],
                            2, 1024, None, stage, ["dve", "act"])
        self.load_weight_bf(w_out, lambda kc, c0, w: wo[j, kc * 128:(kc + 1) * 128, c0:c0 + w],
                            8, D, None, stage, ["dve", "act"])
        self.memset(w_ukz[:, :, :].rearrange("p h c -> p (h c)"), 0.0)
        wz4 = w_ukz[:, :, :].rearrange("p (a two) c -> p a two c", two=2)
        for ci, c0 in enumerate(range(0, 2048, 512)):
            st = stage[ci % 2]
            self.load(st[:, :], wuk[j, :, :, :].rearrange("p a c -> p (a c)")[:, c0:c0 + 512])
            self.cp(wz4[0:64, ci * 2:(ci + 1) * 2, 0, :], st[0:64, :].rearrange("p (a c) -> p a c", a=2))
            self.cp(wz4[64:128, ci * 2:(ci + 1) * 2, 1, :], st[64:128, :].rearrange("p (a c) -> p a c", a=2))
        NI = 16 * NPG
        import os as _os
        _dbg = _os.environ.get("K_DBG", "")
        for c0 in range(0, 0 if "noidx" in _dbg else NI, 512):
            w = min(512, NI - c0)
            st = stage[(c0 // 512) % 2]
            sti = st[:, 0:w].bitcast(I32)
            self.load(sti, V["ptab"][:, c0:c0 + w].rearrange("a n -> (a n)").partition_broadcast(128))
            self.cp(idx[:, c0:c0 + w].bitcast(F32), sti)
            self.ts(idx[:, c0:c0 + w].bitcast(F32), idx[:, c0:c0 + w].bitcast(F32), 128.0, ALU.mult,
                    C["iota_p"], ALU.add)
            if j > 0:
                self.ts(idx[:, c0:c0 + w].bitcast(F32), idx[:, c0:c0 + w].bitcast(F32),
                        float(j * self.NPOOL * 128), ALU.add)
            self.cp(idx[:, c0:c0 + w], idx[:, c0:c0 + w].bitcast(F32))
        pool_c = V["pool_ckv"]
        o_ckv, o_kpe = V["o_pckv"], V["o_pkpe"]
        rope_tok, rope_T = V["rope_tok"], V["rope_T"]

        def load_tile(t):
            r0 = t * 128
            self.load(h_in[t % 2][:, :], src_t[r0:r0 + 128, :], rd=([src_b[t]] if src_b is not None else []))
            self.load(rtok[t % 2][:, :], rope_tok[r0:r0 + 128, :])
            self.load(rT[t % 2][:, :, :], rope_T[:, :, r0:r0 + 128])

        _skip = _os.environ.get("K_SKIP", "")
        for t in range(NTT):
            samp = t == NT
            load_tile(t)
            hin = h_in[t % 2]
            r0 = t * 128
            lnT = self.norm_and_transpose(hin[:, :], 0)
            pq, pk, pg0, pg1 = banks[1], banks[2], banks[3], banks[4]
            for kc in range(8):
                self.mm(pq[:, :], lnT[:, kc, :], w_in[:, kc, 0:512], start=kc == 0, stop=kc == 7)
            for kc in range(8):
                self.mm(pk[:, 0:320], lnT[:, kc, :], w_in[:, kc, 512:832], start=kc == 0, stop=kc == 7)
            for kc in range(8):
                self.mm(pg0[:, :], lnT[:, kc, :], w_in[:, kc, 832:1344], start=kc == 0, stop=kc == 7)
            for kc in range(8):
                self.mm(pg1[:, :], lnT[:, kc, :], w_in[:, kc, 1344:1856], start=kc == 0, stop=kc == 7)
            if "A" not in _skip:
                self.act(T["sq_junk"][:, 0:512], pq[:, :], AF.Square, accum=ss2[:, :])
                self.rstd_of(ss2[:, :], 512, rs2[:, :])
                self.ts(cqn[:, :], pq[:, :], rs2[:, 0:1], ALU.mult)
                ptb = bankb[0]
                for kc in range(4):
                    self.tr(ptb[:, kc * 128:(kc + 1) * 128], cqn[:, kc * 128:(kc + 1) * 128], C["ident_b"][:, :])
                self.cp(cqT[:, :, :], ptb[:, 0:512].rearrange("p (k t) -> p k t", k=4), eng="act")
            if "B" not in _skip:
                self.act(T["sq_junk"][:, 0:256], pk[:, 0:256], AF.Square, accum=ss3[:, :])
                self.rstd_of(ss3[:, :], 256, rs3[:, :])
                self.stt(ckv_f[:, :], pk[:, 0:256], rs3[:, 0:1], kvw_bc[:, :], ALU.mult, ALU.mult)
                self.store(o_ckv[j, r0:r0 + 128, :], ckv_f[:, :])
                self.tt(kpe_f[:, :], pk[:, 256:288], rtok[t % 2][:, 0:32], ALU.mult)
                self.tt(kpe_t[:, :], pk[:, 288:320], rtok[t % 2][:, 32:64], ALU.mult)
                self.tt(kpe_f[:, :], kpe_f[:, :], kpe_t[:, :], ALU.add)
                self.store(o_kpe[j, r0:r0 + 128, :], kpe_f[:, :])
                self.cp(kpe_b[:, :], kpe_f[:, :])
                kv_dst = ckv_s[:, :] if samp else ckv_tok[:, t, :]
                self.cp(kv_dst, ckv_f[:, :], eng="pool")
                for cc in range(2):
                    self.tr(ptb[:, 512 + cc * 128:512 + (cc + 1) * 128], kv_dst[:, cc * 128:(cc + 1) * 128]
                            if False else (ckv_s[:, cc * 128:(cc + 1) * 128] if samp else ckv_tok[:, t, cc * 128:(cc + 1) * 128]),
                            C["ident_b"][:, :])
                kT_dst = ckvT_s[:, :, :] if samp else ckvT[:, :, r0:r0 + 128]
                self.cp(kT_dst, ptb[:, 512:768].rearrange("p (c t) -> p c t", c=2), eng="act")
                self.tr(ptb[0:32, 768:896], kpe_b[:, :], C["ident_b"][:, :])
                kp_dst = kpeT_s[:, :] if samp else kpeT[:, r0:r0 + 128]
                self.cp(kp_dst, ptb[0:32, 768:896], eng="act")
            if "C" not in _skip:
                self.act(sg[:, 0:512], pg0[:, :], AF.Silu)
                self.act(sg[:, 512:1024], pg1[:, :], AF.Silu)
            if "D" not in _skip:
                for hq in range(2):
                    pb = banks[1 + hq]
                    for h4 in range(4):
                        hp = hq * 4 + h4
                        for kc in range(4):
                            self.mm(pb[:, h4 * 128:(h4 + 1) * 128], w_uq[:, kc, hp * 128:(hp + 1) * 128], cqT[:, kc, :],
                                    start=kc == 0, stop=kc == 3)
                    self.cp(q_nopeT[:, hq * 4:(hq + 1) * 4, :].rearrange("p h t -> p (h t)"), pb[:, :], eng="act")
            if "E" not in _skip:
                ppe, ppes = banks[2], banks[3]
                for kc in range(4):
                    self.mm(ppe[:, :], cqT[:, kc, :], w_uq[:, kc, 1024:1536], start=kc == 0, stop=kc == 3)
                for kc in range(4):
                    self.mm(ppes[:, :], cqT[:, kc, :], w_uq[:, kc, 1536:2048], start=kc == 0, stop=kc == 3)
                cosb = rtok[t % 2][:, 0:32].unsqueeze(1).broadcast_to([128, 16, 32])
                sinb = rtok[t % 2][:, 32:64].unsqueeze(1).broadcast_to([128, 16, 32])
                self.tt(qp1[:, :].rearrange("p (h r) -> p h r", h=16), ppe[:, :].rearrange("p (h r) -> p h r", h=16),
                        cosb, ALU.mult)
                self.tt(ppes[:, :].rearrange("p (h r) -> p h r", h=16), ppes[:, :].rearrange("p (h r) -> p h r", h=16),
                        sinb, ALU.mult)
                self.tt(cqn[:, :], qp1[:, :], ppes[:, :], ALU.add)
                for hf in range(2):
                    pbq = bankb[hf]
                    for hh in range(8):
                        h = hf * 8 + hh
                        self.tr(pbq[0:32, hh * 128:(hh + 1) * 128], cqn[:, h * 32:(h + 1) * 32], C["ident_b"][:, :])
                    self.cp(q_peT[:, hf * 8:(hf + 1) * 8, :].rearrange("p h t -> p (h t)"), pbq[0:32, 0:1024], eng="act")
            if "F" not in _skip:
                for q4 in range(4):
                    for cc in range(2):
                        pb = banks[1 + (q4 * 2 + cc) % 2]
                        for hh in range(4):
                            h = q4 * 4 + hh
                            hp = h // 2
                            self.mm(pb[:, hh * 128:(hh + 1) * 128], w_ukz[:, h, cc * 128:(cc + 1) * 128],
                                    q_nopeT[:, hp, :], True, True)
                        self.cp(q_latT[:, cc, q4 * 4:(q4 + 1) * 4, :].rearrange("p h t -> p (h t)"), pb[:, :], eng="act")
            if not samp:
                nk = t + 1
                for q4 in range(4):
                    h0 = q4 * 4
                    pacc = (banks[5], banks[6]) if q4 % 2 == 0 else (banks[1], banks[2])
                    pl = banks[7]

                    def scores(jk, h0=h0):
                        ps = banks[3 + jk % 2]
                        k0 = jk * 128
                        self.mm(ps[:, :], ckvT[:, 0, k0:k0 + 128], q_latT[:, 0, h0:h0 + 4, :], True, False)
                        self.mm(ps[:, :], ckvT[:, 1, k0:k0 + 128], q_latT[:, 1, h0:h0 + 4, :], False, False)
                        self.mm(ps[:, :], kpeT[:, k0:k0 + 128], q_peT[:, h0:h0 + 4, :], False, True)

                    scores(0)
                    for jk in range(nk):
                        if jk + 1 < nk:
                            scores(jk + 1)
                        pt = PTq[self._ptc % 3]
                        self._ptc += 1
                        self.act(pt[:, :], banks[3 + jk % 2][:, :], AF.Exp, scale=SCALE)
                        if jk == t:
                            self.tt(pt[:, :].rearrange("p (h t) -> p h t", h=4), pt[:, :].rearrange("p (h t) -> p h t", h=4),
                                    C["Tb_p"][:, :].unsqueeze(1).broadcast_to([128, 4, 128]), ALU.mult)
                        for cc in range(2):
                            self.mm(pacc[cc][:, :], ckv_tok[:, jk, cc * 128:(cc + 1) * 128], pt[:, :],
                                    jk == 0, jk == nk - 1)
                        for hh in range(4):
                            self.mm(pl[:, h0 + hh:h0 + hh + 1], pt[:, hh * 128:(hh + 1) * 128], C["ones_b"][:, 0:1],
                                    jk == 0 and hh == 0, jk == nk - 1 and hh == 3)
                    for cc in range(2):
                        self.cp(accT[:, cc, h0:h0 + 4, :].rearrange("p h t -> p (h t)"), pacc[cc][:, :], eng="act")
                self.cp(l_tok[:, :], banks[7][:, 0:16])
            else:
                self.memset(l_tok[:, :], 1.0)
                steps = [(b, pgi) for b in range(16) for pgi in range(NPG)]
                NST = len(steps)

                def gather(i):
                    b_, pg_ = steps[i]
                    col = b_ * NPG + pg_
                    dst = pg_c[i % 4]
                    self.S.dma("pool", (lambda e, o_=dst[:, :], i_=pool_c[:, :], x_=idx[:, col:col + 1]:
                                        e.indirect_dma_start(out=o_, out_offset=None, in_=i_,
                                                             in_offset=bass.IndirectOffsetOnAxis(ap=x_, axis=0))),
                               S.tbuf(dst[:, :]), reads=[S.tbuf(idx[:, :])], writes=[S.tbuf(dst[:, :])])

                for i in range(min(3, NST)):
                    gather(i)
                pacc = (banks[5], banks[6])
                pl = banks[7]
                si = 0
                for b in range(16):
                    tsl = slice(b * 8, b * 8 + 8)
                    for pgi in range(NPG + 1):
                        new = pgi == NPG
                        if not new:
                            if si + 3 < NST:
                                gather(si + 3)
                            s4 = si % 4
                            pgf = pg_c[s4]
                            self.cp(pg_b[s4][:, :], pgf[:, :])
                            ptf = bankb[1 + si % 2]
                            for cc in range(2):
                                self.tr(ptf[:, cc * 128:(cc + 1) * 128], pg_b[s4][:, cc * 128:(cc + 1) * 128],
                                        C["ident_b"][:, :])
                            self.tr(ptf[0:32, 256:384], pg_b[s4][:, 256:288], C["ident_b"][:, :])
                            self.cp(pgT[s4][:, :, :], ptf[:, 0:256].rearrange("p (c t) -> p c t", c=2), eng="act")
                            self.cp(pkT[s4][:, :], ptf[0:32, 256:384], eng="act")
                            kT0, kT1, kP, kV = pgT[s4][:, 0, :], pgT[s4][:, 1, :], pkT[s4][:, :], pg_b[s4]
                            si += 1
                        else:
                            kT0, kT1, kP, kV = ckvT_s[:, 0, :], ckvT_s[:, 1, :], kpeT_s[:, :], ckv_s
                        ps = banks[3 + self._ptc % 2]
                        self.mm(ps[:, 0:128], kT0, q_latT[:, 0, :, tsl], True, False)
                        self.mm(ps[:, 0:128], kT1, q_latT[:, 1, :, tsl], False, False)
                        self.mm(ps[:, 0:128], kP, q_peT[:, :, tsl], False, True)
                        pt = PTs[self._ptc % 4]
                        self._ptc += 1
                        self.act(pt[:, :], ps[:, 0:128], AF.Exp, scale=SCALE)
                        if new:
                            self.tt(pt[:, :], pt[:, :], amask_b[:, b, :], ALU.mult)
                        for cc in range(2):
                            self.mm(pacc[cc][:, 0:128], kV[:, cc * 128:(cc + 1) * 128], pt[:, :], pgi == 0, new)
                        self.mm(pl[:, 128:256], C["ones_b"][:, :], pt[:, :], pgi == 0, new)
                    self.recip(rlb[:, :], pl[:, 128:256])
                    for cc in range(2):
                        self.tt(accT[:, cc, :, tsl], pacc[cc][:, 0:128].rearrange("p (h k) -> p h k", h=16),
                                rlb[:, :].rearrange("p (h k) -> p h k", h=16), ALU.mult)
            po = (banks[1], banks[2])
            for h in range(16):
                for cc in range(2):
                    self.mm(po[h // 8][:, (h % 8) * 64:(h % 8 + 1) * 64], accT[:, cc, h, :],
                            w_uv[:, cc, h * 64:(h + 1) * 64], cc == 0, cc == 1)
            self.recip(rl[:, :], l_tok[:, :])
            for half in range(2):
                self.tt(sg[:, half * 512:(half + 1) * 512], po[half][:, :], sg[:, half * 512:(half + 1) * 512], ALU.mult)
                self.tt(og[:, half * 512:(half + 1) * 512].rearrange("p (h v) -> p h v", h=8),
                        sg[:, half * 512:(half + 1) * 512].rearrange("p (h v) -> p h v", h=8),
                        rl[:, half * 8:(half + 1) * 8].unsqueeze(2).broadcast_to([128, 8, 64]), ALU.mult)
            for kc in range(8):
                self.tr(ptb[:, kc * 128:(kc + 1) * 128], og[:, kc * 128:(kc + 1) * 128], C["ident_b"][:, :])
            self.cp(oT[:, :, :], ptb[:, 0:1024].rearrange("p (k t) -> p k t", k=8), eng="act")
            pout = (banks[5], banks[6])
            for half in range(2):
                for kc in range(8):
                    self.mm(pout[half][:, :], oT[:, kc, :], w_out[:, kc, half * 512:(half + 1) * 512],
                            start=kc == 0, stop=kc == 7)
            hn = h_new[t % 2]
            for half in range(2):
                self.tt(hn[:, half * 512:(half + 1) * 512], hin[:, half * 512:(half + 1) * 512],
                        pout[half][:, :], ALU.add)
            if not last:
                self.store(dst_t[r0:r0 + 128, :], hn[:, :], wr=[dst_b[t]])
            else:
                self.final_out(hn[:, :], t, V)
        self.es = old_es
        self._close_layer(es_layer)


def make_consts(SEQ):
    P = 128
    idx = np.arange(P)
    ident = np.eye(P, dtype=np.float32)
    T_p = (idx[:, None] <= idx[None, :]).astype(np.float32)
    same = (idx[:, None] // 8 == idx[None, :] // 8)
    T_s = (T_p.astype(bool) & same).astype(np.float32)
    SEG_p = np.ones((P, P), np.float32)
    SEG_s = same.astype(np.float32)
    ones = np.ones((P, P), np.float32)
    rowmask = (idx[:, None] // 8 == np.arange(16)[None, :]).astype(np.float32)
    colmask = np.broadcast_to((np.arange(16)[:, None] == (idx[None, :] // 8))[None], (P, 16, P))
    colmask = colmask.astype(np.float32).reshape(P, 2048)
    kb, kt = idx // 8, idx % 8
    tok = np.arange(128) % 8
    am = (kb[:, None, None] == np.arange(16)[None, :, None]) & (kt[:, None, None] <= tok[None, None, :])
    am = am.astype(np.float32).reshape(P, 2048)
    iota = idx.astype(np.float32)[:, None]
    cst = np.concatenate([ident, T_p, T_s, SEG_p, SEG_s, ones, rowmask, iota, colmask, am], axis=1)
    return np.ascontiguousarray(cst.astype(np.float32))


def rope_tables(SEQ, past_len):
    pos = np.concatenate([np.arange(SEQ, dtype=np.float64),
                          np.tile(past_len + np.arange(8, dtype=np.float64), 16)])
    inv = 1.0 / (10000.0 ** (np.arange(0, 32, 2, dtype=np.float64) / 32))
    ang = pos[:, None] * inv[None, :]
    cos, sin = np.cos(ang), np.sin(ang)
    tok = np.concatenate([cos, cos, -sin, sin], axis=1).astype(np.float32)
    rt = np.stack([np.concatenate([cos, cos], 1).T, np.concatenate([-sin, sin], 1).T], axis=1)
    return np.ascontiguousarray(tok), np.ascontiguousarray(rt.astype(np.float32))


_PROG_CACHE = {}


def _get_prog(cfg):
    key = (cfg["SEQ"], cfg["NPG"], cfg["NPOOL"], tuple(cfg["LAYERS"]))
    if key not in _PROG_CACHE:
        p = Prog(cfg)
        p.build()
        _PROG_CACHE[key] = p
    return _PROG_CACHE[key]


def run_cfg(cfg, inputs, n_cores, past_len):
    f = lambda a: np.ascontiguousarray(np.asarray(a))
    SEQ, NPG, NPOOL = cfg["SEQ"], cfg["NPG"], cfg["NPOOL"]
    LAYERS = cfg["LAYERS"]
    prog = _get_prog(cfg)
    nS, nM = prog.n_ssd, prog.n_mla
    ROWS = SEQ + 128
    x_prompt, x_sample = f(inputs["x_prompt"]), f(inputs["x_sample"])
    cst = make_consts(SEQ)
    rope_tok, rope_T = rope_tables(SEQ, past_len)
    shared = {"cst": cst, "rope_tok": rope_tok, "rope_T": rope_T,
              "fnw": f(inputs["final_norm_w"]).reshape(1, D)}
    norm_w = f(inputs["norm_w"])
    ssd_idx = [i for i, k in enumerate(LAYERS) if k == "ssd"]
    mla_idx = [i for i, k in enumerate(LAYERS) if k == "mla"]
    if nS:
        w_in = f(inputs["ssd_w_in"]); w_out = f(inputs["ssd_w_out"])
        cwv = f(inputs["ssd_conv_w"]); cbv = f(inputs["ssd_conv_b"])
        dtb = f(inputs["ssd_dt_bias"]); alog = f(inputs["ssd_a_log"]); dsk = f(inputs["ssd_d"])
        gnw = f(inputs["ssd_norm_w"])
        win_l, wout_l, gnw_l, cw_l, hp_l = [], [], [], [], []
        for j in range(nS):
            for sw in range(2):
                z = w_in[j][:, sw * 1024:(sw + 1) * 1024]
                x = w_in[j][:, 2048 + sw * 1024:2048 + (sw + 1) * 1024]
                Bc = w_in[j][:, 4096 + sw * 512:4096 + (sw + 1) * 512]
                Cc = w_in[j][:, 5120 + sw * 512:5120 + (sw + 1) * 512]
                dtc = w_in[j][:, 6144 + sw * 16:6144 + (sw + 1) * 16]
                win_l.append(np.concatenate([z, x, Bc, Cc, dtc], axis=1))
                wout_l.append(w_out[j][sw * 1024:(sw + 1) * 1024, :])
                gnw_l.append(gnw[j][sw * 1024:(sw + 1) * 1024].reshape(8, 128).T)
                ch = np.concatenate([np.arange(sw * 1024, (sw + 1) * 1024),
                                     2048 + np.arange(sw * 512, (sw + 1) * 512),
                                     3072 + np.arange(sw * 512, (sw + 1) * 512)])
                cwb = np.concatenate([cwv[j][:, ch], cbv[j][None, ch]], axis=0)
                cw_l.append(cwb.reshape(5, 16, 128).transpose(2, 1, 0))
                hp_l.append(np.stack([dtb[j][sw * 16:(sw + 1) * 16], alog[j][sw * 16:(sw + 1) * 16],
                                      dsk[j][sw * 16:(sw + 1) * 16]]))
        shared["ssd_win"] = f(np.stack(win_l)); shared["ssd_wout"] = f(np.stack(wout_l))
        shared["ssd_gnw"] = f(np.stack(gnw_l)); shared["ssd_cw"] = f(np.stack(cw_l))
        shared["ssd_hp"] = f(np.stack(hp_l))
        shared["ssd_nrm"] = f(np.stack([norm_w[i].reshape(8, 128).T for i in ssd_idx]))
    if nM:
        mw = f(inputs["mla_w_in"])
        perm = np.concatenate([np.arange(16, 32), np.arange(0, 16)])
        win_l = []
        for jm in range(nM):
            kpe = mw[jm][:, 768:800]
            win_l.append(np.concatenate([mw[jm][:, 0:768], kpe, kpe[:, perm], mw[jm][:, 800:1824]], axis=1))
        shared["mla_win"] = f(np.stack(win_l))
        shared["mla_nrm"] = f(np.stack([norm_w[i].reshape(8, 128).T for i in mla_idx]))
        shared["mla_qnw"] = f(np.stack([f(inputs["mla_q_norm_w"])[jm].reshape(4, 128).T for jm in range(nM)]))
        shared["mla_kvw"] = f(f(inputs["mla_kv_norm_w"])[:nM].reshape(nM, 1, 256))
        wuq = f(inputs["mla_w_uq"])
        nope = wuq[:nM, :, :, 0:64].reshape(nM, 512, 1024)
        pe = wuq[:nM, :, :, 64:96]
        shared["mla_wuq"] = f(np.concatenate([nope, pe.reshape(nM, 512, 512),
                                              pe[..., perm].reshape(nM, 512, 512)], axis=2))
        wuk = f(inputs["mla_w_uk"])[:nM]
        t_ = wuk.transpose(0, 2, 3, 1).reshape(nM, 8, 2, 64, 256)
        shared["mla_wuk"] = f(t_.transpose(0, 2, 3, 1, 4).reshape(nM, 128, 8, 256))
        shared["mla_wuv"] = f(f(inputs["mla_w_uv"])[:nM].reshape(nM, 256, 1024))
        shared["mla_wout"] = f(inputs["mla_w_out"])[:nM]
        shared["pool_all"] = np.concatenate([f(inputs["cache_ckv"])[:nM].reshape(nM * NPOOL * 128, 256),
                                             f(inputs["cache_kpe"])[:nM].reshape(nM * NPOOL * 128, 32)], axis=1)
    in_maps = []
    for c in range(n_cores):
        k = c // 2
        m = dict(shared)
        m["x_in"] = f(np.concatenate([x_prompt[k], x_sample[16 * c:16 * c + 16].reshape(128, D)], axis=0))
        if nS:
            m["st_ssm"] = f(inputs["state_ssm"])[:nS, 16 * c:16 * c + 16].reshape(nS, 16, 2048, 128)
            m["st_conv"] = f(inputs["state_conv"])[:nS, 16 * c:16 * c + 16].reshape(nS, 48, 4096)
        if nM:
            m["ptab"] = f(inputs["page_table"])[16 * c:16 * c + 16].reshape(1, 16 * NPG).astype(np.int32)
        in_maps.append(m)
    import os as _os
    if _os.environ.get("K_TRACE"):
        res = run_bass_kernel_spmd(prog.nc, in_maps, core_ids=list(range(n_cores)), trace=True)
        print("EXEC_TIME_NS", res.exec_time_ns, {e: len(v) for e, v in prog.S.ops.items()})
    else:
        res = run_bass_kernel_spmd(prog.nc, in_maps, core_ids=list(range(n_cores)))
    R = res.results
    nseq = n_cores // 2
    out = {}
    out["y_prompt"] = np.stack([R[2 * k]["y_out"][:SEQ] for k in range(nseq)])
    out["y_sample"] = np.concatenate([R[c]["y_out"][SEQ:].reshape(16, 8, D) for c in range(n_cores)])
    if nS:
        out["p_ssm"] = np.stack([R[2 * k]["o_pssm"].reshape(nS, 32, 64, 128) for k in range(nseq)], axis=1)
        out["p_conv"] = np.stack([R[2 * k]["o_pconv"] for k in range(nseq)], axis=1)
        out["s_ssm"] = np.concatenate([R[c]["o_sssm"].reshape(nS, 16, 32, 64, 128) for c in range(n_cores)], axis=1)
        out["s_conv"] = np.concatenate([R[c]["o_sconv"].reshape(nS, 16, 3, 4096) for c in range(n_cores)], axis=1)
    if nM:
        out["p_ckv"] = np.stack([R[2 * k]["o_ckv"][:, :SEQ] for k in range(nseq)], axis=1)
        out["p_kpe"] = np.stack([R[2 * k]["o_kpe"][:, :SEQ] for k in range(nseq)], axis=1)
        out["s_ckv"] = np.concatenate([R[c]["o_ckv"][:, SEQ:].reshape(nM, 16, 8, 256) for c in range(n_cores)], axis=1)
        out["s_kpe"] = np.concatenate([R[c]["o_kpe"][:, SEQ:].reshape(nM, 16, 8, 32) for c in range(n_cores)], axis=1)
    return out


def kernel(**inputs):
    out = run_cfg(FULL_CFG, inputs, 8, 8192)
    return (out["y_prompt"], out["y_sample"], out["p_ssm"], out["p_conv"], out["p_ckv"], out["p_kpe"],
            out["s_ssm"], out["s_conv"], out["s_ckv"], out["s_kpe"])
```

```python
import math
from contextlib import ExitStack

import numpy as np
import concourse.bass as bass
import concourse.mybir as mybir
from concourse.bass_utils import run_bass_kernel_spmd

F32 = mybir.dt.float32
BF16 = mybir.dt.bfloat16
I32 = mybir.dt.int32
AF = mybir.ActivationFunctionType
ALU = mybir.AluOpType

D = 1024
EPS = 1e-6
NEG = -1.0e5

FULL_CFG = dict(SEQ=4096, NPG=64, NPOOL=10240, LAYERS=("ssd", "mla", "ssd", "mla"))


class Buf:
    __slots__ = ("name", "w", "r", "dsem", "dcount", "dlast", "excl")

    def __init__(self, name):
        self.name = name
        self.w = None
        self.r = []
        self.dsem = None
        self.dcount = 0
        self.dlast = None
        self.excl = False


class Sched:
    ENG = ("pe", "act", "dve", "pool", "sp")

    def __init__(self, nc, es):
        self.nc = nc
        self.es = es
        self.ops = {e: [] for e in self.ENG}
        self.sem = {}
        for e in ("pe", "act", "dve", "pool"):
            self.sem[e] = es.enter_context(nc.semaphore("sem_" + e))
        self.count = {e: 0 for e in self.ENG}
        self.waited = {e: {} for e in self.ENG}
        self.bufs = {}
        self.dma_bufs = []
        self.nsem = 4

    def buf(self, name):
        b = Buf(name)
        return b

    def tbuf(self, ap):
        n = ap.tensor.name
        if n not in self.bufs:
            self.bufs[n] = Buf(n)
        return self.bufs[n]

    def _deps(self, eng, reads, writes):
        deps = []
        for b in reads:
            if b.w is not None:
                deps.append(b.w)
        for b in writes:
            if b.w is not None:
                deps.append(b.w)
            deps.extend(b.r)
        waits = []
        wd = self.waited[eng]
        best = {}
        for (sk, sem, v) in deps:
            if sk == eng and eng == "pe":
                continue
            if wd.get(sk, 0) >= v:
                continue
            if sk not in best or best[sk][1] < v:
                best[sk] = (sem, v)
        for sk, (sem, v) in best.items():
            wd[sk] = v
            waits.append((sem, v))
        return waits

    def op(self, eng, fn, reads=(), writes=()):
        writes = list(dict.fromkeys(list(writes) + [b for b in reads if b.excl]))
        reads = list(dict.fromkeys(b for b in reads if not b.excl))
        waits = self._deps(eng, reads, writes)
        self.count[eng] += 1
        tag = (eng, self.sem[eng], self.count[eng])
        self.ops[eng].append((waits, fn, self.sem[eng], 1))
        for b in reads:
            b.r.append(tag)
        for b in writes:
            b.w = tag
            b.r = []
        return tag

    def dma(self, queue, fn, sb, reads=(), writes=()):
        if sb.dsem is None:
            sb.dsem = self.es.enter_context(self.nc.semaphore("d_" + sb.name))
            self.dma_bufs.append(sb)
            self.nsem += 1
        reads = list(dict.fromkeys(reads))
        writes = list(dict.fromkeys(writes))
        waits = self._deps(queue, reads, writes)
        if sb.dlast is not None:
            sk, sem, v = sb.dlast
            if self.waited[queue].get(sk, 0) < v:
                self.waited[queue][sk] = v
                waits.append((sem, v))
        sb.dcount += 1
        tag = ("d_" + sb.name, sb.dsem, 16 * sb.dcount)
        sb.dlast = tag
        self.ops[queue].append((waits, fn, sb.dsem, 16))
        for b in reads:
            b.r.append(tag)
        for b in writes:
            b.w = tag
            b.r = []
        return tag

    def finish(self):
        waits = []
        for b in self.dma_bufs:
            waits.append((b.dsem, 16 * b.dcount))
        self.ops["sp"].append((waits, None, None, 0))

    def emit(self):
        nc = self.nc
        ops = self.ops

        def run(e, lst):
            for (waits, fn, sem, amt) in lst:
                for (s, v) in waits:
                    e.wait_ge(s, v)
                if fn is not None:
                    inst = fn(e)
                    inst.then_inc(sem, amt)

        with nc.Block() as block:
            @block.sync
            def _(e):
                run(e, ops["sp"])

            @block.scalar
            def _(e):
                run(e, ops["act"])

            @block.vector
            def _(e):
                run(e, ops["dve"])

            @block.gpsimd
            def _(e):
                run(e, ops["pool"])

            @block.tensor
            def _(e):
                run(e, ops["pe"])


class Prog:
    def __init__(self, cfg):
        self.cfg = cfg
        self.SEQ = cfg["SEQ"]
        self.NT = self.SEQ // 128
        self.NPG = cfg["NPG"]
        self.NPOOL = cfg["NPOOL"]
        self.LAYERS = cfg["LAYERS"]
        self.n_ssd = sum(1 for l in self.LAYERS if l == "ssd")
        self.n_mla = sum(1 for l in self.LAYERS if l == "mla")
        self.nc = bass.Bass("TRN2", target_bir_lowering=False)
        self.es = ExitStack()
        self.S = None
        self.dram = {}

    def din(self, name, shape, dt=F32):
        t = self.nc.dram_tensor(name, list(shape), dt, kind="ExternalInput")
        self.dram[name] = t
        return t

    def dout(self, name, shape, dt=F32):
        t = self.nc.dram_tensor(name, list(shape), dt, kind="ExternalOutput")
        self.dram[name] = t
        return t

    def dint(self, name, shape, dt=F32):
        t = self.nc.dram_tensor(name, list(shape), dt, kind="Internal")
        self.dram[name] = t
        return t

    def sb(self, name, shape, dt=F32):
        self._uid = getattr(self, "_uid", 0) + 1
        return self.es.enter_context(self.nc.sbuf_tensor("%s_%d" % (name, self._uid), list(shape), dt))

    def _rw(self, ins, outs, rd, wr):
        S = self.S
        reads = list(rd) if rd is not None else []
        writes = list(wr) if wr is not None else []
        if rd is None:
            reads = [S.tbuf(a) for a in ins if a is not None and not isinstance(a, (int, float))]
        if wr is None:
            writes = [S.tbuf(a) for a in outs if a is not None]
        return reads, writes

    def mm(self, out, lhsT, rhs, start=True, stop=True, rd=None, wr=None, xrd=()):
        reads, writes = self._rw([lhsT, rhs], [out], rd, wr)
        reads += list(xrd)
        self.S.op("pe", lambda e: e.matmul(out, lhsT, rhs, start=start, stop=stop,
                                           skip_group_check=True), reads, writes)

    def tr(self, out, in_, ident, rd=None, wr=None):
        reads, writes = self._rw([in_, ident], [out], rd, wr)
        self.S.op("pe", lambda e: e.transpose(out, in_, ident), reads, writes)

    def act(self, out, in_, func, bias=0.0, scale=1.0, accum=None, eng="act", rd=None, wr=None):
        ins = [in_]
        if not isinstance(bias, (int, float)):
            ins.append(bias)
        if not isinstance(scale, (int, float)):
            ins.append(scale)
        outs = [out] + ([accum] if accum is not None else [])
        reads, writes = self._rw(ins, outs, rd, wr)
        if accum is None:
            self.S.op("act", lambda e: e.activation(out, in_, func, bias=bias, scale=scale), reads, writes)
        else:
            self.S.op("act", lambda e: e.activation(out, in_, func, bias=bias, scale=scale,
                                                    accum_out=accum), reads, writes)

    def tt(self, out, in0, in1, op, eng="dve", rd=None, wr=None):
        reads, writes = self._rw([in0, in1], [out], rd, wr)
        self.S.op(eng, lambda e: e.tensor_tensor(out=out, in0=in0, in1=in1, op=op), reads, writes)

    def ts(self, out, in0, s1, op0, s2=None, op1=None, eng="dve", rd=None, wr=None):
        ins = [in0] + [s for s in (s1, s2) if s is not None and not isinstance(s, (int, float))]
        reads, writes = self._rw(ins, [out], rd, wr)
        if op1 is None:
            self.S.op(eng, lambda e: e.tensor_scalar(out=out, in0=in0, scalar1=s1, scalar2=None, op0=op0),
                      reads, writes)
        else:
            self.S.op(eng, lambda e: e.tensor_scalar(out=out, in0=in0, scalar1=s1, scalar2=s2, op0=op0,
                                                     op1=op1), reads, writes)

    def stt(self, out, in0, scalar, in1, op0, op1, rd=None, wr=None):
        ins = [in0, in1] + ([scalar] if not isinstance(scalar, (int, float)) else [])
        reads, writes = self._rw(ins, [out], rd, wr)
        self.S.op("dve", lambda e: e.scalar_tensor_tensor(out=out, in0=in0, scalar=scalar, in1=in1,
                                                          op0=op0, op1=op1), reads, writes)

    def cp(self, out, in_, eng="dve", rd=None, wr=None):
        reads, writes = self._rw([in_], [out], rd, wr)
        if eng == "act":
            self.S.op("act", lambda e: e.copy(out, in_), reads, writes)
        else:
            self.S.op(eng, lambda e: e.tensor_copy(out=out, in_=in_), reads, writes)

    def memset(self, ap, val, eng="dve", wr=None):
        reads, writes = self._rw([], [ap], None, wr)
        self.S.op(eng, lambda e: e.memset(ap, val), reads, writes)

    def recip(self, out, in_, rd=None, wr=None):
        reads, writes = self._rw([in_], [out], rd, wr)
        self.S.op("dve", lambda e: e.reciprocal(out=out, in_=in_), reads, writes)

    def load(self, out, in_, sbbuf=None, rd=(), wr=None, q="sp", nc_ok=False):
        S = self.S
        sbb = sbbuf if sbbuf is not None else S.tbuf(out)
        writes = [sbb] if wr is None else list(wr)
        if nc_ok:
            fn = lambda e: e.dma_start(out=out, in_=in_, allow_slow_non_contiguous=True)
        else:
            fn = lambda e: e.dma_start(out=out, in_=in_)
        S.dma(q, fn, sbb, reads=list(rd), writes=writes)

    def store(self, out, in_, sbbuf=None, rd=None, wr=(), q="sp", nc_ok=False):
        S = self.S
        sbb = sbbuf if sbbuf is not None else S.tbuf(in_)
        reads = [sbb] if rd is None else list(rd)
        if nc_ok:
            fn = lambda e: e.dma_start(out=out, in_=in_, allow_slow_non_contiguous=True)
        else:
            fn = lambda e: e.dma_start(out=out, in_=in_)
        S.dma(q, fn, sbb, reads=reads, writes=list(wr))

    def rstd_of(self, ssq, n, out):
        self.ts(out, ssq, 1.0 / n, ALU.mult, EPS, ALU.add)
        self.act(out, out, AF.Ln)
        self.act(out, out, AF.Exp, scale=-0.5)

    def build(self):
        nc = self.nc
        cfg = self.cfg
        SEQ, NT, NPG, NPOOL = self.SEQ, self.NT, self.NPG, self.NPOOL
        nS, nM = self.n_ssd, self.n_mla
        NTT = NT + 1
        ROWS = SEQ + 128
        es = self.es
        self.S = S = Sched(nc, es)

        x_in = self.din("x_in", [ROWS, D])
        cst = self.din("cst", [128, 128 * 6 + 16 + 1 + 4096])
        rope_tok = self.din("rope_tok", [ROWS, 64])
        rope_T = self.din("rope_T", [32, 2, ROWS])
        fnw = self.din("fnw", [1, D])
        y_out = self.dout("y_out", [ROWS, D])
        hb = [self.dint("hbA", [ROWS, D]), self.dint("hbB", [ROWS, D])]
        if nS:
            ssd_win = self.din("ssd_win", [nS * 2, D, 3088])
            ssd_wout = self.din("ssd_wout", [nS * 2, 1024, D])
            ssd_nrm = self.din("ssd_nrm", [nS, 128, 8])
            ssd_gnw = self.din("ssd_gnw", [nS * 2, 128, 8])
            ssd_cw = self.din("ssd_cw", [nS * 2, 128, 16, 5])
            ssd_hp = self.din("ssd_hp", [nS * 2, 3, 16])
            st_ssm = self.din("st_ssm", [nS, 16, 2048, 128])
            st_conv = self.din("st_conv", [nS, 48, 4096])
            o_pssm = self.dout("o_pssm", [nS, 2048, 128])
            o_pconv = self.dout("o_pconv", [nS, 3, 4096])
            o_sssm = self.dout("o_sssm", [nS, 16, 2048, 128])
            o_sconv = self.dout("o_sconv", [nS, 48, 4096])
        if nM:
            mla_win = self.din("mla_win", [nM, D, 1856])
            mla_nrm = self.din("mla_nrm", [nM, 128, 8])
            mla_qnw = self.din("mla_qnw", [nM, 128, 4])
            mla_kvw = self.din("mla_kvw", [nM, 1, 256])
            mla_wuq = self.din("mla_wuq", [nM, 512, 2048])
            mla_wuk = self.din("mla_wuk", [nM, 128, 8, 256])
            mla_wuv = self.din("mla_wuv", [nM, 256, 1024])
            mla_wout = self.din("mla_wout", [nM, 1024, D])
            pool_ckv = self.din("pool_all", [nM * NPOOL * 128, 288])
            ptab = self.din("ptab", [1, 16 * NPG], I32)
            o_pckv = self.dout("o_ckv", [nM, ROWS, 256])
            o_pkpe = self.dout("o_kpe", [nM, ROWS, 32])

        sb = self.sb
        NCS = 128 * 6 + 16 + 1
        cst_sb = sb("cst_sb", [128, NCS])
        self.load(cst_sb[:, :], cst[:, 0:NCS])
        o = 0
        ident_f = cst_sb[:, o:o + 128]; o += 128
        T_p = cst_sb[:, o:o + 128]; o += 128
        T_s = cst_sb[:, o:o + 128]; o += 128
        SEG_p = cst_sb[:, o:o + 128]; o += 128
        SEG_s = cst_sb[:, o:o + 128]; o += 128
        ones_f = cst_sb[:, o:o + 128]; o += 128
        rowmask = cst_sb[:, o:o + 16]; o += 16
        iota_p = cst_sb[:, o:o + 1]; o += 1
        ident_b = sb("ident_b", [128, 128], BF16)
        self.cp(ident_b[:, :], ident_f)
        ones_b = sb("ones_b", [128, 128], BF16)
        self.cp(ones_b[:, :], ones_f)
        Tb_p = sb("Tb_p", [128, 128], BF16)
        self.cp(Tb_p[:, :], T_p)
        Tb_s = sb("Tb_s", [128, 128], BF16)
        self.cp(Tb_s[:, :], T_s)
        negm_p = sb("negm_p", [128, 4, 128], BF16)
        negm_s = sb("negm_s", [128, 4, 128], BF16)
        for q4 in range(4):
            self.ts(negm_p[:, q4, :], T_p, -1.0, ALU.add, -NEG, ALU.mult)
            self.ts(negm_s[:, q4, :], T_s, -1.0, ALU.add, -NEG, ALU.mult)
        fnw_bc = sb("fnw_bc", [128, D])
        self.load(fnw_bc[:, :], fnw.ap().rearrange("a d -> (a d)").partition_broadcast(128))

        banks = [self.es.enter_context(nc.psum_tensor("bank%d" % i, [128, 512], F32)) for i in range(8)]
        bankb = [b.bitcast(BF16) for b in banks]
        for i in range(8):
            S.bufs[bankb[i][:, :].tensor.name] = S.tbuf(banks[i][:, :])
            S.tbuf(banks[i][:, :]).excl = True

        C = dict(ident_f=ident_f, ident_b=ident_b, ones_b=ones_b, ones_f=ones_f, T_p=T_p, T_s=T_s,
                 SEG_p=SEG_p, SEG_s=SEG_s, Tb_p=Tb_p, Tb_s=Tb_s, negm_p=negm_p, negm_s=negm_s,
                 rowmask=rowmask, iota_p=iota_p, fnw_bc=fnw_bc,
                 banks=banks, bankb=bankb)
        self.C = C

        hbufs = [[S.buf("hb%d_%d" % (k, t)) for t in range(NTT)] for k in range(3)]

        h_in = [sb("h_in", [128, D])] * 2
        h_acc = None
        h_new = [sb("h_new", [128, D])] * 2
        xn = sb("xn", [128, D], BF16)
        lnT = sb("lnT", [128, 8, 128], BF16)
        sq_junk = sb("sq_junk", [128, D], BF16)
        ssq = sb("ssq", [128, 1])
        rstd = sb("rstd", [128, 1])
        self.T = dict(h_in=h_in, h_acc=h_acc, h_new=h_new, xn=xn, lnT=lnT, sq_junk=sq_junk, ssq=ssq,
                      rstd=rstd)

        src = (x_in, None)
        li_s = li_m = 0
        nL = len(self.LAYERS)
        cur = None
        for li, kind in enumerate(self.LAYERS):
            last = li == nL - 1
            if kind == "ssd":
                self.ssd_layer(li, li_s, last, locals())
                li_s += 1
            else:
                self.mla_layer(li, li_m, last, locals())
                li_m += 1
        S.finish()
        S.emit()
        return nc

    def stream_src(self, li, sweep=0):
        raise NotImplementedError

    def norm_and_transpose(self, h_tile, fold_bank):
        T, C = self.T, self.C
        self.act(T["sq_junk"][:, :], h_tile, AF.Square, accum=T["ssq"][:, :])
        self.rstd_of(T["ssq"][:, :], D, T["rstd"][:, :])
        self.ts(T["xn"][:, :], h_tile, T["rstd"][:, 0:1], ALU.mult)
        pb = C["bankb"][fold_bank]
        for kc in range(8):
            self.tr(pb[:, kc * 128:(kc + 1) * 128], T["xn"][:, kc * 128:(kc + 1) * 128], C["ident_b"][:, :])
        self.cp(T["lnT"][:, :, :], pb[:, 0:1024].rearrange("p (k t) -> p k t", k=8), eng="act")
        return T["lnT"]

    def final_out(self, hn_ap, t, V):
        T, C = self.T, self.C
        y_out = V["y_out"]
        yt = self.yfin[t % 2]
        self.act(T["sq_junk"][:, :], hn_ap, AF.Square, accum=self.ssq2[:, :])
        self.rstd_of(self.ssq2[:, :], D, self.rstd2[:, :])
        self.stt(yt[:, :], hn_ap, self.rstd2[:, 0:1], C["fnw_bc"][:, :], ALU.mult, ALU.mult)
        self.store(y_out[t * 128:(t + 1) * 128, :], yt[:, :])

    def load_weight_bf(self, dst_bf, src_ap_fn, nk, ncols, scale_col_fn, stage, eng_cycle):
        CB = stage[0].shape[1]
        i = 0
        for kc in range(nk):
            for c0 in range(0, ncols, CB):
                w = min(CB, ncols - c0)
                st = stage[i % 2]
                self.load(st[:, 0:w], src_ap_fn(kc, c0, w))
                sc = scale_col_fn(kc) if scale_col_fn is not None else None
                eng = eng_cycle[i % len(eng_cycle)]
                if sc is None:
                    self.cp(dst_bf[:, kc, c0:c0 + w], st[:, 0:w], eng=eng)
                elif eng == "act":
                    self.act(dst_bf[:, kc, c0:c0 + w], st[:, 0:w], AF.Copy, scale=sc)
                else:
                    self.ts(dst_bf[:, kc, c0:c0 + w], st[:, 0:w], sc, ALU.mult, eng=eng)
                i += 1

    def ssd_layer(self, li, j, last, V):
        nc, S, C, T = self.nc, self.S, self.C, self.T
        sb = self.sb
        NT, NTT, SEQ = self.NT, self.NT + 1, self.SEQ
        banks, bankb = C["banks"], C["bankb"]
        x_in, hb, hbufs = V["x_in"], V["hb"], V["hbufs"]
        es_layer = ExitStack()
        old_es = self.es
        self.es = es_layer

        if li == 0:
            src_t, src_b = x_in, None
        else:
            src_t, src_b = hb[(li - 1) % 2], hbufs[(li - 1) % 2]
        mid_t, mid_b = hb[2 - 2] if False else None, None
        if "hbC" not in self.dram:
            self.dint("hbC", [SEQ + 128, D])
        mid_t, mid_b = self.dram["hbC"], hbufs[2]
        dst_t, dst_b = hb[li % 2], hbufs[li % 2]

        w_in = sb("s_win", [128, 8, 3088], BF16)
        w_out = sb("s_wout", [128, 8, D], BF16)
        stage = [sb("s_stage%d" % i, [128, 1024]) for i in range(2)]
        h_acc = [sb("s_hacc%d" % i, [128, D]) for i in range(2)]
        nrm = sb("s_nrm", [128, 8])
        gnw = sb("s_gnw", [128, 8])
        cw = sb("s_cw", [128, 16, 5])
        hp_bc = sb("s_hp", [128, 3, 16])
        A_bc = sb("s_A", [128, 16])
        raw = sb("s_raw", [128, 16, 131])
        raw_s = sb("s_raws", [128, 16, 16, 11])
        cacc4 = [sb("s_cacc%d" % i, [128, 4, 128]) for i in range(2)]
        cacc = [cacc4[0][:, 0, :], cacc4[1][:, 0, :]]
        xact4 = [sb("s_xact%d" % i, [128, 512]) for i in range(2)]
        x_tok = sb("s_xtok", [128, 1024])
        xdt = sb("s_xdt", [128, 1024], BF16)
        xdd = sb("s_xdd", [128, 1024], BF16)
        BT = sb("s_BT", [128, 4, 128], BF16)
        CT = sb("s_CT", [128, 4, 128], BF16)
        Btok = sb("s_Btok", [128, 4, 128], BF16)
        dtx = sb("s_dtx", [128, 16])
        dtt = sb("s_dtt", [128, 16])
        dt = sb("s_dt", [128, 16])
        a_t = sb("s_a", [128, 16])
        a_hi = sb("s_ahi", [128, 16], BF16)
        a_lo = sb("s_alo", [128, 16], BF16)
        nacs = sb("s_nacs", [128, 16])
        dec_in = sb("s_decin", [128, 16])
        dec_end = sb("s_decend", [128, 16])
        cdec = sb("s_cdec", [128, 16])
        rhs_hi = sb("s_rhshi", [128, 16, 128], BF16)
        rhs_lo = sb("s_rhslo", [128, 16, 128], BF16)
        LT = [sb("s_LT%d" % i, [128, 128]) for i in range(4)]
        GT = [sb("s_GT%d" % i, [128, 128], BF16) for i in range(4)]
        CBm_2 = [sb("s_CBm%d" % i, [128, 128]) for i in range(2)]
        yo_sb_2 = [sb("s_yo%d" % i, [128, 256]) for i in range(2)]
        y_g_2 = [sb("s_yg%d" % i, [128, 256]) for i in range(2)]
        yt_2 = [sb("s_yt%d" % i, [128, 256]) for i in range(2)]
        sz_2 = [sb("s_sz%d" % i, [128, 256], BF16) for i in range(2)]
        sz_all = sb("s_szall", [128, 1024])
        yn_2 = [sb("s_yn%d" % i, [128, 256], BF16) for i in range(2)]
        ynT_2 = [sb("s_ynT%d" % i, [128, 2, 128], BF16) for i in range(2)]
        gss_2 = [sb("s_gss%d" % i, [128, 1]) for i in range(2)]
        grs_2 = [sb("s_grs%d" % i, [128, 1]) for i in range(2)]
        ST = sb("s_ST", [128, 1024])
        ST_bf = sb("s_STbf", [128, 1024], BF16)
        sold = [sb("s_sold%d" % i, [128, 128]) for i in range(8)]
        snew = [sb("s_snew%d" % i, [128, 128]) for i in range(8)]
        soT = [sb("s_soT%d" % i, [128, 256], BF16) for i in range(4)]
        CTm = [sb("s_CTm%d" % i, [128, 128], BF16) for i in range(4)]
        xdm = [sb("s_xdm%d" % i, [128, 256], BF16) for i in range(4)]
        a_bc = sb("s_abc", [128, 1024])
        cd_s = sb("s_cds", [128, 8, 16])
        cst48 = sb("s_cst48", [48, 2048])
        cout = cst48
        colmask_b = sb("s_colmask", [128, 16, 128], BF16)
        C["colmask_b"] = colmask_b
        for hh in range(2):
            self.load(stage[hh][:, :], V["cst"][:, 785 + hh * 1024:785 + (hh + 1) * 1024])
            self.cp(colmask_b[:, hh * 8:(hh + 1) * 8, :], stage[hh][:, :].rearrange("p (b t) -> p b t", b=8))
        if last:
            self.yfin = [sb("s_yfin", [128, D])] * 2
            self.ssq2 = sb("s_ssq2", [128, 1])
            self.rstd2 = sb("s_rstd2", [128, 1])
        h_in, h_new = T["h_in"], T["h_new"]

        ssd_win, ssd_wout = V["ssd_win"], V["ssd_wout"]
        for sw in range(2):
            ls = j * 2 + sw
            self.load(nrm[:, :], V["ssd_nrm"][j, :, :])
            self.load(gnw[:, :], V["ssd_gnw"][ls, :, :])
            self.load(cw[:, :, :], V["ssd_cw"][ls, :, :, :])
            self.load(hp_bc[:, :, :].rearrange("p a b -> p (a b)"),
                      V["ssd_hp"][ls, :, :].rearrange("a b -> (a b)").partition_broadcast(128))
            self.act(A_bc[:, :], hp_bc[:, 1, :], AF.Exp)
            self.ts(A_bc[:, :], A_bc[:, :], -1.0, ALU.mult)
            self.load_weight_bf(w_in, lambda kc, c0, w: ssd_win[ls, kc * 128:(kc + 1) * 128, c0:c0 + w],
                                8, 3088, lambda kc: nrm[:, kc:kc + 1], stage, ["dve", "act"])
            self.load_weight_bf(w_out, lambda kc, c0, w: ssd_wout[ls, kc * 128:(kc + 1) * 128, c0:c0 + w],
                                8, D, lambda kc: gnw[:, kc:kc + 1], stage, ["dve", "act"])
            self.memset(raw[:, :, 0:3], 0.0)
            self.memset(ST[:, :], 0.0)
            self.memset(ST_bf[:, :], 0.0)

            def load_tile(t):
                r0 = t * 128
                self.load(h_in[t % 2][:, :], src_t[r0:r0 + 128, :],
                          rd=([src_b[t]] if src_b is not None else []))
                if sw == 1:
                    self.load(h_acc[t % 2][:, :], mid_t[r0:r0 + 128, :], rd=[mid_b[t]])

            for t in range(NTT):
                samp = t == NT
                load_tile(t)
                hin = h_in[t % 2]
                hacc = hin if sw == 0 else h_acc[t % 2]
                lnT = self.norm_and_transpose(hin[:, :], 0)

                pdt = banks[2]
                for kc in range(8):
                    self.mm(pdt[:, 0:16], lnT[:, kc, :], w_in[:, kc, 3072:3088], start=kc == 0, stop=kc == 7)
                self.tt(dtx[:, :], pdt[:, 0:16], hp_bc[:, 0, :], ALU.add)
                self.act(dtt[:, :], dtx[:, :], AF.Abs)
                self.act(dtt[:, :], dtt[:, :], AF.Exp, scale=-1.0)
                self.act(dtt[:, :], dtt[:, :], AF.Ln, bias=1.0)
                self.stt(dt[:, :], dtx[:, :], 0.0, dtt[:, :], ALU.max, ALU.add)
                self.tt(a_t[:, :], dt[:, :], A_bc[:, :], ALU.mult)
                if samp:
                    stc = V["st_conv"]
                    self.load(cst48[:, 0:1024], stc[j, :, sw * 1024:(sw + 1) * 1024])
                    self.load(cst48[:, 1024:1536], stc[j, :, 2048 + sw * 512:2048 + (sw + 1) * 512])
                    self.load(cst48[:, 1536:2048], stc[j, :, 3072 + sw * 512:3072 + (sw + 1) * 512])
                    for cc in range(16):
                        pb = banks[1]
                        self.tr(pb[:, 0:48], cst48[:, cc * 128:(cc + 1) * 128], C["ident_f"][0:48, 0:48])
                        self.cp(raw_s[:, cc, :, 0:3], pb[:, 0:48].rearrange("p (b k) -> p b k", b=16), eng="act")
                for grp in range(4):
                    pb = banks[1 + grp % 2]
                    ca4 = cacc4[grp % 2]
                    for c4 in range(4):
                        cc = grp * 4 + c4
                        col0 = 1024 + cc * 128
                        for kc in range(8):
                            self.mm(pb[:, c4 * 128:(c4 + 1) * 128], w_in[:, kc, col0:col0 + 128], lnT[:, kc, :],
                                    start=kc == 0, stop=kc == 7)
                    if not samp:
                        self.cp(raw[:, grp * 4:(grp + 1) * 4, 3:131], pb[:, :].rearrange("p (c t) -> p c t", c=4),
                                eng="act")
                        for k in range(4):
                            for c4 in range(4):
                                cc = grp * 4 + c4
                                if k == 0:
                                    self.ts(ca4[:, c4, :], raw[:, cc, 0:128], cw[:, cc, 0:1], ALU.mult,
                                            cw[:, cc, 4:5], ALU.add)
                                else:
                                    self.stt(ca4[:, c4, :], raw[:, cc, k:k + 128], cw[:, cc, k:k + 1], ca4[:, c4, :],
                                             ALU.mult, ALU.add)
                    else:
                        for c4 in range(4):
                            cc = grp * 4 + c4
                            self.cp(raw_s[:, cc, :, 3:11],
                                    pb[:, c4 * 128:(c4 + 1) * 128].rearrange("p (b k) -> p b k", b=16), eng="act")
                        for k in range(4):
                            for c4 in range(4):
                                cc = grp * 4 + c4
                                ca3 = ca4[:, c4, :].rearrange("p (b k) -> p b k", b=16)
                                if k == 0:
                                    self.ts(ca3, raw_s[:, cc, :, 0:8], cw[:, cc, 0:1], ALU.mult, cw[:, cc, 4:5], ALU.add)
                                else:
                                    self.stt(ca3, raw_s[:, cc, :, k:k + 8], cw[:, cc, k:k + 1], ca3, ALU.mult, ALU.add)
                    ca_flat = ca4[:, :, :].rearrange("p c t -> p (c t)")
                    if grp < 2:
                        xa4 = xact4[grp % 2]
                        self.act(xa4[:, :], ca_flat, AF.Silu)
                        pt_ = banks[0]
                        for c4 in range(4):
                            self.tr(pt_[:, c4 * 128:(c4 + 1) * 128], xa4[:, c4 * 128:(c4 + 1) * 128], C["ident_f"])
                        self.cp(x_tok[:, grp * 512:(grp + 1) * 512], pt_[:, :], eng="act")
                    elif grp == 2:
                        self.act(BT[:, :, :].rearrange("p g t -> p (g t)"), ca_flat, AF.Silu)
                        ptb = bankb[0]
                        for g in range(4):
                            self.tr(ptb[:, g * 128:(g + 1) * 128], BT[:, g, :], C["ident_b"][:, :])
                        self.cp(Btok[:, :, :].rearrange("p g t -> p (g t)"), ptb[:, 0:512], eng="act")
                    else:
                        self.act(CT[:, :, :].rearrange("p g t -> p (g t)"), ca_flat, AF.Silu)
                for half in range(2):
                    pz = banks[1 + half]
                    for kc in range(8):
                        self.mm(pz[:, :], lnT[:, kc, :], w_in[:, kc, half * 512:(half + 1) * 512],
                                start=kc == 0, stop=kc == 7)
                    self.act(sz_all[:, half * 512:(half + 1) * 512], pz[:, :], AF.Silu)
                if samp or t == NT - 1:
                    ncol = 48 if samp else 3
                    for cc in range(16):
                        pb = banks[1]
                        if samp:
                            src_ap = raw_s[:, cc, :, 8:11]
                            stg = cacc[cc % 2]
                            self.cp(stg[:, 0:48].rearrange("p (b k) -> p b k", b=16), src_ap)
                            self.tr(pb[0:48, 0:128], stg[:, 0:48], C["ident_f"])
                        else:
                            self.tr(pb[0:3, 0:128], raw[:, cc, 128:131], C["ident_f"])
                        self.cp(cout[0:ncol, cc * 128:(cc + 1) * 128], pb[0:ncol, 0:128], eng="act")
                    oc = V["o_sconv"] if samp else V["o_pconv"]
                    self.store(oc[j, 0:ncol, sw * 1024:(sw + 1) * 1024], cout[0:ncol, 0:1024])
                    self.store(oc[j, 0:ncol, 2048 + sw * 512:2048 + (sw + 1) * 512], cout[0:ncol, 1024:1536])
                    self.store(oc[j, 0:ncol, 3072 + sw * 512:3072 + (sw + 1) * 512], cout[0:ncol, 1536:2048])
                if not samp:
                    self.cp(raw[:, :, 0:3], raw[:, :, 128:131], eng="pool")

                Tm = C["T_s"] if samp else C["T_p"]
                SEGm = C["SEG_s"] if samp else C["SEG_p"]
                self.mm(pdt[:, 16:32], Tm, a_t[:, :], True, True)
                self.mm(pdt[:, 32:48], SEGm, a_t[:, :], True, True)
                self.ts(nacs[:, :], pdt[:, 16:32], -1.0, ALU.mult)
                self.act(dec_in[:, :], pdt[:, 16:32], AF.Exp)
                self.act(cdec[:, :], pdt[:, 32:48], AF.Exp)
                self.tt(dec_end[:, :], pdt[:, 32:48], nacs[:, :], ALU.add)
                self.act(dec_end[:, :], dec_end[:, :], AF.Exp)
                self.cp(a_hi[:, :], a_t[:, :])
                self.tt(a_lo[:, :], a_t[:, :], a_hi[:, :], ALU.subtract)
                Tb = C["Tb_s"] if samp else C["Tb_p"]
                negm = C["negm_s"] if samp else C["negm_p"]
                self.tt(rhs_hi[:, :, :], Tb[:, :].unsqueeze(1).broadcast_to([128, 16, 128]),
                        a_hi[:, :].unsqueeze(2).broadcast_to([128, 16, 128]), ALU.mult)
                self.tt(rhs_lo[:, :, :], Tb[:, :].unsqueeze(1).broadcast_to([128, 16, 128]),
                        a_lo[:, :].unsqueeze(2).broadcast_to([128, 16, 128]), ALU.mult)

                x3 = x_tok[:, :].rearrange("p (h d) -> p h d", h=16)
                self.tt(xdt[:, :].rearrange("p (h d) -> p h d", h=16), x3,
                        dt[:, :].unsqueeze(2).broadcast_to([128, 16, 64]), ALU.mult)
                self.tt(xdd[:, :].rearrange("p (h d) -> p h d", h=16), xdt[:, :].rearrange("p (h d) -> p h d", h=16),
                        dec_end[:, :].unsqueeze(2).broadcast_to([128, 16, 64]), ALU.mult)
                if samp:
                    self.cp(a_bc[:, :].rearrange("p (h d) -> p h d", h=16),
                            a_t[:, :].unsqueeze(2).broadcast_to([128, 16, 64]))
                    pcd = banks[2]
                    for hp in range(8):
                        self.mm(pcd[:, 64 + hp * 16:64 + (hp + 1) * 16], a_bc[:, hp * 128:(hp + 1) * 128],
                                C["rowmask"], True, True)
                    self.act(cd_s[:, :, :].rearrange("p a b -> p (a b)"), pcd[:, 64:192], AF.Exp)

                pout = (banks[6], banks[7])
                def group_gen(g, samp=samp, Tm=Tm, negm=negm):
                    p_ = g % 2
                    hs = g * 4
                    pL = banks[3] if p_ == 0 else banks[0]
                    self.mm(pL[:, :], C["ones_b"][:, :], rhs_hi[:, hs:hs + 4, :], True, False)
                    yield
                    self.mm(pL[:, :], C["ones_b"][:, :], rhs_lo[:, hs:hs + 4, :], False, False)
                    yield
                    self.mm(pL[:, :], C["ident_b"][:, :], negm[:, :, :], False, True)
                    yield
                    pC = banks[4] if p_ == 0 else banks[1]
                    self.mm(pC[:, 0:128], BT[:, g, :], CT[:, g, :], True, True)
                    yield
                    self.tt(CBm_2[p_][:, :], pC[:, 0:128], Tm, ALU.mult)
                    yield
                    for hh in range(4):
                        h = hs + hh
                        lt = LT[p_ * 2 + h % 2]
                        gt = GT[p_ * 2 + h % 2]
                        self.act(lt[:, :], pL[:, hh * 128:(hh + 1) * 128], AF.Exp, bias=nacs[:, h:h + 1])
                        yield
                        self.tt(gt[:, :], lt[:, :], CBm_2[p_][:, :], ALU.mult)
                        yield
                        self.mm(pC[:, 128 + hh * 64:128 + (hh + 1) * 64], gt[:, :], xdt[:, h * 64:(h + 1) * 64],
                                True, True)
                        yield
                    pY = banks[5] if p_ == 0 else banks[2]
                    if not samp:
                        self.mm(pY[:, 0:256], CT[:, g, :], ST_bf[:, g * 256:(g + 1) * 256], True, True)
                        yield
                        self.mm(pY[:, 256:512], Btok[:, g, :], xdd[:, g * 256:(g + 1) * 256], True, True)
                        yield
                        self.cp(yo_sb_2[p_][:, :], pY[:, 0:256], eng="act")
                        yield
                        stg_ = ST[:, g * 256:(g + 1) * 256]
                        self.tt(stg_.rearrange("p (h d) -> p h d", h=4), stg_.rearrange("p (h d) -> p h d", h=4),
                                cdec[:, hs:hs + 4].unsqueeze(2).broadcast_to([128, 4, 64]), ALU.mult)
                        yield
                        self.tt(stg_, stg_, pY[:, 256:512], ALU.add)
                        yield
                        self.cp(ST_bf[:, g * 256:(g + 1) * 256], stg_, eng="pool")
                        yield
                    else:
                        sst, osst = V["st_ssm"], V["o_sssm"]
                        for b in range(16):
                            som = soT[p_ * 2 + b % 2]
                            for hp2 in range(2):
                                hp = g * 2 + hp2
                                so = sold[p_ * 4 + (b * 2 + hp2) % 4]
                                r0 = (sw * 8 + hp) * 128
                                self.load(so[:, :], sst[j, b, r0:r0 + 128, :])
                                yield
                                ptr = pL
                                self.tr(ptr[:, 128:256], so[:, :], C["ident_f"])
                                yield
                                self.cp(som[:, hp2 * 128:(hp2 + 1) * 128], ptr[:, 128:256], eng="act")
                                yield
                            ctm = CTm[p_ * 2 + b % 2]
                            self.tt(ctm[:, :], CT[:, g, :], C["colmask_b"][:, b, :], ALU.mult, eng="pool")
                            yield
                            self.mm(pY[:, 0:256], ctm[:, :], som[:, :], b == 0, b == 15)
                            yield
                            xm = xdm[p_ * 2 + b % 2]
                            self.ts(xm[:, :], xdd[:, g * 256:(g + 1) * 256], C["rowmask"][:, b:b + 1], ALU.mult)
                            yield
                            for hp2 in range(2):
                                hp = g * 2 + hp2
                                so = sold[p_ * 4 + (b * 2 + hp2) % 4]
                                sn = snew[p_ * 4 + (b * 2 + hp2) % 4]
                                r0 = (sw * 8 + hp) * 128
                                pS = pL
                                self.mm(pS[:, 256 + hp2 * 128:256 + (hp2 + 1) * 128], xm[:, hp2 * 128:(hp2 + 1) * 128],
                                        Btok[:, g, :], True, True)
                                yield
                                self.stt(sn[:, :], so[:, :], cd_s[:, hp, b:b + 1],
                                         pS[:, 256 + hp2 * 128:256 + (hp2 + 1) * 128], ALU.mult, ALU.add)
                                yield
                                self.store(osst[j, b, r0:r0 + 128, :], sn[:, :])
                                yield
                        self.cp(yo_sb_2[p_][:, :], pY[:, 0:256], eng="act")
                        yield
                    y3 = y_g_2[p_][:, :].rearrange("p (h d) -> p h d", h=4)
                    self.tt(y3, yo_sb_2[p_][:, :].rearrange("p (h d) -> p h d", h=4),
                            dec_in[:, hs:hs + 4].unsqueeze(2).broadcast_to([128, 4, 64]), ALU.mult)
                    yield
                    self.tt(y_g_2[p_][:, :], y_g_2[p_][:, :], pC[:, 128:384], ALU.add)
                    yield
                    self.tt(yt_2[p_][:, :].rearrange("p (h d) -> p h d", h=4),
                            x_tok[:, hs * 64:(hs + 4) * 64].rearrange("p (h d) -> p h d", h=4),
                            hp_bc[:, 2, hs:hs + 4].unsqueeze(2).broadcast_to([128, 4, 64]), ALU.mult)
                    yield
                    self.tt(y_g_2[p_][:, :], y_g_2[p_][:, :], yt_2[p_][:, :], ALU.add)
                    yield
                    self.tt(y_g_2[p_][:, :], y_g_2[p_][:, :], sz_all[:, g * 256:(g + 1) * 256], ALU.mult)
                    yield
                    self.act(sz_2[p_][:, :], y_g_2[p_][:, :], AF.Square, accum=gss_2[p_][:, :])
                    yield
                    self.rstd_of(gss_2[p_][:, :], 256, grs_2[p_][:, :])
                    yield
                    self.ts(yn_2[p_][:, :], y_g_2[p_][:, :], grs_2[p_][:, 0:1], ALU.mult)
                    yield
                    pt2 = bankb[3] if p_ == 0 else bankb[0]
                    for c2 in range(2):
                        self.tr(pt2[:, 512 + c2 * 128:512 + (c2 + 1) * 128], yn_2[p_][:, c2 * 128:(c2 + 1) * 128],
                                C["ident_b"][:, :])
                        yield
                    self.cp(ynT_2[p_][:, :, :], pt2[:, 512:768].rearrange("p (c t) -> p c t", c=2), eng="act")
                    yield
                    for c2 in range(2):
                        ec = g * 2 + c2
                        for half in range(2):
                            self.mm(pout[half][:, :], ynT_2[p_][:, c2, :], w_out[:, ec, half * 512:(half + 1) * 512],
                                    start=(ec == 0), stop=(ec == 7))
                            yield

                def _interleave(*gens):
                    gens = list(gens)
                    while gens:
                        for g_ in list(gens):
                            try:
                                next(g_)
                            except StopIteration:
                                gens.remove(g_)

                _interleave(group_gen(0), group_gen(1))
                _interleave(group_gen(2), group_gen(3))
                hn = h_new[t % 2]
                for half in range(2):
                    self.tt(hn[:, half * 512:(half + 1) * 512], hacc[:, half * 512:(half + 1) * 512],
                            pout[half][:, :], ALU.add)
                r0 = t * 128
                if sw == 0:
                    self.store(mid_t[r0:r0 + 128, :], hn[:, :], wr=[mid_b[t]])
                elif not last:
                    self.store(dst_t[r0:r0 + 128, :], hn[:, :], wr=[dst_b[t]])
                else:
                    self.final_out(hn[:, :], t, V)
            for hp in range(8):
                ptr = banks[0]
                self.tr(ptr[:, 256:384], ST[:, hp * 128:(hp + 1) * 128], C["ident_f"])
                sn = snew[hp % 4]
                self.cp(sn[:, :], ptr[:, 256:384], eng="act")
                r0 = (sw * 8 + hp) * 128
                self.store(V["o_pssm"][j, r0:r0 + 128, :], sn[:, :])
        self.es = old_es
        self._close_layer(es_layer)

    def _close_layer(self, es_layer):
        self.barrier()
        es_layer.close()

    def barrier(self):
        S = self.S
        tags = []
        for e in ("pe", "act", "dve", "pool"):
            if S.count[e] > 0:
                tags.append((e, S.sem[e], S.count[e]))
        for b in S.dma_bufs:
            if b.dlast is not None:
                tags.append(b.dlast)
        for e in S.ENG:
            waits = []
            for (sk, sem, v) in tags:
                if S.waited[e].get(sk, 0) < v:
                    S.waited[e][sk] = v
                    waits.append((sem, v))
            if waits:
                S.ops[e].append((waits, None, None, 0))

    def mla_layer(self, li, j, last, V):
        nc, S, C, T = self.nc, self.S, self.C, self.T
        sb = self.sb
        NT, NTT, SEQ, NPG = self.NT, self.NT + 1, self.SEQ, self.NPG
        banks, bankb = C["banks"], C["bankb"]
        x_in, hb, hbufs = V["x_in"], V["hb"], V["hbufs"]
        es_layer = ExitStack()
        old_es = self.es
        self.es = es_layer
        SCALE = float((64 + 32) ** -0.5)
        if li == 0:
            src_t, src_b = x_in, None
        else:
            src_t, src_b = hb[(li - 1) % 2], hbufs[(li - 1) % 2]
        dst_t, dst_b = hb[li % 2], hbufs[li % 2]

        w_in = sb("m_win", [128, 8, 1856], BF16)
        w_uq = sb("m_wuq", [128, 4, 2048], BF16)
        w_ukz = sb("m_wukz", [128, 16, 256], BF16)
        w_uv = sb("m_wuv", [128, 2, 1024], BF16)
        w_out = sb("m_wout", [128, 8, D], BF16)
        stage = [sb("m_stage%d" % i, [128, 512]) for i in range(2)]
        nrm = sb("m_nrm", [128, 8])
        qnw = sb("m_qnw", [128, 4])
        kvw_bc = sb("m_kvw", [128, 256])
        ckvT = sb("m_ckvT", [128, 2, SEQ], BF16)
        ckv_tok = sb("m_ckvtok", [128, NT, 256], BF16)
        kpeT = sb("m_kpeT", [32, SEQ], BF16)
        ckvT_s = sb("m_ckvTs", [128, 2, 128], BF16)
        ckv_s = sb("m_ckvs", [128, 256], BF16)
        kpeT_s = sb("m_kpeTs", [32, 128], BF16)
        self._ptc = 0
        PTq = [sb("m_PTq%d" % i, [128, 512], BF16) for i in range(3)]
        q_nopeT = sb("m_qnT", [128, 8, 128], BF16)
        q_latT = sb("m_qlT", [128, 2, 16, 128], BF16)
        q_peT = sb("m_qpT", [32, 16, 128], BF16)
        accT = sb("m_accT", [128, 2, 16, 128], BF16)
        cqn = sb("m_cqn", [128, 512], BF16)
        cqT = sb("m_cqT", [128, 4, 128], BF16)
        ckv_f = sb("m_ckvf", [128, 256])
        kpe_f = sb("m_kpef", [128, 32])
        kpe_t = sb("m_kpet", [128, 32])
        kpe_b = sb("m_kpeb", [128, 32], BF16)
        rtok = [sb("m_rtok%d" % i, [128, 64]) for i in range(2)]
        rT = [sb("m_rT%d" % i, [32, 2, 128]) for i in range(2)]
        qp1 = sb("m_qp1", [128, 512])
        sg = sb("m_sg", [128, D])
        og = sb("m_og", [128, D], BF16)
        oT = sb("m_oT", [128, 8, 128], BF16)
        l_tok = sb("m_ltok", [128, 16])
        rl = sb("m_rl", [128, 16])
        ss2 = sb("m_ss2", [128, 1])
        rs2 = sb("m_rs2", [128, 1])
        ss3 = sb("m_ss3", [128, 1])
        rs3 = sb("m_rs3", [128, 1])
        amask_b = sb("m_amask", [128, 16, 128], BF16)
        idx = sb("m_idx", [128, 16 * NPG], I32)
        pg_c = [sb("m_pgc%d" % i, [128, 288]) for i in range(4)]
        pg_b = [sb("m_pgb%d" % i, [128, 288], BF16) for i in range(4)]
        pgT = [sb("m_pgT%d" % i, [128, 2, 128], BF16) for i in range(4)]
        pkT = [sb("m_pkT%d" % i, [32, 128], BF16) for i in range(4)]
        PTs = [sb("m_PTs%d" % i, [128, 128], BF16) for i in range(4)]
        rlb = sb("m_rlb", [128, 128])
        if last:
            self.yfin = [sb("m_yfin", [128, D])] * 2
            self.ssq2 = sb("m_ssq2", [128, 1])
            self.rstd2 = sb("m_rstd2", [128, 1])
        h_in, h_new = T["h_in"], T["h_new"]

        self.load(nrm[:, :], V["mla_nrm"][j, :, :])
        self.load(qnw[:, :], V["mla_qnw"][j, :, :])
        self.load(kvw_bc[:, :], V["mla_kvw"][j, :, :].rearrange("a d -> (a d)").partition_broadcast(128))
        for hh in range(4):
            self.load(stage[hh % 2][:, :], V["cst"][:, 785 + 2048 + hh * 512:785 + 2048 + (hh + 1) * 512])
            self.cp(amask_b[:, hh * 4:(hh + 1) * 4, :], stage[hh % 2][:, :].rearrange("p (b t) -> p b t", b=4))
        mw, wuq, wuk, wuv, wo = V["mla_win"], V["mla_wuq"], V["mla_wuk"], V["mla_wuv"], V["mla_wout"]
        self.load_weight_bf(w_in, lambda kc, c0, w: mw[j, kc * 128:(kc + 1) * 128, c0:c0 + w],
                            8, 1856, lambda kc: nrm[:, kc:kc + 1], stage, ["dve", "act"])
        self.load_weight_bf(w_uq, lambda kc, c0, w: wuq[j, kc * 128:(kc + 1) * 128, c0:c0 + w],
                            4, 2048, lambda kc: qnw[:, kc:kc + 1], stage, ["dve", "act"])
        self.load_weight_bf(w_uv, lambda kc, c0, w: wuv[j, kc * 128:(kc + 1) * 128, c0:c0 + w],
                            2, 1024, None, stage, ["dve", "act"])
        self.load_weight_bf(w_out, lambda kc, c0, w: wo[j, kc * 128:(kc + 1) * 128, c0:c0 + w],
                            8, D, None, stage, ["dve", "act"])
        self.memset(w_ukz[:, :, :].rearrange("p h c -> p (h c)"), 0.0)
        wz4 = w_ukz[:, :, :].rearrange("p (a two) c -> p a two c", two=2)
        for ci, c0 in enumerate(range(0, 2048, 512)):
            st = stage[ci % 2]
            self.load(st[:, :], wuk[j, :, :, :].rearrange("p a c -> p (a c)")[:, c0:c0 + 512])
            self.cp(wz4[0:64, ci * 2:(ci + 1) * 2, 0, :], st[0:64, :].rearrange("p (a c) -> p a c", a=2))
            self.cp(wz4[64:128, ci * 2:(ci + 1) * 2, 1, :], st[64:128, :].rearrange("p (a c) -> p a c", a=2))
        NI = 16 * NPG
        import os as _os
        _dbg = _os.environ.get("K_DBG", "")
        for c0 in range(0, 0 if "noidx" in _dbg else NI, 512):
            w = min(512, NI - c0)
            st = stage[(c0 // 512) % 2]
            sti = st[:, 0:w].bitcast(I32)
            self.load(sti, V["ptab"][:, c0:c0 + w].rearrange("a n -> (a n)").partition_broadcast(128))
            self.cp(idx[:, c0:c0 + w].bitcast(F32), sti)
            self.ts(idx[:, c0:c0 + w].bitcast(F32), idx[:, c0:c0 + w].bitcast(F32), 128.0, ALU.mult,
                    C["iota_p"], ALU.add)
            if j > 0:
                self.ts(idx[:, c0:c0 + w].bitcast(F32), idx[:, c0:c0 + w].bitcast(F32),
                        float(j * self.NPOOL * 128), ALU.add)
            self.cp(idx[:, c0:c0 + w], idx[:, c0:c0 + w].bitcast(F32))
        pool_c = V["pool_ckv"]
        o_ckv, o_kpe = V["o_pckv"], V["o_pkpe"]
        rope_tok, rope_T = V["rope_tok"], V["rope_T"]

        def load_tile(t):
            r0 = t * 128
            self.load(h_in[t % 2][:, :], src_t[r0:r0 + 128, :], rd=([src_b[t]] if src_b is not None else []))
            self.load(rtok[t % 2][:, :], rope_tok[r0:r0 + 128, :])
            self.load(rT[t % 2][:, :, :], rope_T[:, :, r0:r0 + 128])

        _skip = _os.environ.get("K_SKIP", "")
        for t in range(NTT):
            samp = t == NT
            load_tile(t)
            hin = h_in[t % 2]
            r0 = t * 128
            lnT = self.norm_and_transpose(hin[:, :], 0)
            pq, pk, pg0, pg1 = banks[1], banks[2], banks[3], banks[4]
            for kc in range(8):
                self.mm(pq[:, :], lnT[:, kc, :], w_in[:, kc, 0:512], start=kc == 0, stop=kc == 7)
            for kc in range(8):
                self.mm(pk[:, 0:320], lnT[:, kc, :], w_in[:, kc, 512:832], start=kc == 0, stop=kc == 7)
            for kc in range(8):
                self.mm(pg0[:, :], lnT[:, kc, :], w_in[:, kc, 832:1344], start=kc == 0, stop=kc == 7)
            for kc in range(8):
                self.mm(pg1[:, :], lnT[:, kc, :], w_in[:, kc, 1344:1856], start=kc == 0, stop=kc == 7)
            if "A" not in _skip:
                self.act(T["sq_junk"][:, 0:512], pq[:, :], AF.Square, accum=ss2[:, :])
                self.rstd_of(ss2[:, :], 512, rs2[:, :])
                self.ts(cqn[:, :], pq[:, :], rs2[:, 0:1], ALU.mult)
                ptb = bankb[0]
                for kc in range(4):
                    self.tr(ptb[:, kc * 128:(kc + 1) * 128], cqn[:, kc * 128:(kc + 1) * 128], C["ident_b"][:, :])
                self.cp(cqT[:, :, :], ptb[:, 0:512].rearrange("p (k t) -> p k t", k=4), eng="act")
            if "B" not in _skip:
                self.act(T["sq_junk"][:, 0:256], pk[:, 0:256], AF.Square, accum=ss3[:, :])
                self.rstd_of(ss3[:, :], 256, rs3[:, :])
                self.stt(ckv_f[:, :], pk[:, 0:256], rs3[:, 0:1], kvw_bc[:, :], ALU.mult, ALU.mult)
                self.store(o_ckv[j, r0:r0 + 128, :], ckv_f[:, :])
                self.tt(kpe_f[:, :], pk[:, 256:288], rtok[t % 2][:, 0:32], ALU.mult)
                self.tt(kpe_t[:, :], pk[:, 288:320], rtok[t % 2][:, 32:64], ALU.mult)
                self.tt(kpe_f[:, :], kpe_f[:, :], kpe_t[:, :], ALU.add)
                self.store(o_kpe[j, r0:r0 + 128, :], kpe_f[:, :])
                self.cp(kpe_b[:, :], kpe_f[:, :])
                kv_dst = ckv_s[:, :] if samp else ckv_tok[:, t, :]
                self.cp(kv_dst, ckv_f[:, :], eng="pool")
                for cc in range(2):
                    self.tr(ptb[:, 512 + cc * 128:512 + (cc + 1) * 128], kv_dst[:, cc * 128:(cc + 1) * 128]
                            if False else (ckv_s[:, cc * 128:(cc + 1) * 128] if samp else ckv_tok[:, t, cc * 128:(cc + 1) * 128]),
                            C["ident_b"][:, :])
                kT_dst = ckvT_s[:, :, :] if samp else ckvT[:, :, r0:r0 + 128]
                self.cp(kT_dst, ptb[:, 512:768].rearrange("p (c t) -> p c t", c=2), eng="act")
                self.tr(ptb[0:32, 768:896], kpe_b[:, :], C["ident_b"][:, :])
                kp_dst = kpeT_s[:, :] if samp else kpeT[:, r0:r0 + 128]
                self.cp(kp_dst, ptb[0:32, 768:896], eng="act")
            if "C" not in _skip:
                self.act(sg[:, 0:512], pg0[:, :], AF.Silu)
                self.act(sg[:, 512:1024], pg1[:, :], AF.Silu)
            if "D" not in _skip:
                for hq in range(2):
                    pb = banks[1 + hq]
                    for h4 in range(4):
                        hp = hq * 4 + h4
                        for kc in range(4):
                            self.mm(pb[:, h4 * 128:(h4 + 1) * 128], w_uq[:, kc, hp * 128:(hp + 1) * 128], cqT[:, kc, :],
                                    start=kc == 0, stop=kc == 3)
                    self.cp(q_nopeT[:, hq * 4:(hq + 1) * 4, :].rearrange("p h t -> p (h t)"), pb[:, :], eng="act")
            if "E" not in _skip:
                ppe, ppes = banks[2], banks[3]
                for kc in range(4):
                    self.mm(ppe[:, :], cqT[:, kc, :], w_uq[:, kc, 1024:1536], start=kc == 0, stop=kc == 3)
                for kc in range(4):
                    self.mm(ppes[:, :], cqT[:, kc, :], w_uq[:, kc, 1536:2048], start=kc == 0, stop=kc == 3)
                cosb = rtok[t % 2][:, 0:32].unsqueeze(1).broadcast_to([128, 16, 32])
                sinb = rtok[t % 2][:, 32:64].unsqueeze(1).broadcast_to([128, 16, 32])
                self.tt(qp1[:, :].rearrange("p (h r) -> p h r", h=16), ppe[:, :].rearrange("p (h r) -> p h r", h=16),
                        cosb, ALU.mult)
                self.tt(ppes[:, :].rearrange("p (h r) -> p h r", h=16), ppes[:, :].rearrange("p (h r) -> p h r", h=16),
                        sinb, ALU.mult)
                self.tt(cqn[:, :], qp1[:, :], ppes[:, :], ALU.add)
                for hf in range(2):
                    pbq = bankb[hf]
                    for hh in range(8):
                        h = hf * 8 + hh
                        self.tr(pbq[0:32, hh * 128:(hh + 1) * 128], cqn[:, h * 32:(h + 1) * 32], C["ident_b"][:, :])
                    self.cp(q_peT[:, hf * 8:(hf + 1) * 8, :].rearrange("p h t -> p (h t)"), pbq[0:32, 0:1024], eng="act")
            if "F" not in _skip:
                for q4 in range(4):
                    for cc in range(2):
                        pb = banks[1 + (q4 * 2 + cc) % 2]
                        for hh in range(4):
                            h = q4 * 4 + hh
                            hp = h // 2
                            self.mm(pb[:, hh * 128:(hh + 1) * 128], w_ukz[:, h, cc * 128:(cc + 1) * 128],
                                    q_nopeT[:, hp, :], True, True)
                        self.cp(q_latT[:, cc, q4 * 4:(q4 + 1) * 4, :].rearrange("p h t -> p (h t)"), pb[:, :], eng="act")
            if not samp:
                nk = t + 1
                for q4 in range(4):
                    h0 = q4 * 4
                    pacc = (banks[5], banks[6]) if q4 % 2 == 0 else (banks[1], banks[2])
                    pl = banks[7]

                    def scores(jk, h0=h0):
                        ps = banks[3 + jk % 2]
                        k0 = jk * 128
                        self.mm(ps[:, :], ckvT[:, 0, k0:k0 + 128], q_latT[:, 0, h0:h0 + 4, :], True, False)
                        self.mm(ps[:, :], ckvT[:, 1, k0:k0 + 128], q_latT[:, 1, h0:h0 + 4, :], False, False)
                        self.mm(ps[:, :], kpeT[:, k0:k0 + 128], q_peT[:, h0:h0 + 4, :], False, True)

                    scores(0)
                    for jk in range(nk):
                        if jk + 1 < nk:
                            scores(jk + 1)
                        pt = PTq[self._ptc % 3]
                        self._ptc += 1
                        self.act(pt[:, :], banks[3 + jk % 2][:, :], AF.Exp, scale=SCALE)
                        if jk == t:
                            self.tt(pt[:, :].rearrange("p (h t) -> p h t", h=4), pt[:, :].rearrange("p (h t) -> p h t", h=4),
                                    C["Tb_p"][:, :].unsqueeze(1).broadcast_to([128, 4, 128]), ALU.mult)
                        for cc in range(2):
                            self.mm(pacc[cc][:, :], ckv_tok[:, jk, cc * 128:(cc + 1) * 128], pt[:, :],
                                    jk == 0, jk == nk - 1)
                        for hh in range(4):
                            self.mm(pl[:, h0 + hh:h0 + hh + 1], pt[:, hh * 128:(hh + 1) * 128], C["ones_b"][:, 0:1],
                                    jk == 0 and hh == 0, jk == nk - 1 and hh == 3)
                    for cc in range(2):
                        self.cp(accT[:, cc, h0:h0 + 4, :].rearrange("p h t -> p (h t)"), pacc[cc][:, :], eng="act")
                self.cp(l_tok[:, :], banks[7][:, 0:16])
            else:
                self.memset(l_tok[:, :], 1.0)
                steps = [(b, pgi) for b in range(16) for pgi in range(NPG)]
                NST = len(steps)

                def gather(i):
                    b_, pg_ = steps[i]
                    col = b_ * NPG + pg_
                    dst = pg_c[i % 4]
                    self.S.dma("pool", (lambda e, o_=dst[:, :], i_=pool_c[:, :], x_=idx[:, col:col + 1]:
                                        e.indirect_dma_start(out=o_, out_offset=None, in_=i_,
                                                             in_offset=bass.IndirectOffsetOnAxis(ap=x_, axis=0))),
                               S.tbuf(dst[:, :]), reads=[S.tbuf(idx[:, :])], writes=[S.tbuf(dst[:, :])])

                for i in range(min(3, NST)):
                    gather(i)
                pacc = (banks[5], banks[6])
                pl = banks[7]
                si = 0
                for b in range(16):
                    tsl = slice(b * 8, b * 8 + 8)
                    for pgi in range(NPG + 1):
                        new = pgi == NPG
                        if not new:
                            if si + 3 < NST:
                                gather(si + 3)
                            s4 = si % 4
                            pgf = pg_c[s4]
                            self.cp(pg_b[s4][:, :], pgf[:, :])
                            ptf = bankb[1 + si % 2]
                            for cc in range(2):
                                self.tr(ptf[:, cc * 128:(cc + 1) * 128], pg_b[s4][:, cc * 128:(cc + 1) * 128],
                                        C["ident_b"][:, :])
                            self.tr(ptf[0:32, 256:384], pg_b[s4][:, 256:288], C["ident_b"][:, :])
                            self.cp(pgT[s4][:, :, :], ptf[:, 0:256].rearrange("p (c t) -> p c t", c=2), eng="act")
                            self.cp(pkT[s4][:, :], ptf[0:32, 256:384], eng="act")
                            kT0, kT1, kP, kV = pgT[s4][:, 0, :], pgT[s4][:, 1, :], pkT[s4][:, :], pg_b[s4]
                            si += 1
                        else:
                            kT0, kT1, kP, kV = ckvT_s[:, 0, :], ckvT_s[:, 1, :], kpeT_s[:, :], ckv_s
                        ps = banks[3 + self._ptc % 2]
                        self.mm(ps[:, 0:128], kT0, q_latT[:, 0, :, tsl], True, False)
                        self.mm(ps[:, 0:128], kT1, q_latT[:, 1, :, tsl], False, False)
                        self.mm(ps[:, 0:128], kP, q_peT[:, :, tsl], False, True)
                        pt = PTs[self._ptc % 4]
                        self._ptc += 1
                        self.act(pt[:, :], ps[:, 0:128], AF.Exp, scale=SCALE)
                        if new:
                            self.tt(pt[:, :], pt[:, :], amask_b[:, b, :], ALU.mult)
                        for cc in range(2):
                            self.mm(pacc[cc][:, 0:128], kV[:, cc * 128:(cc + 1) * 128], pt[:, :], pgi == 0, new)
                        self.mm(pl[:, 128:256], C["ones_b"][:, :], pt[:, :], pgi == 0, new)
                    self.recip(rlb[:, :], pl[:, 128:256])
                    for cc in range(2):
                        self.tt(accT[:, cc, :, tsl], pacc[cc][:, 0:128].rearrange("p (h k) -> p h k", h=16),
                                rlb[:, :].rearrange("p (h k) -> p h k", h=16), ALU.mult)
            po = (banks[1], banks[2])
            for h in range(16):
                for cc in range(2):
                    self.mm(po[h // 8][:, (h % 8) * 64:(h % 8 + 1) * 64], accT[:, cc, h, :],
                            w_uv[:, cc, h * 64:(h + 1) * 64], cc == 0, cc == 1)
            self.recip(rl[:, :], l_tok[:, :])
            for half in range(2):
                self.tt(sg[:, half * 512:(half + 1) * 512], po[half][:, :], sg[:, half * 512:(half + 1) * 512], ALU.mult)
                self.tt(og[:, half * 512:(half + 1) * 512].rearrange("p (h v) -> p h v", h=8),
                        sg[:, half * 512:(half + 1) * 512].rearrange("p (h v) -> p h v", h=8),
                        rl[:, half * 8:(half + 1) * 8].unsqueeze(2).broadcast_to([128, 8, 64]), ALU.mult)
            for kc in range(8):
                self.tr(ptb[:, kc * 128:(kc + 1) * 128], og[:, kc * 128:(kc + 1) * 128], C["ident_b"][:, :])
            self.cp(oT[:, :, :], ptb[:, 0:1024].rearrange("p (k t) -> p k t", k=8), eng="act")
            pout = (banks[5], banks[6])
            for half in range(2):
                for kc in range(8):
                    self.mm(pout[half][:, :], oT[:, kc, :], w_out[:, kc, half * 512:(half + 1) * 512],
                            start=kc == 0, stop=kc == 7)
            hn = h_new[t % 2]
            for half in range(2):
                self.tt(hn[:, half * 512:(half + 1) * 512], hin[:, half * 512:(half + 1) * 512],
                        pout[half][:, :], ALU.add)
            if not last:
                self.store(dst_t[r0:r0 + 128, :], hn[:, :], wr=[dst_b[t]])
            else:
                self.final_out(hn[:, :], t, V)
        self.es = old_es
        self._close_layer(es_layer)


def make_consts(SEQ):
    P = 128
    idx = np.arange(P)
    ident = np.eye(P, dtype=np.float32)
    T_p = (idx[:, None] <= idx[None, :]).astype(np.float32)
    same = (idx[:, None] // 8 == idx[None, :] // 8)
    T_s = (T_p.astype(bool) & same).astype(np.float32)
    SEG_p = np.ones((P, P), np.float32)
    SEG_s = same.astype(np.float32)
    ones = np.ones((P, P), np.float32)
    rowmask = (idx[:, None] // 8 == np.arange(16)[None, :]).astype(np.float32)
    colmask = np.broadcast_to((np.arange(16)[:, None] == (idx[None, :] // 8))[None], (P, 16, P))
    colmask = colmask.astype(np.float32).reshape(P, 2048)
    kb, kt = idx // 8, idx % 8
    tok = np.arange(128) % 8
    am = (kb[:, None, None] == np.arange(16)[None, :, None]) & (kt[:, None, None] <= tok[None, None, :])
    am = am.astype(np.float32).reshape(P, 2048)
    iota = idx.astype(np.float32)[:, None]
    cst = np.concatenate([ident, T_p, T_s, SEG_p, SEG_s, ones, rowmask, iota, colmask, am], axis=1)
    return np.ascontiguousarray(cst.astype(np.float32))


def rope_tables(SEQ, past_len):
    pos = np.concatenate([np.arange(SEQ, dtype=np.float64),
                          np.tile(past_len + np.arange(8, dtype=np.float64), 16)])
    inv = 1.0 / (10000.0 ** (np.arange(0, 32, 2, dtype=np.float64) / 32))
    ang = pos[:, None] * inv[None, :]
    cos, sin = np.cos(ang), np.sin(ang)
    tok = np.concatenate([cos, cos, -sin, sin], axis=1).astype(np.float32)
    rt = np.stack([np.concatenate([cos, cos], 1).T, np.concatenate([-sin, sin], 1).T], axis=1)
    return np.ascontiguousarray(tok), np.ascontiguousarray(rt.astype(np.float32))


_PROG_CACHE = {}


def _get_prog(cfg):
    key = (cfg["SEQ"], cfg["NPG"], cfg["NPOOL"], tuple(cfg["LAYERS"]))
    if key not in _PROG_CACHE:
        p = Prog(cfg)
        p.build()
        _PROG_CACHE[key] = p
    return _PROG_CACHE[key]


def run_cfg(cfg, inputs, n_cores, past_len):
    f = lambda a: np.ascontiguousarray(np.asarray(a))
    SEQ, NPG, NPOOL = cfg["SEQ"], cfg["NPG"], cfg["NPOOL"]
    LAYERS = cfg["LAYERS"]
    prog = _get_prog(cfg)
    nS, nM = prog.n_ssd, prog.n_mla
    ROWS = SEQ + 128
    x_prompt, x_sample = f(inputs["x_prompt"]), f(inputs["x_sample"])
    cst = make_consts(SEQ)
    rope_tok, rope_T = rope_tables(SEQ, past_len)
    shared = {"cst": cst, "rope_tok": rope_tok, "rope_T": rope_T,
              "fnw": f(inputs["final_norm_w"]).reshape(1, D)}
    norm_w = f(inputs["norm_w"])
    ssd_idx = [i for i, k in enumerate(LAYERS) if k == "ssd"]
    mla_idx = [i for i, k in enumerate(LAYERS) if k == "mla"]
    if nS:
        w_in = f(inputs["ssd_w_in"]); w_out = f(inputs["ssd_w_out"])
        cwv = f(inputs["ssd_conv_w"]); cbv = f(inputs["ssd_conv_b"])
        dtb = f(inputs["ssd_dt_bias"]); alog = f(inputs["ssd_a_log"]); dsk = f(inputs["ssd_d"])
        gnw = f(inputs["ssd_norm_w"])
        win_l, wout_l, gnw_l, cw_l, hp_l = [], [], [], [], []
        for j in range(nS):
            for sw in range(2):
                z = w_in[j][:, sw * 1024:(sw + 1) * 1024]
                x = w_in[j][:, 2048 + sw * 1024:2048 + (sw + 1) * 1024]
                Bc = w_in[j][:, 4096 + sw * 512:4096 + (sw + 1) * 512]
                Cc = w_in[j][:, 5120 + sw * 512:5120 + (sw + 1) * 512]
                dtc = w_in[j][:, 6144 + sw * 16:6144 + (sw + 1) * 16]
                win_l.append(np.concatenate([z, x, Bc, Cc, dtc], axis=1))
                wout_l.append(w_out[j][sw * 1024:(sw + 1) * 1024, :])
                gnw_l.append(gnw[j][sw * 1024:(sw + 1) * 1024].reshape(8, 128).T)
                ch = np.concatenate([np.arange(sw * 1024, (sw + 1) * 1024),
                                     2048 + np.arange(sw * 512, (sw + 1) * 512),
                                     3072 + np.arange(sw * 512, (sw + 1) * 512)])
                cwb = np.concatenate([cwv[j][:, ch], cbv[j][None, ch]], axis=0)
                cw_l.append(cwb.reshape(5, 16, 128).transpose(2, 1, 0))
                hp_l.append(np.stack([dtb[j][sw * 16:(sw + 1) * 16], alog[j][sw * 16:(sw + 1) * 16],
                                      dsk[j][sw * 16:(sw + 1) * 16]]))
        shared["ssd_win"] = f(np.stack(win_l)); shared["ssd_wout"] = f(np.stack(wout_l))
        shared["ssd_gnw"] = f(np.stack(gnw_l)); shared["ssd_cw"] = f(np.stack(cw_l))
        shared["ssd_hp"] = f(np.stack(hp_l))
        shared["ssd_nrm"] = f(np.stack([norm_w[i].reshape(8, 128).T for i in ssd_idx]))
    if nM:
        mw = f(inputs["mla_w_in"])
        perm = np.concatenate([np.arange(16, 32), np.arange(0, 16)])
        win_l = []
        for jm in range(nM):
            kpe = mw[jm][:, 768:800]
            win_l.append(np.concatenate([mw[jm][:, 0:768], kpe, kpe[:, perm], mw[jm][:, 800:1824]], axis=1))
        shared["mla_win"] = f(np.stack(win_l))
        shared["mla_nrm"] = f(np.stack([norm_w[i].reshape(8, 128).T for i in mla_idx]))
        shared["mla_qnw"] = f(np.stack([f(inputs["mla_q_norm_w"])[jm].reshape(4, 128).T for jm in range(nM)]))
        shared["mla_kvw"] = f(f(inputs["mla_kv_norm_w"])[:nM].reshape(nM, 1, 256))
        wuq = f(inputs["mla_w_uq"])
        nope = wuq[:nM, :, :, 0:64].reshape(nM, 512, 1024)
        pe = wuq[:nM, :, :, 64:96]
        shared["mla_wuq"] = f(np.concatenate([nope, pe.reshape(nM, 512, 512),
                                              pe[..., perm].reshape(nM, 512, 512)], axis=2))
        wuk = f(inputs["mla_w_uk"])[:nM]
        t_ = wuk.transpose(0, 2, 3, 1).reshape(nM, 8, 2, 64, 256)
        shared["mla_wuk"] = f(t_.transpose(0, 2, 3, 1, 4).reshape(nM, 128, 8, 256))
        shared["mla_wuv"] = f(f(inputs["mla_w_uv"])[:nM].reshape(nM, 256, 1024))
        shared["mla_wout"] = f(inputs["mla_w_out"])[:nM]
        shared["pool_all"] = np.concatenate([f(inputs["cache_ckv"])[:nM].reshape(nM * NPOOL * 128, 256),
                                             f(inputs["cache_kpe"])[:nM].reshape(nM * NPOOL * 128, 32)], axis=1)
    in_maps = []
    for c in range(n_cores):
        k = c // 2
        m = dict(shared)
        m["x_in"] = f(np.concatenate([x_prompt[k], x_sample[16 * c:16 * c + 16].reshape(128, D)], axis=0))
        if nS:
            m["st_ssm"] = f(inputs["state_ssm"])[:nS, 16 * c:16 * c + 16].reshape(nS, 16, 2048, 128)
            m["st_conv"] = f(inputs["state_conv"])[:nS, 16 * c:16 * c + 16].reshape(nS, 48, 4096)
        if nM:
            m["ptab"] = f(inputs["page_table"])[16 * c:16 * c + 16].reshape(1, 16 * NPG).astype(np.int32)
        in_maps.append(m)
    import os as _os
    if _os.environ.get("K_TRACE"):
        res = run_bass_kernel_spmd(prog.nc, in_maps, core_ids=list(range(n_cores)), trace=True)
        print("EXEC_TIME_NS", res.exec_time_ns, {e: len(v) for e, v in prog.S.ops.items()})
    else:
        res = run_bass_kernel_spmd(prog.nc, in_maps, core_ids=list(range(n_cores)))
    R = res.results
    nseq = n_cores // 2
    out = {}
    out["y_prompt"] = np.stack([R[2 * k]["y_out"][:SEQ] for k in range(nseq)])
    out["y_sample"] = np.concatenate([R[c]["y_out"][SEQ:].reshape(16, 8, D) for c in range(n_cores)])
    if nS:
        out["p_ssm"] = np.stack([R[2 * k]["o_pssm"].reshape(nS, 32, 64, 128) for k in range(nseq)], axis=1)
        out["p_conv"] = np.stack([R[2 * k]["o_pconv"] for k in range(nseq)], axis=1)
        out["s_ssm"] = np.concatenate([R[c]["o_sssm"].reshape(nS, 16, 32, 64, 128) for c in range(n_cores)], axis=1)
        out["s_conv"] = np.concatenate([R[c]["o_sconv"].reshape(nS, 16, 3, 4096) for c in range(n_cores)], axis=1)
    if nM:
        out["p_ckv"] = np.stack([R[2 * k]["o_ckv"][:, :SEQ] for k in range(nseq)], axis=1)
        out["p_kpe"] = np.stack([R[2 * k]["o_kpe"][:, :SEQ] for k in range(nseq)], axis=1)
        out["s_ckv"] = np.concatenate([R[c]["o_ckv"][:, SEQ:].reshape(nM, 16, 8, 256) for c in range(n_cores)], axis=1)
        out["s_kpe"] = np.concatenate([R[c]["o_kpe"][:, SEQ:].reshape(nM, 16, 8, 32) for c in range(n_cores)], axis=1)
    return out


def kernel(**inputs):
    out = run_cfg(FULL_CFG, inputs, 8, 8192)
    return (out["y_prompt"], out["y_sample"], out["p_ssm"], out["p_conv"], out["p_ckv"], out["p_kpe"],
            out["s_ssm"], out["s_conv"], out["s_ckv"], out["s_kpe"])
```
